# Optimizing a Trainium2 kernel written in Bass

```python
import math
import jax
import jax.numpy as jnp
from jax import lax
import numpy as np

D_MODEL = 1024
BATCH = 8
SEQ = 4096
DEPTH = 2

HEAD_DIM = 64
BRANCH_WIDTH = 256
N_MIXERS = 4
NSA_Q_HEADS = 4
NSA_KV_HEADS = 2
NSA_GROUP = NSA_Q_HEADS // NSA_KV_HEADS
NSA_CMP_LEN = 32
NSA_CMP_STRIDE = 16
NSA_SEL_LEN = 64
NSA_SEL_TOPN = 16
NSA_WINDOW = 512
NSA_SEL_QBLOCK = 32
NSA_WIN_QBLOCK = 128
NSA_FORCE_SCORE = 1.0e4
SC_WIDTH = BRANCH_WIDTH
SC_CONV_LEN = 3
SB_HEADS = 4
SB_WIDTH = SB_HEADS * HEAD_DIM
SB_QBLOCK = 128
S5_WIDTH = BRANCH_WIDTH
S5_GROUP_CH = 16
S5_GROUPS = S5_WIDTH // S5_GROUP_CH
S5_STATE = 64

ROPE_THETA = 10000.0
NORM_EPS = 1e-6
POS_OFFSET_MAX = 1024

PROJ_SIZES = (
    NSA_Q_HEADS * HEAD_DIM,
    3 * 2 * NSA_KV_HEADS * HEAD_DIM,
    3 * NSA_Q_HEADS,
    BRANCH_WIDTH,
    3 * SC_WIDTH,
    BRANCH_WIDTH,
    3 * SB_WIDTH,
    BRANCH_WIDTH,
    S5_WIDTH,
    BRANCH_WIDTH,
    N_MIXERS * D_MODEL,
)
IN_PROJ_WIDTH = sum(PROJ_SIZES)

kernel_name = "hybrid_nsa_shortconv_stickbreak_s5"


def rms_norm(x, g):
    xf = x.astype(jnp.float32)
    y = xf * lax.rsqrt(jnp.mean(xf * xf, axis=-1, keepdims=True) + NORM_EPS)
    return (y * g.astype(jnp.float32)).astype(x.dtype)


def rope(x, pos):
    half = HEAD_DIM // 2
    inv_freq = jnp.power(ROPE_THETA, -jnp.arange(half, dtype=jnp.float32) / half)
    ang = pos.astype(jnp.float32)[..., None] * inv_freq
    ang = ang.reshape(ang.shape[:2] + (1,) * (x.ndim - 3) + (half,))
    cos, sin = jnp.cos(ang), jnp.sin(ang)
    xf = x.astype(jnp.float32)
    x1, x2 = xf[..., :half], xf[..., half:]
    return jnp.concatenate([x1 * cos - x2 * sin, x2 * cos + x1 * sin], axis=-1).astype(x.dtype)


def masked_softmax(scores, mask):
    s = jnp.where(mask, scores.astype(jnp.float32), -jnp.inf)
    m = jnp.max(s, axis=-1, keepdims=True)
    m = jnp.where(jnp.isfinite(m), m, 0.0)
    e = jnp.where(mask, jnp.exp(s - m), 0.0)
    return e / jnp.maximum(jnp.sum(e, axis=-1, keepdims=True), 1e-30)


def nsa_mixer(q_in, kv_in, gate_in, positions, qk_g, cmp_pe, cmp_w1, cmp_w2):
    b, s, _ = q_in.shape
    kh, grp, dh = NSA_KV_HEADS, NSA_GROUP, HEAD_DIM
    scale = dh ** -0.5
    t_idx = jnp.arange(s)
    q = rope(rms_norm(q_in.reshape(b, s, kh, grp, dh), qk_g[0]), positions)
    kv = kv_in.reshape(b, s, 3, 2, kh, dh)
    k_c, v_c = kv[:, :, 0, 0], kv[:, :, 0, 1]
    k_s = rope(rms_norm(kv[:, :, 1, 0], qk_g[2]), positions)
    v_s = kv[:, :, 1, 1]
    k_w = rope(rms_norm(kv[:, :, 2, 0], qk_g[3]), positions)
    v_w = kv[:, :, 2, 1]

    n_cmp = (s - NSA_CMP_LEN) // NSA_CMP_STRIDE + 1
    cmp_start = jnp.arange(n_cmp) * NSA_CMP_STRIDE
    cmp_end = cmp_start + NSA_CMP_LEN - 1
    blk_idx = cmp_start[:, None] + jnp.arange(NSA_CMP_LEN)[None, :]

    def compress(t, j):
        blk = t[:, blk_idx] + cmp_pe[j][None, None, :, None, :]
        blk = jnp.moveaxis(blk, 3, 2).reshape(b, n_cmp, kh, NSA_CMP_LEN * dh)
        return jax.nn.silu(blk @ cmp_w1[j]) @ cmp_w2[j]

    k_cmp = rope(rms_norm(compress(k_c, 0), qk_g[1]), positions[:, cmp_end])
    v_cmp = compress(v_c, 1)
    sc_cmp = jnp.einsum('bskgd,bnkd->bkgsn', q, k_cmp).astype(jnp.float32) * scale
    p_cmp = masked_softmax(sc_cmp, cmp_end[None, :] <= t_idx[:, None])
    o_cmp = jnp.einsum('bkgsn,bnkd->bskgd', p_cmp.astype(v_cmp.dtype), v_cmp)

    n_sel = s // NSA_SEL_LEN
    sel_start = jnp.arange(n_sel) * NSA_SEL_LEN
    overlap = ((cmp_start[:, None] < sel_start[None, :] + NSA_SEL_LEN)
               & (cmp_start[:, None] + NSA_CMP_LEN > sel_start[None, :])).astype(jnp.float32)
    imp = jnp.einsum('bkgsn,nj->bskj', p_cmp, overlap)
    cur = t_idx // NSA_SEL_LEN
    jb = jnp.arange(n_sel)
    forced = (jb[None, :] == 0) | (jb[None, :] == cur[:, None]) | (jb[None, :] == cur[:, None] - 1)
    valid = sel_start[None, :] <= t_idx[:, None]
    imp = jnp.where(forced[None, :, None, :], NSA_FORCE_SCORE,
                    jnp.where(valid[None, :, None, :], imp, -NSA_FORCE_SCORE))
    top_n = min(NSA_SEL_TOPN, n_sel)
    _, sel_idx = lax.top_k(imp, top_n)

    k_blk = k_s.reshape(b, n_sel, NSA_SEL_LEN, kh, dh).transpose(0, 3, 1, 2, 4)
    v_blk = v_s.reshape(b, n_sel, NSA_SEL_LEN, kh, dh).transpose(0, 3, 1, 2, 4)
    qbs = NSA_SEL_QBLOCK
    nq = s // qbs
    q_b = jnp.moveaxis(q.reshape(b, nq, qbs, kh, grp, dh), 1, 0)
    i_b = jnp.moveaxis(sel_idx.reshape(b, nq, qbs, kh, top_n), 1, 0)
    t_b = t_idx.reshape(nq, qbs)
    bi = jnp.arange(b)[:, None, None, None]
    hi = jnp.arange(kh)[None, None, :, None]
    n_keys = top_n * NSA_SEL_LEN

    def sel_block(args):
        qb, ib, tb = args
        kg = k_blk[bi, hi, ib].reshape(b, qbs, kh, n_keys, dh)
        vg = v_blk[bi, hi, ib].reshape(b, qbs, kh, n_keys, dh)
        pos = (ib[..., None] * NSA_SEL_LEN + jnp.arange(NSA_SEL_LEN)).reshape(b, qbs, kh, n_keys)
        mask = (pos <= tb[None, :, None, None]).transpose(0, 2, 1, 3)[:, :, None]
        sc = jnp.einsum('bqkgd,bqkmd->bkgqm', qb, kg).astype(jnp.float32) * scale
        p = masked_softmax(sc, mask)
        return jnp.einsum('bkgqm,bqkmd->bqkgd', p.astype(vg.dtype), vg)

    o_slc = jnp.moveaxis(lax.map(sel_block, (q_b, i_b, t_b)), 0, 1).reshape(b, s, kh, grp, dh)

    qbw = NSA_WIN_QBLOCK
    nw = s // qbw
    kp = jnp.pad(k_w, ((0, 0), (NSA_WINDOW, 0), (0, 0), (0, 0)))
    vp = jnp.pad(v_w, ((0, 0), (NSA_WINDOW, 0), (0, 0), (0, 0)))
    q_w = jnp.moveaxis(q.reshape(b, nw, qbw, kh, grp, dh), 1, 0)
    w_starts = jnp.arange(nw) * qbw

    def win_block(args):
        qb, start = args
        kb = lax.dynamic_slice_in_dim(kp, start, NSA_WINDOW + qbw, axis=1)
        vb = lax.dynamic_slice_in_dim(vp, start, NSA_WINDOW + qbw, axis=1)
        tq = start + jnp.arange(qbw)
        tk = start - NSA_WINDOW + jnp.arange(NSA_WINDOW + qbw)
        mask = (tk[None, :] <= tq[:, None]) & (tk[None, :] > tq[:, None] - NSA_WINDOW) & (tk[None, :] >= 0)
        sc = jnp.einsum('bqkgd,bmkd->bkgqm', qb, kb).astype(jnp.float32) * scale
        p = masked_softmax(sc, mask)
        return jnp.einsum('bkgqm,bmkd->bqkgd', p.astype(vb.dtype), vb)

    o_win = jnp.moveaxis(lax.map(win_block, (q_w, w_starts)), 0, 1).reshape(b, s, kh, grp, dh)

    g = jax.nn.sigmoid(gate_in).reshape(b, s, 3, kh, grp)[..., None]
    o = g[:, :, 0] * o_cmp + g[:, :, 1] * o_slc + g[:, :, 2] * o_win
    return o.reshape(b, s, NSA_Q_HEADS * dh)


def short_conv_mixer(bcx, conv_w):
    bg, cg, xin = jnp.split(bcx, 3, axis=-1)
    u = cg * xin
    y = lax.conv_general_dilated(u, conv_w[:, None, :], window_strides=(1,),
                                 padding=[(SC_CONV_LEN - 1, 0)],
                                 dimension_numbers=('NWC', 'WIO', 'NWC'),
                                 feature_group_count=SC_WIDTH)
    return bg * y


def stick_breaking_mixer(qkv):
    b, s, _ = qkv.shape
    t = qkv.reshape(b, s, 3, SB_HEADS, HEAD_DIM)
    q, k, v = t[:, :, 0], t[:, :, 1], t[:, :, 2]
    scale = HEAD_DIM ** -0.5
    nq = s // SB_QBLOCK
    q_b = jnp.moveaxis(q.reshape(b, nq, SB_QBLOCK, SB_HEADS, HEAD_DIM), 1, 0)
    q_starts = jnp.arange(nq) * SB_QBLOCK
    t_k = jnp.arange(s)

    def block(args):
        qb, start = args
        z = jnp.einsum('bqhd,bshd->bhqs', qb, k).astype(jnp.float32) * scale
        tq = start + jnp.arange(SB_QBLOCK)
        mask = t_k[None, :] < tq[:, None]
        log_1mb = jnp.where(mask, jax.nn.log_sigmoid(-z), 0.0)
        after = lax.cumsum(log_1mb, axis=3, reverse=True) - log_1mb
        w = jnp.where(mask, jnp.exp(jax.nn.log_sigmoid(z) + after), 0.0)
        return jnp.einsum('bhqs,bshd->bqhd', w.astype(v.dtype), v)

    o = jnp.moveaxis(lax.map(block, (q_b, q_starts)), 0, 1)
    return o.reshape(b, s, SB_WIDTH)


def _ssm_combine(e1, e2):
    a1r, a1i, b1r, b1i = e1
    a2r, a2i, b2r, b2i = e2
    return (a2r * a1r - a2i * a1i,
            a2r * a1i + a2i * a1r,
            a2r * b1r - a2i * b1i + b2r,
            a2r * b1i + a2i * b1r + b2i)


def s5_mixer(u, a_re, a_im, log_dt, b_re, b_im, c_re, c_im, d_skip, glu_w, glu_b):
    f32 = jnp.float32
    bsz, s, _ = u.shape
    ug = u.reshape(bsz, s, S5_GROUPS, S5_GROUP_CH).astype(f32)
    dt = jnp.exp(log_dt.astype(f32))[:, None]
    lr, li = a_re.astype(f32), a_im.astype(f32)
    mag = jnp.exp(lr * dt)
    ab_re, ab_im = mag * jnp.cos(li * dt), mag * jnp.sin(li * dt)
    den = lr * lr + li * li
    coef_re = ((ab_re - 1.0) * lr + ab_im * li) / den
    coef_im = (ab_im * lr - (ab_re - 1.0) * li) / den
    br, bim = b_re.astype(f32), b_im.astype(f32)
    bb_re = coef_re[..., None] * br - coef_im[..., None] * bim
    bb_im = coef_re[..., None] * bim + coef_im[..., None] * br
    bu_re = jnp.einsum('gpc,bsgc->bsgp', bb_re, ug)
    bu_im = jnp.einsum('gpc,bsgc->bsgp', bb_im, ug)
    a_r = jnp.broadcast_to(ab_re, bu_re.shape)
    a_i = jnp.broadcast_to(ab_im, bu_re.shape)
    _, _, x_re, x_im = lax.associative_scan(_ssm_combine, (a_r, a_i, bu_re, bu_im), axis=1)
    y = (jnp.einsum('gcp,bsgp->bsgc', c_re.astype(f32), x_re)
         - jnp.einsum('gcp,bsgp->bsgc', c_im.astype(f32), x_im)
         + d_skip.astype(f32) * ug)
    y = y.reshape(bsz, s, S5_WIDTH).astype(u.dtype)
    a, g = jnp.split(y @ glu_w + glu_b, 2, axis=-1)
    return a * jax.nn.sigmoid(g)


def setup_inputs(seed: int = 0) -> dict:
    key = jax.random.key(seed)
    ks = iter(jax.random.split(key, 32))
    f32 = jnp.float32
    L, dh = DEPTH, HEAD_DIM

    def nrm(shape, scale):
        return jax.random.normal(next(ks), shape, f32) * scale

    x = nrm((BATCH, SEQ, D_MODEL), 1.0)
    positions = (jnp.arange(SEQ, dtype=jnp.int32)[None, :]
                 + jax.random.randint(next(ks), (BATCH, 1), 0, POS_OFFSET_MAX, dtype=jnp.int32))
    norm_g = 1.0 + nrm((L, D_MODEL), 0.02)
    w_in = nrm((L, D_MODEL, IN_PROJ_WIDTH), D_MODEL ** -0.5)
    nsa_qk_g = 1.0 + nrm((L, 4, dh), 0.02)
    nsa_cmp_pe = nrm((L, 2, NSA_CMP_LEN, dh), 0.1)
    nsa_cmp_w1 = nrm((L, 2, NSA_CMP_LEN * dh, dh), (NSA_CMP_LEN * dh) ** -0.5)
    nsa_cmp_w2 = nrm((L, 2, dh, dh), dh ** -0.5)
    sc_conv_w = nrm((L, SC_CONV_LEN, SC_WIDTH), SC_CONV_LEN ** -0.5)
    n_idx = jnp.arange(S5_STATE, dtype=f32)
    s5_a_re = -0.5 + nrm((L, S5_GROUPS, S5_STATE), 0.01)
    s5_a_im = math.pi * n_idx + nrm((L, S5_GROUPS, S5_STATE), 0.01)
    s5_log_dt = jax.random.uniform(next(ks), (L, S5_GROUPS), f32, math.log(1e-3), math.log(1e-1))
    s5_b_re = nrm((L, S5_GROUPS, S5_STATE, S5_GROUP_CH), (2 * S5_GROUP_CH) ** -0.5)
    s5_b_im = nrm((L, S5_GROUPS, S5_STATE, S5_GROUP_CH), (2 * S5_GROUP_CH) ** -0.5)
    s5_c_re = nrm((L, S5_GROUPS, S5_GROUP_CH, S5_STATE), 0.5)
    s5_c_im = nrm((L, S5_GROUPS, S5_GROUP_CH, S5_STATE), 0.5)
    s5_d = nrm((L, S5_GROUPS, S5_GROUP_CH), 0.5)
    s5_glu_w = nrm((L, S5_WIDTH, 2 * S5_WIDTH), S5_WIDTH ** -0.5)
    s5_glu_b = nrm((L, 2 * S5_WIDTH), 0.01)
    w_branch = nrm((L, N_MIXERS, BRANCH_WIDTH, D_MODEL), BRANCH_WIDTH ** -0.5)
    w_out = nrm((L, D_MODEL, D_MODEL), D_MODEL ** -0.5)
    return {"x": x, "positions": positions, "norm_g": norm_g, "w_in": w_in,
            "nsa_qk_g": nsa_qk_g, "nsa_cmp_pe": nsa_cmp_pe, "nsa_cmp_w1": nsa_cmp_w1,
            "nsa_cmp_w2": nsa_cmp_w2, "sc_conv_w": sc_conv_w, "s5_a_re": s5_a_re,
            "s5_a_im": s5_a_im, "s5_log_dt": s5_log_dt, "s5_b_re": s5_b_re, "s5_b_im": s5_b_im,
            "s5_c_re": s5_c_re, "s5_c_im": s5_c_im, "s5_d": s5_d, "s5_glu_w": s5_glu_w,
            "s5_glu_b": s5_glu_b, "w_branch": w_branch, "w_out": w_out}


def reference(x, positions, norm_g, w_in, nsa_qk_g, nsa_cmp_pe, nsa_cmp_w1, nsa_cmp_w2,
              sc_conv_w, s5_a_re, s5_a_im, s5_log_dt, s5_b_re, s5_b_im, s5_c_re, s5_c_im,
              s5_d, s5_glu_w, s5_glu_b, w_branch, w_out):
    b, s, _ = x.shape
    split_at = [int(o) for o in np.cumsum(PROJ_SIZES)[:-1]]
    for l in range(DEPTH):
        h = rms_norm(x, norm_g[l])
        proj = h @ w_in[l]
        (nsa_q, nsa_kv, nsa_gate, nsa_z, sc_bcx, sc_z, sb_qkv, sb_z,
         s5_u, s5_z, merge) = jnp.split(proj, split_at, axis=-1)
        outs = (
            nsa_mixer(nsa_q, nsa_kv, nsa_gate, positions, nsa_qk_g[l], nsa_cmp_pe[l],
                      nsa_cmp_w1[l], nsa_cmp_w2[l]) * jax.nn.silu(nsa_z),
            short_conv_mixer(sc_bcx, sc_conv_w[l]) * jax.nn.silu(sc_z),
            stick_breaking_mixer(sb_qkv) * jax.nn.silu(sb_z),
            s5_mixer(s5_u, s5_a_re[l], s5_a_im[l], s5_log_dt[l], s5_b_re[l], s5_b_im[l],
                     s5_c_re[l], s5_c_im[l], s5_d[l], s5_glu_w[l], s5_glu_b[l]) * jax.nn.silu(s5_z),
        )
        gates = jax.nn.sigmoid(merge).reshape(b, s, N_MIXERS, D_MODEL)
        mixed = gates[:, :, 0] * (outs[0] @ w_branch[l, 0])
        for m in range(1, N_MIXERS):
            mixed = mixed + gates[:, :, m] * (outs[m] @ w_branch[l, m])
        x = x + mixed @ w_out[l]
    return x
```

```python
import math
from contextlib import ExitStack
import numpy as np
import concourse.bass as bass
import concourse.mybir as mybir
from concourse.bass_utils import run_bass_kernel_spmd

F32 = mybir.dt.float32
BF16 = mybir.dt.bfloat16
I32 = mybir.dt.int32
AF = mybir.ActivationFunctionType
ALU = mybir.AluOpType
AX = mybir.AxisListType

D = 1024
NKC = 8
EPS = 1e-6
C_NQ, C_NKV, C_NG, C_NZ = 0, 256, 1024, 1036
C_SC, C_SCZ = 1292, 2060
C_SB, C_SBZ = 2316, 3084
C_S5U, C_S5Z = 3340, 3596
C_MG = 3852
WIN = 7948
NDS = 24
NDW = 6
SAME_ENGINE_WAIT = True
USE_CAST_DMA = True
CAST_Q = "pool"


class Buf:
    __slots__ = ("name", "lw", "rd")

    def __init__(self, name=""):
        self.name = name
        self.lw = []
        self.rd = {}


class Prog:
    def __init__(self, nc, es):
        self.nc = nc
        self.eng = {"pe": nc.tensor, "dve": nc.vector, "act": nc.scalar, "pool": nc.gpsimd, "sp": nc.sync}
        self.sem = {e: es.enter_context(nc.semaphore("s_" + e)) for e in ("pe", "dve", "act", "pool")}
        self.cnt = {e: 0 for e in self.sem}
        self.dsem = [es.enter_context(nc.semaphore("d%d" % i)) for i in range(NDS)]
        self.dsem += [es.enter_context(nc.semaphore("w%d" % i)) for i in range(NDW)]
        self.dcnt = [0] * (NDS + NDW)
        self.dnext = 0
        self.wnext = 0
        self.seen = {e: {} for e in self.eng}
        self.nwaits = 0
        self.nins = 0

    def _wait(self, e, tok):
        if tok is None:
            return
        k, v = tok
        if k == ("c", e) and (e == "pe" or not SAME_ENGINE_WAIT):
            return
        if self.seen[e].get(k, 0) >= v:
            return
        self.seen[e][k] = v
        s = self.sem[k[1]] if k[0] == "c" else self.dsem[k[1]]
        self.eng[e].wait_ge(s, v)
        self.nwaits += 1

    def _deps(self, e, reads, writes, isdma=False):
        for b in reads:
            for t in b.lw:
                self._wait(e, t)
        for b in writes:
            if isdma and not b.rd and b.lw and all(t[0][0] == "d" for t in b.lw):
                continue
            for t in b.lw:
                self._wait(e, t)
            for k, v in b.rd.items():
                self._wait(e, (k, v))

    def _done(self, tok, reads, writes, isdma=False):
        k, v = tok
        for b in reads:
            if b.rd.get(k, 0) < v:
                b.rd[k] = v
        for b in writes:
            if isdma and not b.rd and b.lw and all(t[0][0] == "d" for t in b.lw):
                b.lw.append(tok)
            else:
                b.lw = [tok]
            b.rd = {}

    def op(self, e, fn, reads=(), writes=()):
        self._deps(e, reads, writes)
        ins = fn(self.eng[e])
        self.cnt[e] += 1
        self.nins += 1
        ins.then_inc(self.sem[e], 1)
        tok = (("c", e), self.cnt[e])
        self._done(tok, reads, writes)
        return tok

    def dma(self, out, in_, reads=(), writes=(), q="sp"):
        if q == "pool":
            i = NDS + self.wnext
            self.wnext = (self.wnext + 1) % NDW
        else:
            i = self.dnext
            self.dnext = (i + 1) % NDS
        if self.dcnt[i] > 0:
            self._wait(q, (("d", i), self.dcnt[i]))
        self._deps(q, reads, writes, True)
        ins = self.eng[q].dma_start(out=out, in_=in_)
        self.dcnt[i] += 16
        self.nins += 1
        ins.then_inc(self.dsem[i], 16)
        tok = (("d", i), self.dcnt[i])
        self._done(tok, reads, writes, True)
        return tok

    def barrier(self):
        toks = [(("c", e), self.cnt[e]) for e in self.cnt if self.cnt[e] > 0]
        toks += [(("d", i), self.dcnt[i]) for i in range(NDS + NDW) if self.dcnt[i] > 0]
        for e in self.eng:
            for t in toks:
                self._wait(e, t)


_UID = [0]


def run_pipeline(iters, nst, young_first=True):
    n = len(iters)
    for step in range(n + nst - 1):
        for st in (range(nst) if young_first else range(nst - 1, -1, -1)):
            k = step - st
            if 0 <= k < n and iters[k][st] is not None:
                iters[k][st]()


class Ring:
    def __init__(self, nc, es, name, shape, dtype, n, psum=False):
        self.t = []
        self.b = []
        for i in range(n):
            _UID[0] += 1
            if psum:
                t = es.enter_context(nc.psum_tensor("rp_%s%d_%d" % (name, i, _UID[0]), shape, dtype))
            else:
                t = es.enter_context(nc.sbuf_tensor("rs_%s%d_%d" % (name, i, _UID[0]), shape, dtype))
            self.t.append(t)
            self.b.append(Buf("%s%d" % (name, i)))
        self.i = 0
        self.n = n

    def get(self):
        i = self.i
        self.i = (i + 1) % self.n
        return self.t[i], self.b[i]


def host_consts(S):
    import ml_dtypes
    bf = ml_dtypes.bfloat16
    NT = S // 128
    c = {}
    c["ident"] = np.eye(128, dtype=np.float32)
    half = 32
    invf = np.power(np.float32(10000.0), -np.arange(half, dtype=np.float32) / np.float32(half)).astype(np.float32)
    c["invf"] = np.tile(invf[None, :], (128, 1)).astype(np.float32)
    s_ = np.arange(128)[:, None]
    t_ = np.arange(512)[None, :]
    c["sbm"] = np.stack([(128 * d + s_ < t_) for d in range(4)], 1).astype(bf)
    c["slm"] = np.stack([(128 * d + s_ <= t_) for d in range(4)], 1).astype(bf)
    band = []
    for d in range(-4, 4):
        sa = 128 * d + s_
        band.append((sa <= t_) & (sa > t_ - 512))
    c["band"] = np.stack(band, 1).astype(bf)
    c["ustrict"] = (np.arange(128)[:, None] > np.arange(128)[None, :]).astype(bf)
    c["onesb"] = np.ones((128, 128), dtype=bf)
    ee = np.zeros((128, S), np.float32)
    ee[:64] = (np.arange(S)[None, :] // 64 == np.arange(64)[:, None])
    c["eexp"] = ee.astype(bf)
    t = np.arange(S)
    j = np.arange(64)[None, :]
    cur = (t // 64)[:, None]
    forced = (j == 0) | (j == cur) | (j == cur - 1)
    valid = (j * 64) <= t[:, None]
    vm = (valid & ~forced).astype(np.float32)
    fv = np.where(forced, 1.0e4, np.where(valid, 0.0, -1.0e4)).astype(np.float32)
    c["selvm"] = vm.reshape(NT, 128, 64).transpose(1, 0, 2).astype(bf)
    c["selfv"] = fv.reshape(NT, 128, 64).transpose(1, 0, 2).astype(bf)
    n = np.arange(256)[:, None]
    ovl = ((16 * n < 64 * j + 64) & (16 * n + 32 > 64 * j)).astype(np.float32)
    ext = np.concatenate([ovl, np.ones((256, 1), np.float32), np.zeros((256, 1), np.float32)], 1)
    ext[255, :] = 0
    c["ovl"] = ext.reshape(2, 128, 66).transpose(1, 0, 2).copy()
    return c


def build(S, n_layers, debug=None, phases=("conv", "s5", "sb", "nsa", "final")):
    NT = S // 128
    NB = S // 512
    NCMP = (S - 32) // 16 + 1
    SCALE = 0.125
    nc = bass.Bass("TRN2", target_bir_lowering=False)

    def din(name, shape, dtype=F32):
        return nc.dram_tensor(name, list(shape), dtype, kind="ExternalInput").ap()

    L = n_layers
    x_in = din("x", [S, D])
    pos_in = din("pos", [128, NT + 2], I32)
    norm_g = din("norm_g", [L, D])
    w_in = din("w_in", [L, D, WIN])
    convw = din("convw", [L, 256, 3])
    s5_lr = din("s5_lr", [L, 128, 8]); s5_li = din("s5_li", [L, 128, 8]); s5_ldt = din("s5_ldt", [L, 128, 8])
    s5_bre = din("s5_bre", [L, 128, 8, 128]); s5_bim = din("s5_bim", [L, 128, 8, 128])
    s5_cre = din("s5_cre", [L, 128, 8, 128]); s5_cim = din("s5_cim", [L, 128, 8, 128])
    s5_d = din("s5_d", [L, 128, 2]); glu_w = din("glu_w", [L, 256, 512]); glu_b = din("glu_b", [L, 128, 4])
    w_br = din("w_br", [L, 4, 256, D]); w_o = din("w_o", [L, D, D])
    qkg_row = din("qkg_row", [L, 512]); qkg1 = din("qkg1", [L, 128])
    pe_t = din("pe_t", [L, 2, 128, 32]); w1bd = din("w1bd", [L, 2, 128, 32 * 128]); w2bd = din("w2bd", [L, 2, 128, 128])
    c_ident = din("ident", [128, 128]); c_invf = din("invf", [128, 32])
    c_sbm = din("sbm", [128, 4, 512], BF16); c_slm = din("slm", [128, 4, 512], BF16); c_band = din("band", [128, 8, 512], BF16)
    c_ustrict = din("ustrict", [128, 128], BF16); c_onesb = din("onesb", [128, 128], BF16); c_eexp = din("eexp", [128, S], BF16)
    c_selvm = din("selvm", [128, NT, 64], BF16); c_selfv = din("selfv", [128, NT, 64], BF16); c_ovl = din("ovl", [128, 2, 66])
    y_out = nc.dram_tensor("y", [S, D], F32, kind="ExternalOutput").ap()
    oT_d = nc.dram_tensor("oT_d", [4, 256, S], BF16, kind="Internal").ap()
    xmid = nc.dram_tensor("xmid", [S, D], F32, kind="Internal").ap()
    dbg = None
    if debug == "oT":
        dbg = nc.dram_tensor("dbg", [4, 256, S], BF16, kind="ExternalOutput").ap()

    with ExitStack() as es:
        P = Prog(nc, es)

        def sbt(ctx, name, shape, dtype):
            _UID[0] += 1
            return ctx.enter_context(nc.sbuf_tensor("sb_%s_%d" % (name, _UID[0]), list(shape), dtype))

        identf = sbt(es, "identf", [128, 128], F32)
        identb = sbt(es, "identb", [128, 128], BF16)
        onesb = sbt(es, "onesb", [128, 128], BF16)
        b_c = Buf("const")
        P.dma(identf[:], c_ident, writes=[b_c])
        P.dma(onesb[:], c_onesb, writes=[b_c])
        P.op("dve", lambda e: e.tensor_copy(out=identb[:], in_=identf[:]), reads=[b_c], writes=[b_c])
        pst = [es.enter_context(nc.psum_tensor("ps%d" % i, [128, 512], F32)) for i in range(8)]
        psb = [Buf("ps%d" % i) for i in range(8)]

        class PR:
            def __init__(self, idx):
                self.idx = list(idx); self.i = 0
            def get(self):
                k = self.idx[self.i]; self.i = (self.i + 1) % len(self.idx)
                return pst[k], psb[k]
        ps = PR(range(8))

        hT = sbt(es, "hT", [128, NKC, S], BF16)
        cosT = sbt(es, "cosT", [128, NT, 32], F32); sinT = sbt(es, "sinT", [128, NT, 32], F32)
        cosC = sbt(es, "cosC", [128, 2, 32], F32); sinC = sbt(es, "sinC", [128, 2, 32], F32)

        def sincos(ctx, ang, shape, out_sin, out_cos, tag, rd=()):
            b = Buf(tag)
            ki = sbt(ctx, tag + "ki", shape, I32); kf = sbt(ctx, tag + "kf", shape, F32)
            r = sbt(ctx, tag + "r", shape, F32); c1 = sbt(ctx, tag + "c1", shape, F32)
            TWO_PI = 2.0 * math.pi
            for shift, dst in ((0.0, out_sin), (0.5 * math.pi, out_cos)):
                P.op("dve", lambda e: e.tensor_scalar(out=ki[:], in0=ang, scalar1=1.0 / TWO_PI, scalar2=shift / TWO_PI + 0.5,
                                                      op0=ALU.mult, op1=ALU.add), reads=list(rd), writes=[b])
                P.op("dve", lambda e: e.tensor_copy(out=kf[:], in_=ki[:]), reads=[b], writes=[b])
                P.op("dve", lambda e: e.scalar_tensor_tensor(out=r[:], in0=kf[:], scalar=-TWO_PI, in1=ang, op0=ALU.mult, op1=ALU.add),
                     reads=[b] + list(rd), writes=[b])
                if shift != 0.0:
                    P.op("dve", lambda e: e.tensor_scalar(out=r[:], in0=r[:], scalar1=shift, scalar2=None, op0=ALU.add), reads=[b], writes=[b])
                P.op("dve", lambda e: e.tensor_scalar(out=c1[:], in0=r[:], scalar1=math.pi, scalar2=-TWO_PI, op0=ALU.is_gt, op1=ALU.mult),
                     reads=[b], writes=[b])
                P.op("dve", lambda e: e.tensor_tensor(out=r[:], in0=r[:], in1=c1[:], op=ALU.add), reads=[b], writes=[b])
                P.op("dve", lambda e: e.tensor_scalar(out=c1[:], in0=r[:], scalar1=-math.pi, scalar2=TWO_PI, op0=ALU.is_lt, op1=ALU.mult),
                     reads=[b], writes=[b])
                P.op("dve", lambda e: e.tensor_tensor(out=r[:], in0=r[:], in1=c1[:], op=ALU.add), reads=[b], writes=[b])
                P.op("dve", lambda e: e.tensor_scalar(out=r[:], in0=r[:], scalar1=3.1415925, scalar2=-3.1415925, op0=ALU.min, op1=ALU.max),
                     reads=[b], writes=[b])
                P.op("act", lambda e: e.activation(out=dst, in_=r[:], func=AF.Sin), reads=[b], writes=[b])

        with ExitStack() as ph:
            invf = sbt(ph, "invf", [128, 32], F32)
            posi = sbt(ph, "posi", [128, NT + 2], I32); posf = sbt(ph, "posf", [128, NT + 2], F32)
            ang = sbt(ph, "ang", [128, NT + 2, 32], F32)
            b0 = Buf("rope")
            P.dma(invf[:], c_invf, writes=[b0])
            P.dma(posi[:], pos_in, writes=[b0])
            P.op("dve", lambda e: e.tensor_copy(out=posf[:], in_=posi[:]), reads=[b0], writes=[b0])
            P.op("dve", lambda e: e.tensor_tensor(out=ang[:], in0=posf[:].unsqueeze(2).to_broadcast([128, NT + 2, 32]),
                                                  in1=invf[:].unsqueeze(1).to_broadcast([128, NT + 2, 32]), op=ALU.mult),
                 reads=[b0], writes=[b0])
            sn = sbt(ph, "sn_all", [128, NT + 2, 32], F32); cs = sbt(ph, "cs_all", [128, NT + 2, 32], F32)
            sincos(ph, ang[:], [128, NT + 2, 32], sn[:], cs[:], "rp", rd=[b0])
            P.barrier()
            P.op("dve", lambda e: e.tensor_copy(out=cosT[:], in_=cs[:, 0:NT, :]))
            P.op("dve", lambda e: e.tensor_copy(out=sinT[:], in_=sn[:, 0:NT, :]))
            P.op("dve", lambda e: e.tensor_copy(out=cosC[:], in_=cs[:, NT:NT + 2, :]))
            P.op("dve", lambda e: e.tensor_copy(out=sinC[:], in_=sn[:, NT:NT + 2, :]))
            P.barrier()

        def loadw(wst, wbf, src, cols, eng="pool"):
            if USE_CAST_DMA:
                return loadw_cast(wbf, src, cols)
            ws, bws = wst.get(); wt, bw = wbf.get()
            off = 0
            for (c0, n) in cols:
                P.dma(ws[:, :, off:off + n], src[:, c0:c0 + n].rearrange("(k p) c -> p k c", p=128), writes=[bws])
                off += n
            P.op(eng, lambda e: e.tensor_copy(out=wt[:, :, 0:off], in_=ws[:, :, 0:off]), reads=[bws], writes=[bw])
            return wt, bw

        _simstg = {}
        if CAST_Q != "pool":
            _simstg["t"] = sbt(es, "simstg", [128, NKC * 512], F32)
            _simstg["b"] = Buf("simstg")

        def cast_dma(out, in_, bw):
            if CAST_Q == "pool":
                P.dma(out, in_, writes=[bw], q="pool")
                return
            n = 1
            for d_ in out.shape[1:]:
                n *= d_
            stg = _simstg["t"][:, 0:n]
            if len(out.shape) == 3:
                stg = stg.rearrange("p (a b) -> p a b", a=out.shape[1])
            P.dma(stg, in_, reads=[_simstg["b"]], writes=[_simstg["b"]])
            P.op("dve", lambda e: e.tensor_copy(out=out, in_=stg), reads=[_simstg["b"]], writes=[bw, _simstg["b"]])

        def loadw_cast(wbf, src, cols):
            wt, bw = wbf.get()
            off = 0
            for (c0, n) in cols:
                cast_dma(wt[:, :, off:off + n], src[:, c0:c0 + n].rearrange("(k p) c -> p k c", p=128), bw)
                off += n
            return wt, bw

        def mm_fm(pt, bp, wt, bw, c0, cw, tb):
            for kc in range(NKC):
                P.op("pe", lambda e: e.matmul(pt[0:cw, :], lhsT=wt[:, kc, c0:c0 + cw], rhs=hT[:, kc, tb * 512:(tb + 1) * 512],
                                              start=(kc == 0), stop=(kc == NKC - 1)), reads=[bw], writes=[bp])

        def mm_tm(pt, bp, wt, bw, c0, cw, tt, po=0):
            for kc in range(NKC):
                P.op("pe", lambda e: e.matmul(pt[:, po:po + cw], lhsT=hT[:, kc, tt * 128:(tt + 1) * 128], rhs=wt[:, kc, c0:c0 + cw],
                                              start=(kc == 0), stop=(kc == NKC - 1)), reads=[bw], writes=[bp])

        for l in range(n_layers):
            xsrc = x_in if l == 0 else xmid
            xdst = y_out if l == n_layers - 1 else xmid
            with ExitStack() as ph:
                gB = sbt(ph, "gB", [128, D], F32)
                b_g = Buf("gB")
                P.dma(gB[:], norm_g[l:l + 1, :].partition_broadcast(128), writes=[b_g])
                xr = Ring(nc, ph, "xt", [128, D], F32, 5)
                junk = Ring(nc, ph, "junk", [128, D], F32, 1)
                xnr = Ring(nc, ph, "xn", [128, D], BF16, 2)
                st = Ring(nc, ph, "st", [128, 4], F32, 4)
                b_h = Buf("hT")
                pp0 = PR([0, 1, 2, 3])
                iters = []

                def mk_p0(i):
                    xt, bx = xr.get(); s_, bs = st.get(); xn, bn = xnr.get(); pt, bp = pp0.get()

                    def s0():
                        P.dma(xt[:], xsrc[i * 128:(i + 1) * 128, :], writes=[bx])

                    def s1():
                        jt, bj = junk.get()
                        P.op("act", lambda e: e.activation(out=jt[:], in_=xt[:], func=AF.Square, accum_out=s_[:, 0:1]), reads=[bx], writes=[bj, bs])
                        P.op("act", lambda e: e.activation(out=s_[:, 1:2], in_=s_[:, 0:1], func=AF.Sqrt, bias=EPS, scale=1.0 / D), reads=[bs], writes=[bs])

                    def s2():
                        P.op("dve", lambda e: e.reciprocal(out=s_[:, 2:3], in_=s_[:, 1:2]), reads=[bs], writes=[bs])
                        P.op("dve", lambda e: e.scalar_tensor_tensor(out=xn[:], in0=xt[:], scalar=s_[:, 2:3], in1=gB[:], op0=ALU.mult, op1=ALU.mult),
                             reads=[bx, bs, b_g], writes=[bn])

                    def s3():
                        ptb = pt[:].bitcast(BF16)
                        for kc in range(NKC):
                            P.op("pe", lambda e: e.transpose(out=ptb[:, kc * 128:(kc + 1) * 128], in_=xn[:, kc * 128:(kc + 1) * 128], identity=identb[:]),
                                 reads=[bn, b_c], writes=[bp])

                    def s4():
                        ptb = pt[:].bitcast(BF16)
                        P.op("act" if i % 2 == 0 else "pool_or_dve", None) if False else None
                        if i % 3 == 2:
                            P.op("dve", lambda e: e.tensor_copy(out=hT[:, :, i * 128:(i + 1) * 128], in_=ptb.rearrange("p (k t) -> p k t", k=NKC)), reads=[bp], writes=[b_h])
                        else:
                            P.op("act", lambda e: e.activation(out=hT[:, :, i * 128:(i + 1) * 128], in_=ptb.rearrange("p (k t) -> p k t", k=NKC), func=AF.Copy),
                                 reads=[bp], writes=[b_h])
                    return [s0, s1, s2, s3, s4]

                for i in range(NT):
                    iters.append(mk_p0(i))
                run_pipeline(iters, 5)
                P.barrier()

            if "conv" in phases:
                with ExitStack() as ph:
                    wst = Ring(nc, ph, "wst", [128, NKC, 512], F32, 2); wbf = Ring(nc, ph, "wbf", [128, NKC, 512], BF16, 2)
                    cwt = sbt(ph, "cwt", [128, 2, 3], F32); b_cw = Buf()
                    P.dma(cwt[:], convw[l].rearrange("(c p) j -> p c j", p=128), writes=[b_cw])
                    u = sbt(ph, "u", [128, S + 2], F32); b_u = Buf()
                    xs_r = Ring(nc, ph, "xs", [128, 512], F32, 2); sz_r = Ring(nc, ph, "sz", [128, 512], F32, 2)
                    bz_r = Ring(nc, ph, "bz", [128, 512], F32, 2); y_r = Ring(nc, ph, "y", [128, 512], F32, 2)
                    o_r = Ring(nc, ph, "o", [128, 512], BF16, 2)
                    for cc in range(2):
                        wt, bw = loadw(wst, wbf, w_in[l], [(C_SC + cc * 128, 128), (C_SC + 256 + cc * 128, 128),
                                                           (C_SC + 512 + cc * 128, 128), (C_SCZ + cc * 128, 128)])
                        P.op("pool", lambda e: e.memset(u[:, 0:2], 0.0), writes=[b_u])
                        for tb in range(NB):
                            t0 = tb * 512
                            pB, bB = ps.get(); mm_fm(pB, bB, wt, bw, 0, 128, tb)
                            pC, bC = ps.get(); mm_fm(pC, bC, wt, bw, 128, 128, tb)
                            pX, bX = ps.get(); mm_fm(pX, bX, wt, bw, 256, 128, tb)
                            pZ, bZ = ps.get(); mm_fm(pZ, bZ, wt, bw, 384, 128, tb)
                            xs, bxs = xs_r.get()
                            P.op("act", lambda e: e.activation(out=xs[:], in_=pX[:], func=AF.Copy), reads=[bX], writes=[bxs])
                            P.op("dve", lambda e: e.tensor_tensor(out=u[:, 2 + t0:2 + t0 + 512], in0=pC[:], in1=xs[:], op=ALU.mult),
                                 reads=[bC, bxs], writes=[b_u])
                            sz, bsz = sz_r.get()
                            P.op("act", lambda e: e.activation(out=sz[:], in_=pZ[:], func=AF.Silu), reads=[bZ], writes=[bsz])
                            bz, bbz = bz_r.get()
                            P.op("dve", lambda e: e.tensor_tensor(out=bz[:], in0=pB[:], in1=sz[:], op=ALU.mult), reads=[bB, bsz], writes=[bbz])
                            y, by = y_r.get()
                            P.op("dve", lambda e: e.tensor_scalar(out=y[:], in0=u[:, 2 + t0:2 + t0 + 512], scalar1=cwt[:, cc, 2:3], scalar2=None,
                                                                  op0=ALU.mult), reads=[b_u, b_cw], writes=[by])
                            P.op("dve", lambda e: e.scalar_tensor_tensor(out=y[:], in0=u[:, 1 + t0:1 + t0 + 512], scalar=cwt[:, cc, 1:2], in1=y[:],
                                                                         op0=ALU.mult, op1=ALU.add), reads=[b_u, by], writes=[by])
                            P.op("dve", lambda e: e.scalar_tensor_tensor(out=y[:], in0=u[:, t0:t0 + 512], scalar=cwt[:, cc, 0:1], in1=y[:],
                                                                         op0=ALU.mult, op1=ALU.add), reads=[b_u, by], writes=[by])
                            o, bo = o_r.get()
                            P.op("pool", lambda e: e.tensor_tensor(out=o[:], in0=y[:], in1=bz[:], op=ALU.mult), reads=[by, bbz], writes=[bo])
                            P.dma(oT_d[1, cc * 128:(cc + 1) * 128, t0:t0 + 512], o[:], reads=[bo])
                    P.barrier()


            if "s5" in phases:
                with ExitStack() as ph:
                    TC = 512
                    sp_ = {}
                    bp_ = Buf("s5par")
                    for nm, src in (("lr", s5_lr), ("li", s5_li), ("ldt", s5_ldt)):
                        sp_[nm] = sbt(ph, "s5" + nm, [128, 8], F32)
                        P.dma(sp_[nm][:], src[l], writes=[bp_])
                    def t8(nm, w=8):
                        sp_[nm] = sbt(ph, "s5" + nm, [128, w], F32)
                        return sp_[nm]
                    V = lambda nm: sp_[nm][:]
                    def dv(fn, eng="dve"):
                        P.op(eng, fn, reads=[bp_], writes=[bp_])
                    t8("dt"); t8("lrd"); t8("mag"); t8("ang"); t8("sn"); t8("cs"); t8("abre"); t8("abim"); t8("den"); t8("rden")
                    t8("abm1"); t8("t1"); t8("t2"); t8("cre"); t8("cim")
                    dv(lambda e: e.activation(out=V("dt"), in_=V("ldt"), func=AF.Exp), "act")
                    dv(lambda e: e.tensor_tensor(out=V("lrd"), in0=V("lr"), in1=V("dt"), op=ALU.mult))
                    dv(lambda e: e.activation(out=V("mag"), in_=V("lrd"), func=AF.Exp), "act")
                    dv(lambda e: e.tensor_tensor(out=V("ang"), in0=V("li"), in1=V("dt"), op=ALU.mult))
                    P.barrier()
                    sincos(ph, V("ang"), [128, 8], V("sn"), V("cs"), "s5sc")
                    P.barrier()
                    dv(lambda e: e.tensor_tensor(out=V("abre"), in0=V("mag"), in1=V("cs"), op=ALU.mult))
                    dv(lambda e: e.tensor_tensor(out=V("abim"), in0=V("mag"), in1=V("sn"), op=ALU.mult))
                    dv(lambda e: e.tensor_tensor(out=V("t1"), in0=V("lr"), in1=V("lr"), op=ALU.mult))
                    dv(lambda e: e.tensor_tensor(out=V("t2"), in0=V("li"), in1=V("li"), op=ALU.mult))
                    dv(lambda e: e.tensor_tensor(out=V("den"), in0=V("t1"), in1=V("t2"), op=ALU.add))
                    dv(lambda e: e.reciprocal(out=V("rden"), in_=V("den")))
                    dv(lambda e: e.tensor_scalar(out=V("abm1"), in0=V("abre"), scalar1=-1.0, scalar2=None, op0=ALU.add))
                    dv(lambda e: e.tensor_tensor(out=V("t1"), in0=V("abm1"), in1=V("lr"), op=ALU.mult))
                    dv(lambda e: e.tensor_tensor(out=V("t2"), in0=V("abim"), in1=V("li"), op=ALU.mult))
                    dv(lambda e: e.tensor_tensor(out=V("t1"), in0=V("t1"), in1=V("t2"), op=ALU.add))
                    dv(lambda e: e.tensor_tensor(out=V("cre"), in0=V("t1"), in1=V("rden"), op=ALU.mult))
                    dv(lambda e: e.tensor_tensor(out=V("t1"), in0=V("abim"), in1=V("lr"), op=ALU.mult))
                    dv(lambda e: e.tensor_tensor(out=V("t2"), in0=V("abm1"), in1=V("li"), op=ALU.mult))
                    dv(lambda e: e.tensor_tensor(out=V("t1"), in0=V("t1"), in1=V("t2"), op=ALU.subtract))
                    dv(lambda e: e.tensor_tensor(out=V("cim"), in0=V("t1"), in1=V("rden"), op=ALU.mult))
                    pc = sbt(ph, "s5pc", [128, 8, 10], F32); pn = sbt(ph, "s5pn", [128, 8, 10], F32)
                    dv(lambda e: e.tensor_copy(out=pc[:, :, 0], in_=V("cs")))
                    dv(lambda e: e.tensor_copy(out=pn[:, :, 0], in_=V("sn")))
                    for m in range(1, 10):
                        dv(lambda e: e.tensor_tensor(out=V("t1"), in0=pc[:, :, m - 1], in1=pc[:, :, m - 1], op=ALU.mult))
                        dv(lambda e: e.tensor_tensor(out=V("t2"), in0=pn[:, :, m - 1], in1=pn[:, :, m - 1], op=ALU.mult))
                        dv(lambda e: e.tensor_tensor(out=pc[:, :, m], in0=V("t1"), in1=V("t2"), op=ALU.subtract))
                        dv(lambda e: e.tensor_tensor(out=V("t1"), in0=pc[:, :, m - 1], in1=pn[:, :, m - 1], op=ALU.mult))
                        dv(lambda e: e.tensor_scalar(out=pn[:, :, m], in0=V("t1"), scalar1=2.0, scalar2=None, op0=ALU.mult))
                    Ec = sbt(ph, "s5Ec", [128, 8, TC], F32); Es = sbt(ph, "s5Es", [128, 8, TC], F32)
                    bbT = sbt(ph, "s5bbT", [128, 2, 8, 128], BF16)
                    CTr = sbt(ph, "s5CTr", [128, 8, 128], BF16); CTi = sbt(ph, "s5CTi", [128, 8, 128], BF16)
                    dsk = sbt(ph, "s5dsk", [128, 2], F32); glb = sbt(ph, "s5glb", [128, 4], F32)
                    glw = sbt(ph, "s5glw", [128, 2, 512], BF16)
                    uT = sbt(ph, "s5uT", [128, 2, S], BF16); b_uT = Buf("uT")
                    wbf = Ring(nc, ph, "wbf", [128, NKC, 512], BF16, 1)
                    tmpc = ExitStack()
                    tA = sbt(tmpc, "s5tA", [128, 8, TC // 2], F32); tB = sbt(tmpc, "s5tB", [128, 8, TC // 2], F32)
                    dv(lambda e: e.memset(Ec[:, :, 0:1], 1.0)); dv(lambda e: e.memset(Es[:, :, 0:1], 0.0))
                    k = 1
                    m = 0
                    while k < TC:
                        ck = pc[:, :, m:m + 1].to_broadcast([128, 8, k]); sk = pn[:, :, m:m + 1].to_broadcast([128, 8, k])
                        dv(lambda e: e.tensor_tensor(out=tA[:, :, 0:k], in0=Ec[:, :, 0:k], in1=ck, op=ALU.mult))
                        dv(lambda e: e.tensor_tensor(out=tB[:, :, 0:k], in0=Es[:, :, 0:k], in1=sk, op=ALU.mult))
                        dv(lambda e: e.tensor_tensor(out=Ec[:, :, k:2 * k], in0=tA[:, :, 0:k], in1=tB[:, :, 0:k], op=ALU.subtract))
                        dv(lambda e: e.tensor_tensor(out=tA[:, :, 0:k], in0=Ec[:, :, 0:k], in1=sk, op=ALU.mult))
                        dv(lambda e: e.tensor_tensor(out=tB[:, :, 0:k], in0=Es[:, :, 0:k], in1=ck, op=ALU.mult))
                        dv(lambda e: e.tensor_tensor(out=Es[:, :, k:2 * k], in0=tA[:, :, 0:k], in1=tB[:, :, 0:k], op=ALU.add))
                        k *= 2; m += 1
                    P.barrier()
                    tmpc.close()
                    tmpc = ExitStack()
                    bex_r = sbt(tmpc, "s5bexr", [128, 8, 128], F32); bex_i = sbt(tmpc, "s5bexi", [128, 8, 128], F32)
                    P.dma(bex_r[:], s5_bre[l], writes=[bp_]); P.dma(bex_i[:], s5_bim[l], writes=[bp_])
                    bbs = Ring(nc, tmpc, "s5bbs", [128, 128], F32, 2); bbt = Ring(nc, tmpc, "s5bbt", [128, 128], F32, 2)
                    for j in range(8):
                        for ri in range(2):
                            tt_, btt = bbt.get(); bb_, bbb = bbs.get()
                            if ri == 0:
                                P.op("dve", lambda e: e.tensor_scalar(out=tt_[:], in0=bex_i[:, j, :], scalar1=sp_["cim"][:, j:j + 1], scalar2=None, op0=ALU.mult),
                                     reads=[bp_], writes=[btt])
                                P.op("dve", lambda e: e.scalar_tensor_tensor(out=bb_[:], in0=bex_r[:, j, :], scalar=sp_["cre"][:, j:j + 1], in1=tt_[:],
                                                                             op0=ALU.mult, op1=ALU.subtract), reads=[bp_, btt], writes=[bbb])
                            else:
                                P.op("dve", lambda e: e.tensor_scalar(out=tt_[:], in0=bex_r[:, j, :], scalar1=sp_["cim"][:, j:j + 1], scalar2=None, op0=ALU.mult),
                                     reads=[bp_], writes=[btt])
                                P.op("dve", lambda e: e.scalar_tensor_tensor(out=bb_[:], in0=bex_i[:, j, :], scalar=sp_["cre"][:, j:j + 1], in1=tt_[:],
                                                                             op0=ALU.mult, op1=ALU.add), reads=[bp_, btt], writes=[bbb])
                            pt, bpt = ps.get()
                            P.op("pe", lambda e: e.transpose(out=pt[:, 0:128], in_=bb_[:], identity=identf[:]), reads=[bbb, b_c], writes=[bpt])
                            P.op("act", lambda e: e.activation(out=bbT[:, ri, j, :], in_=pt[:, 0:128], func=AF.Copy), reads=[bpt], writes=[bp_])
                    cst = sbt(tmpc, "s5cst", [128, 8, 128], F32)
                    P.dma(cst[:], s5_cre[l], writes=[bp_])
                    dv(lambda e: e.tensor_copy(out=CTr[:], in_=cst[:]))
                    P.dma(cst[:], s5_cim[l], reads=[bp_], writes=[bp_])
                    dv(lambda e: e.tensor_scalar(out=CTi[:], in0=cst[:], scalar1=-1.0, scalar2=None, op0=ALU.mult))
                    P.dma(dsk[:], s5_d[l], writes=[bp_]); P.dma(glb[:], glu_b[l], writes=[bp_])
                    gst = sbt(tmpc, "s5gst", [128, 2, 512], F32)
                    P.dma(gst[:], glu_w[l].rearrange("(k p) c -> p k c", p=128), writes=[bp_])
                    dv(lambda e: e.tensor_copy(out=glw[:], in_=gst[:]))
                    wst = Ring(nc, tmpc, "wst", [128, NKC, 512], F32, 1)
                    wt, bw = loadw(wst, wbf, w_in[l], [(C_S5U, 256), (C_S5Z, 256)])
                    P.barrier()
                    tmpc.close()
                    for tb in range(NB):
                        for cc in range(2):
                            pt, bpt = ps.get(); mm_fm(pt, bpt, wt, bw, cc * 128, 128, tb)
                            P.op("act", lambda e: e.activation(out=uT[:, cc, tb * 512:(tb + 1) * 512], in_=pt[:], func=AF.Copy), reads=[bpt], writes=[b_uT])
                    ini = sbt(ph, "s5ini", [128, 2, 8], F32); b_ini = Buf("ini")
                    P.op("dve", lambda e: e.memset(ini[:], 0.0), writes=[b_ini])
                    itmp = sbt(ph, "s5itmp", [128, 8, 4], F32)
                    b_inis = [b_ini] * 8
                    b_it = [Buf("it%d" % q) for q in range(8)]
                    nsT9 = sbt(ph, "s5nsT9", [128, 8], F32)
                    P.op("dve", lambda e: e.tensor_scalar(out=nsT9[:], in0=pn[:, :, 9], scalar1=-1.0, scalar2=None, op0=ALU.mult), writes=[b_ini])
                    b_inis = [Buf("ini%d" % q) for q in range(8)]
                    P.barrier()
                    ab_r = Ring(nc, ph, "s5ab", [128, 4, 512], F32, 1)
                    mre = Ring(nc, ph, "s5mre", [128, 2, 512], F32, 2)
                    xre = Ring(nc, ph, "s5xre", [128, 2, 512], F32, 2)
                    dd_r = Ring(nc, ph, "s5dd", [128, 4, 512], F32, 1)
                    xs_r = Ring(nc, ph, "s5xs", [128, 2, 8, 512], BF16, 1)
                    yT_r = Ring(nc, ph, "s5yT", [128, 2, 512], BF16, 1)
                    sg_r = Ring(nc, ph, "s5sg", [128, 512], F32, 1); rs_r = Ring(nc, ph, "s5rs", [128, 512], F32, 1)
                    sz_r = Ring(nc, ph, "s5sz", [128, 512], F32, 1); o_r = Ring(nc, ph, "s5o", [128, 512], BF16, 2)
                    TT = lambda e, o, a, b, op: e.tensor_tensor(out=o, in0=a, in1=b, op=op)
                    ppr = PR([0, 1]); ppi = PR([2, 3]); pep = PR([4, 5, 6, 7])
                    iters = []

                    def mk_s5(tb, j, xs, bxs):
                        t0 = tb * 512
                        pr, bpr = ppr.get(); pi_, bpi = ppi.get()
                        ab, bab = ab_r.get(); mm_, bmm = mre.get(); xx, bxx = xre.get(); dd4, bdd = dd_r.get()
                        cT = pc[:, j, 9:10]; sT = pn[:, j, 9:10]

                        def s0():
                            P.op("pe", lambda e: e.matmul(pr[:], lhsT=bbT[:, 0, j, :], rhs=uT[:, j // 4, t0:t0 + 512], start=True, stop=True),
                                 reads=[b_uT], writes=[bpr])
                            P.op("pe", lambda e: e.matmul(pi_[:], lhsT=bbT[:, 1, j, :], rhs=uT[:, j // 4, t0:t0 + 512], start=True, stop=True),
                                 reads=[b_uT], writes=[bpi])

                        def s1():
                            P.op("dve", lambda e: TT(e, ab[:, 0, :], pr[:], Ec[:, j, :], ALU.mult), reads=[bpr], writes=[bab])
                            P.op("dve", lambda e: TT(e, ab[:, 1, :], pi_[:], Es[:, j, :], ALU.mult), reads=[bpi], writes=[bab])
                            P.op("dve", lambda e: TT(e, ab[:, 2, :], pi_[:], Ec[:, j, :], ALU.mult), reads=[bpi], writes=[bab])
                            P.op("dve", lambda e: TT(e, ab[:, 3, :], pr[:], Es[:, j, :], ALU.mult), reads=[bpr], writes=[bab])
                            P.op("pool", lambda e: TT(e, mm_[:, 0, :], ab[:, 0, :], ab[:, 1, :], ALU.add), reads=[bab], writes=[bmm])
                            P.op("pool", lambda e: TT(e, mm_[:, 1, :], ab[:, 2, :], ab[:, 3, :], ALU.subtract), reads=[bab], writes=[bmm])

                        def s2():
                            P.op("dve", lambda e: e.tensor_tensor_scan(out=xx[:, 0, :], data0=sp_["mag"][:, j:j + 1].to_broadcast([128, 512]), data1=mm_[:, 0, :],
                                                                       initial=ini[:, 0, j:j + 1], op0=ALU.mult, op1=ALU.add), reads=[bmm, b_inis[j]], writes=[bxx])
                            P.op("dve", lambda e: e.tensor_tensor_scan(out=xx[:, 1, :], data0=sp_["mag"][:, j:j + 1].to_broadcast([128, 512]), data1=mm_[:, 1, :],
                                                                       initial=ini[:, 1, j:j + 1], op0=ALU.mult, op1=ALU.add), reads=[bmm, b_inis[j]], writes=[bxx])
                            it_ = itmp[:, j, :]
                            P.op("act", lambda e: e.activation(out=it_[:, 0:1], in_=xx[:, 1, 511:512], func=AF.Copy, scale=nsT9[:, j:j + 1]), reads=[bxx], writes=[b_it[j]])
                            P.op("act", lambda e: e.activation(out=it_[:, 1:2], in_=xx[:, 1, 511:512], func=AF.Copy, scale=cT), reads=[bxx], writes=[b_it[j]])
                            P.op("act", lambda e: e.activation(out=ini[:, 0, j:j + 1], in_=xx[:, 0, 511:512], func=AF.Identity, scale=cT, bias=it_[:, 0:1]),
                                 reads=[bxx, b_it[j]], writes=[b_inis[j]])
                            P.op("act", lambda e: e.activation(out=ini[:, 1, j:j + 1], in_=xx[:, 0, 511:512], func=AF.Identity, scale=sT, bias=it_[:, 1:2]),
                                 reads=[bxx, b_it[j]], writes=[b_inis[j]])

                        def s3():
                            P.op("dve", lambda e: TT(e, dd4[:, 0, :], xx[:, 0, :], Ec[:, j, :], ALU.mult), reads=[bxx], writes=[bdd])
                            P.op("dve", lambda e: TT(e, dd4[:, 1, :], xx[:, 1, :], Es[:, j, :], ALU.mult), reads=[bxx], writes=[bdd])
                            P.op("dve", lambda e: TT(e, dd4[:, 2, :], xx[:, 0, :], Es[:, j, :], ALU.mult), reads=[bxx], writes=[bdd])
                            P.op("pool", lambda e: TT(e, dd4[:, 3, :], xx[:, 1, :], Ec[:, j, :], ALU.mult), reads=[bxx], writes=[bdd])
                            P.op("pool", lambda e: TT(e, xs[:, 0, j, :], dd4[:, 0, :], dd4[:, 1, :], ALU.subtract), reads=[bdd], writes=[bxs])
                            P.op("pool", lambda e: TT(e, xs[:, 1, j, :], dd4[:, 2, :], dd4[:, 3, :], ALU.add), reads=[bdd], writes=[bxs])
                            if j != 7:
                                return
                            yT, byT = yT_r.get()
                            for cc in range(2):
                                py, bpy = pep.get()
                                for jj in range(4):
                                    j2 = cc * 4 + jj
                                    P.op("pe", lambda e: e.matmul(py[:], lhsT=CTr[:, j2, :], rhs=xs[:, 0, j2, :], start=(jj == 0), stop=False), reads=[bxs], writes=[bpy])
                                    P.op("pe", lambda e: e.matmul(py[:], lhsT=CTi[:, j2, :], rhs=xs[:, 1, j2, :], start=False, stop=(jj == 3)), reads=[bxs], writes=[bpy])
                                P.op("dve", lambda e: e.scalar_tensor_tensor(out=yT[:, cc, :], in0=uT[:, cc, t0:t0 + 512], scalar=dsk[:, cc:cc + 1], in1=py[:],
                                                                             op0=ALU.mult, op1=ALU.add), reads=[bpy, b_uT], writes=[byT])
                            for c in range(2):
                                pa, bpa = pep.get(); pg, bpg = pep.get()
                                for kk in range(2):
                                    P.op("pe", lambda e: e.matmul(pa[:], lhsT=glw[:, kk, c * 128:(c + 1) * 128], rhs=yT[:, kk, :], start=(kk == 0), stop=(kk == 1)),
                                         reads=[byT], writes=[bpa])
                                for kk in range(2):
                                    P.op("pe", lambda e: e.matmul(pg[:], lhsT=glw[:, kk, (c + 2) * 128:(c + 3) * 128], rhs=yT[:, kk, :], start=(kk == 0), stop=(kk == 1)),
                                         reads=[byT], writes=[bpg])
                                sg, bsg = sg_r.get()
                                P.op("act", lambda e: e.activation(out=sg[:], in_=pg[:], func=AF.Sigmoid, bias=glb[:, c + 2:c + 3]), reads=[bpg], writes=[bsg])
                                rs, brs = rs_r.get()
                                P.op("dve", lambda e: e.scalar_tensor_tensor(out=rs[:], in0=pa[:], scalar=glb[:, c:c + 1], in1=sg[:], op0=ALU.add, op1=ALU.mult),
                                     reads=[bpa, bsg], writes=[brs])
                                pz, bpz = pep.get(); mm_fm(pz, bpz, wt, bw, 256 + c * 128, 128, tb)
                                sz, bsz = sz_r.get()
                                P.op("act", lambda e: e.activation(out=sz[:], in_=pz[:], func=AF.Silu), reads=[bpz], writes=[bsz])
                                o, bo = o_r.get()
                                P.op("pool", lambda e: TT(e, o[:], rs[:], sz[:], ALU.mult), reads=[brs, bsz], writes=[bo])
                                P.dma(oT_d[3, c * 128:(c + 1) * 128, t0:t0 + 512], o[:], reads=[bo])
                        return [s0, s1, s2, s3]

                    for tb in range(NB):
                        xs, bxs = xs_r.get()
                        for j in range(8):
                            iters.append(mk_s5(tb, j, xs, bxs))
                    run_pipeline(iters, 4)
                    P.barrier()

            if "sb" in phases:
                with ExitStack() as ph:
                    TT = lambda e, o, a, b, op: e.tensor_tensor(out=o, in0=a, in1=b, op=op)
                    sbm = sbt(ph, "sbm", [128, 4, 512], BF16); ustr = sbt(ph, "ustr", [128, 128], BF16)
                    b_k = Buf("sbconst")
                    P.dma(sbm[:], c_sbm, writes=[b_k]); P.dma(ustr[:], c_ustrict, writes=[b_k])
                    F32R = mybir.dt.float32r
                    ustrf = sbt(ph, "ustrf", [128, 128], F32); onesf = sbt(ph, "onesf", [128, 128], F32)
                    P.op("dve", lambda e: e.tensor_copy(out=ustrf[:].bitcast(F32R), in_=ustr[:]), reads=[b_k], writes=[b_k])
                    P.op("dve", lambda e: e.tensor_copy(out=onesf[:].bitcast(F32R), in_=onesb[:]), reads=[b_k, b_c], writes=[b_k])
                    qT = sbt(ph, "sbqT", [128, 2, S], BF16); kT = sbt(ph, "sbkT", [128, 2, S], BF16)
                    vz = sbt(ph, "sbvz", [128, NT, 4, 128], BF16)
                    P.op("pool", lambda e: e.memset(vz[:], 0.0), writes=[b_k])
                    szT = sbt(ph, "sbszT", [128, 2, S], BF16)
                    tmpc = ExitStack()
                    wbf = Ring(nc, tmpc, "wbf", [128, NKC, 512], BF16, 1)
                    wst = Ring(nc, tmpc, "wst", [128, NKC, 512], F32, 1)
                    wt, bw = loadw(wst, wbf, w_in[l], [(C_SB, 512)])
                    for tb in range(NB):
                        for c4 in range(4):
                            pt, bpt = ps.get(); mm_fm(pt, bpt, wt, bw, c4 * 128, 128, tb)
                            bsl = slice(tb * 512, (tb + 1) * 512)
                            if c4 < 2:
                                P.op("act", lambda e: e.activation(out=qT[:, c4, bsl], in_=pt[:], func=AF.Copy), reads=[bpt], writes=[b_k])
                            else:
                                P.op("dve", lambda e: e.tensor_copy(out=kT[:, c4 - 2, bsl], in_=pt[:]), reads=[bpt], writes=[b_k])
                    wt, bw = loadw(wst, wbf, w_in[l], [(C_SB + 512, 256), (C_SBZ, 256)])
                    for tt in range(NT):
                        pt, bpt = ps.get(); mm_tm(pt, bpt, wt, bw, 0, 256, tt)
                        vz4 = vz[:, tt].rearrange("p (a b) c -> p a b c", b=2)
                        pt4 = pt[:, 0:256].rearrange("p (a b d) -> p a b d", b=2, d=64)
                        P.op("act", lambda e: e.activation(out=vz4[:, :, 0, 0:64], in_=pt4[:, :, 0, :], func=AF.Copy), reads=[bpt], writes=[b_k])
                        P.op("dve", lambda e: e.tensor_copy(out=vz4[:, :, 1, 64:128], in_=pt4[:, :, 1, :]), reads=[bpt], writes=[b_k])
                    for tb in range(NB):
                        for hc in range(2):
                            pt, bpt = ps.get(); mm_fm(pt, bpt, wt, bw, 256 + hc * 128, 128, tb)
                            P.op("act", lambda e: e.activation(out=szT[:, hc, tb * 512:(tb + 1) * 512], in_=pt[:], func=AF.Silu), reads=[bpt], writes=[b_k])
                    P.barrier()
                    tmpc.close()
                    pz_ = PR([0, 1, 2]); pc_ = PR([3, 4]); pk_ = PR([5, 6]); pa_ = PR([7])
                    e1_r = Ring(nc, ph, "sbe1", [128, 512], F32, 2); sp_r = Ring(nc, ph, "sbsp", [128, 512], F32, 3)
                    spb_r = Ring(nc, ph, "sbspb", [128, 512], F32, 3); t1_r = Ring(nc, ph, "sbt1", [128, 512], F32, 3)
                    t2_r = Ring(nc, ph, "sbt2", [128, 512], F32, 3); w_r = Ring(nc, ph, "sbw", [128, 512], BF16, 3)
                    spp_r = Ring(nc, ph, "sbspp", [128, 512], F32, 2)
                    MBIG = 200.0
                    mbias = sbt(ph, "sbmbias", [128, 2], F32)
                    P.op("pool", lambda e: e.memset(mbias[:], -MBIG), writes=[b_k])
                    R_r = Ring(nc, ph, "sbR", [128, 512], F32, 3)
                    rstate = {}
                    sz_r = Ring(nc, ph, "sbsz", [128, 512], F32, 1); o_r = Ring(nc, ph, "sbo", [128, 512], BF16, 2)
                    iters = []

                    qz_r = Ring(nc, ph, "sbqz", [128, 2, 512], BF16, 2)
                    for qi in range(2):
                        P.op("pool", lambda e: e.memset(qz_r.t[qi][:], 0.0), writes=[qz_r.b[qi]])

                    def mk_iter(tb, hc, hh, i, pacc, bacc, qz, bqz):
                        t0 = tb * 512
                        h = hc * 2 + hh
                        plo, phi = hh * 64, hh * 64 + 64
                        imax = 4 * tb + 3
                        d = i - 4 * tb
                        first = (i == imax)
                        pz, bpz = pz_.get(); e1, be1 = e1_r.get(); sp, bsp = sp_r.get(); spb, bspb = spb_r.get()
                        pcm, bpcm = pc_.get(); pcl, bpcl = pk_.get(); t1, bt1 = t1_r.get(); t2, bt2 = t2_r.get()
                        w, bw_ = w_r.get()
                        wm, bwm = w, bw_
                        spp = bspp = None
                        if d >= 0:
                            spp, bspp = spp_r.get()
                        Rin = None if first else rstate["R"]
                        Rout = None
                        if i > 0:
                            Rout = R_r.get()
                            rstate["R"] = Rout

                        def stA0():
                            P.op("pe", lambda e: e.matmul(pz[:], lhsT=kT[:, hc, i * 128:(i + 1) * 128], rhs=qz[:, hh, :],
                                                          start=True, stop=True), reads=[bqz], writes=[bpz])

                        def stA():
                            P.op("act", lambda e: e.activation(out=e1[:], in_=pz[:], func=AF.Exp, scale=SCALE), reads=[bpz], writes=[be1])
                            P.op("act", lambda e: e.activation(out=sp[:].bitcast(F32R), in_=e1[:], func=AF.Ln, bias=1.0), reads=[be1], writes=[bsp])
                            if d >= 0:
                                P.op("pool", lambda e: TT(e, spb[:].bitcast(F32R), sp[:], sbm[:, d, :], ALU.mult), reads=[bsp], writes=[bspb])
                                P.op("dve", lambda e: e.scalar_tensor_tensor(out=spp[:], in0=sbm[:, d, :], scalar=-MBIG, in1=sp[:], op0=ALU.mult, op1=ALU.add),
                                     reads=[bsp], writes=[bspp])

                        def stB0():
                            sp1, bsp1 = (spp, bspp) if d >= 0 else (sp, bsp)
                            P.op("dve", lambda e: e.scalar_tensor_tensor(out=t1[:], in0=pz[:], scalar=SCALE, in1=sp1[:], op0=ALU.mult, op1=ALU.subtract),
                                 reads=[bpz, bsp1], writes=[bt1])
                            rsp, brsp = (spb, bspb) if d >= 0 else (sp, bsp)
                            P.op("pe", lambda e: e.matmul(pcm[:], lhsT=ustrf[:].bitcast(F32R), rhs=rsp[:].bitcast(F32R), start=True, stop=True), reads=[brsp], writes=[bpcm])
                            if i > 0:
                                P.op("pe", lambda e: e.matmul(pcl[:], lhsT=onesf[:].bitcast(F32R), rhs=rsp[:].bitcast(F32R), start=True, stop=True), reads=[brsp], writes=[bpcl])

                        def stB():
                            P.op("dve", lambda e: TT(e, t2[:], t1[:], pcm[:], ALU.subtract), reads=[bt1, bpcm], writes=[bt2])
                            if Rout is not None:
                                if Rin is None:
                                    P.op("dve", lambda e: e.tensor_copy(out=Rout[0][:], in_=pcl[:]), reads=[bpcl], writes=[Rout[1]])
                                else:
                                    P.op("dve", lambda e: TT(e, Rout[0][:], Rin[0][:], pcl[:], ALU.add), reads=[bpcl, Rin[1]], writes=[Rout[1]])

                        def stB2():
                            if Rin is not None:
                                P.op("pool", lambda e: TT(e, t2[:], t2[:], Rin[0][:], ALU.subtract), reads=[bt2, Rin[1]], writes=[bt2])

                        def stC0():
                            if d >= 0:
                                P.op("act", lambda e: e.activation(out=w[:], in_=t2[:], func=AF.Exp, bias=mbias[:, 0:1]), reads=[bt2], writes=[bw_])
                            else:
                                P.op("act", lambda e: e.activation(out=w[:], in_=t2[:], func=AF.Exp), reads=[bt2], writes=[bw_])

                        def stC():
                            P.op("pe", lambda e: e.matmul(pacc[:], lhsT=vz[:, i, h, :], rhs=wm[:],
                                                          start=(first and hh == 0), stop=(i == 0 and hh == 1), skip_group_check=True), reads=[bwm], writes=[bacc])
                            if i == 0 and hh == 1:
                                o, bo = o_r.get()
                                P.op("dve", lambda e: TT(e, o[:], pacc[:], szT[:, hc, t0:t0 + 512], ALU.mult), reads=[bacc], writes=[bo])
                                P.dma(oT_d[2, hc * 128:(hc + 1) * 128, t0:t0 + 512], o[:], reads=[bo])
                        return [stA0, stA, stB0, stB, stB2, stC0, stC]

                    def mk_qz(tb, hc, qz, bqz):
                        def s():
                            t0 = tb * 512
                            P.op("pool", lambda e: e.tensor_copy(out=qz[0:64, 0, :], in_=qT[0:64, hc, t0:t0 + 512]), writes=[bqz])
                            P.op("pool", lambda e: e.tensor_copy(out=qz[64:128, 1, :], in_=qT[64:128, hc, t0:t0 + 512]), writes=[bqz])
                        return [s, None, None, None, None, None, None]

                    for tb in range(NB):
                        for hc in range(2):
                            pacc, bacc = pa_.get()
                            qz, bqz = qz_r.get()
                            iters.append(mk_qz(tb, hc, qz, bqz))
                            for hh in range(2):
                                for i in range(4 * tb + 3, -1, -1):
                                    iters.append(mk_iter(tb, hc, hh, i, pacc, bacc, qz, bqz))
                    run_pipeline(iters, 7)
                    P.barrier()

            if "nsa" in phases:
                with ExitStack() as ph:
                    TT = lambda e, o, a, b, op: e.tensor_tensor(out=o, in0=a, in1=b, op=op)
                    NCH = (NCMP + 127) // 128
                    b_k = Buf("nsaconst")
                    kcmpT = sbt(ph, "kcmpT", [128, 2, 256], BF16)
                    vcX = sbt(ph, "vcX", [128, 2, 2, 130], BF16)
                    tmpc = ExitStack()
                    g1 = sbt(tmpc, "g1", [128, 128], F32)
                    P.dma(g1[:], qkg1[l:l + 1, :].partition_broadcast(128), writes=[b_k])
                    wbf = Ring(nc, tmpc, "wbf", [128, NKC, 512], BF16, 1)
                    wst = Ring(nc, tmpc, "wst", [128, NKC, 512], F32, 1)
                    ovs = sbt(tmpc, "ovs", [128, 2, 66], F32)
                    P.dma(ovs[:], c_ovl, writes=[b_k])
                    for kh in range(2):
                        P.op("dve", lambda e: e.tensor_copy(out=vcX[:, :, kh, 64:130], in_=ovs[:]), reads=[b_k], writes=[b_k])
                    pet = sbt(tmpc, "pet", [128, 2, 32], F32)
                    P.dma(pet[:], pe_t[l].rearrange("j p l -> p j l"), writes=[b_k])
                    w1b = sbt(tmpc, "w1b", [128, 2, 32, 128], BF16)
                    for j in range(2):
                        for hf in range(2):
                            cast_dma(w1b[:, j, hf * 16:(hf + 1) * 16, :].rearrange("p a b -> p (a b)"), w1bd[l, j][:, hf * 2048:(hf + 1) * 2048], b_k)
                    w2s = sbt(tmpc, "w2s", [128, 2, 128], F32); w2b = sbt(tmpc, "w2b", [128, 2, 128], BF16)
                    P.dma(w2s[:], w2bd[l].rearrange("j p c -> p j c"), writes=[b_k])
                    P.op("dve", lambda e: e.tensor_copy(out=w2b[:], in_=w2s[:]), reads=[b_k], writes=[b_k])
                    kAB = sbt(tmpc, "kAB", [128, 2, 2, S], BF16)
                    hidT = sbt(tmpc, "hidT", [128, 2, 256], BF16)
                    P.op("pool", lambda e: e.memset(hidT[:], 0.0), writes=[b_k])
                    wt, bw = loadw(wst, wbf, w_in[l], [(C_NKV, 256)])
                    for tb in range(NB):
                        for j in range(2):
                            pt, bpt = ps.get(); mm_fm(pt, bpt, wt, bw, j * 128, 128, tb)
                            for ab in range(2):
                                P.op("dve", lambda e: TT(e, kAB[:, j, ab, tb * 512:(tb + 1) * 512].rearrange("p (a b) -> p a b", b=16),
                                                         pt[:].rearrange("p (a b) -> p a b", b=16),
                                                         pet[:, j, ab * 16:(ab + 1) * 16].unsqueeze(1).to_broadcast([128, 32, 16]), ALU.add),
                                     reads=[bpt, b_k], writes=[b_k])
                    sq_r = Ring(nc, tmpc, "csq", [128, 128], F32, 2); st_r = Ring(nc, tmpc, "cst", [128, 8], F32, 2)
                    xn_r = Ring(nc, tmpc, "cxn", [128, 128], F32, 2); ra_r = Ring(nc, tmpc, "cra", [128, 2, 32], F32, 4)
                    rk_r = Ring(nc, tmpc, "crk", [128, 2, 2, 32], BF16, 2); dup_r = Ring(nc, tmpc, "cdup", [128, 2, 2, 64], BF16, 2)
                    for j in range(2):
                        phd, bphd = ps.get()
                        for ll in range(32):
                            ab = ll // 16
                            rhs = kAB[:, j, ab, :].rearrange("p (a b) -> p a b", b=16)[:, ab:ab + NCMP, ll % 16]
                            P.op("pe", lambda e: e.matmul(phd[:, 0:NCMP], lhsT=w1b[:, j, ll, :], rhs=rhs, start=(ll == 0), stop=(ll == 31)),
                                 reads=[b_k], writes=[bphd])
                        P.op("act", lambda e: e.activation(out=hidT[:, j, 0:NCMP], in_=phd[:, 0:NCMP], func=AF.Silu), reads=[bphd, b_k], writes=[b_k])
                        for nch in range(NCH):
                            po, bpo = ps.get()
                            P.op("pe", lambda e: e.matmul(po[:, 0:128], lhsT=hidT[:, j, nch * 128:(nch + 1) * 128], rhs=w2b[:, j, :], start=True, stop=True),
                                 reads=[b_k], writes=[bpo])
                            if j == 1:
                                P.op("act", lambda e: e.activation(out=vcX[:, nch, :, 0:64], in_=po[:, 0:128].rearrange("p (k d) -> p k d", k=2), func=AF.Copy),
                                     reads=[bpo, b_k], writes=[b_k])
                                continue
                            sq, bsq = sq_r.get(); st, bst = st_r.get()
                            P.op("act", lambda e: e.activation(out=sq[:], in_=po[:, 0:128], func=AF.Square), reads=[bpo], writes=[bsq])
                            P.op("dve", lambda e: e.reduce_sum(out=st[:, 0:2], in_=sq[:].rearrange("p (k d) -> p k d", k=2), axis=AX.X), reads=[bsq], writes=[bst])
                            P.op("act", lambda e: e.activation(out=st[:, 2:4], in_=st[:, 0:2], func=AF.Sqrt, bias=EPS, scale=1.0 / 64), reads=[bst], writes=[bst])
                            P.op("dve", lambda e: e.reciprocal(out=st[:, 4:6], in_=st[:, 2:4]), reads=[bst], writes=[bst])
                            xn, bxn = xn_r.get()
                            P.op("dve", lambda e: TT(e, xn[:].rearrange("p (k d) -> p k d", k=2), po[:, 0:128].rearrange("p (k d) -> p k d", k=2),
                                                     st[:, 4:6].unsqueeze(2).to_broadcast([128, 2, 64]), ALU.mult), reads=[bpo, bst], writes=[bxn])
                            P.op("dve", lambda e: TT(e, xn[:], xn[:], g1[:], ALU.mult), reads=[bxn, b_k], writes=[bxn])
                            x4 = xn[:].rearrange("p (k h d) -> p k h d", k=2, h=2)
                            cb = cosC[:, nch, :].unsqueeze(1).to_broadcast([128, 2, 32]); sb_ = sinC[:, nch, :].unsqueeze(1).to_broadcast([128, 2, 32])
                            rk, brk = rk_r.get()
                            a_, ba = ra_r.get(); b_, bb = ra_r.get()
                            P.op("dve", lambda e: TT(e, a_[:], x4[:, :, 0, :], cb, ALU.mult), reads=[bxn], writes=[ba])
                            P.op("dve", lambda e: TT(e, b_[:], x4[:, :, 1, :], sb_, ALU.mult), reads=[bxn], writes=[bb])
                            P.op("dve", lambda e: TT(e, rk[:, :, 0, :], a_[:], b_[:], ALU.subtract), reads=[ba, bb], writes=[brk])
                            a_, ba = ra_r.get(); b_, bb = ra_r.get()
                            P.op("dve", lambda e: TT(e, a_[:], x4[:, :, 1, :], cb, ALU.mult), reads=[bxn], writes=[ba])
                            P.op("dve", lambda e: TT(e, b_[:], x4[:, :, 0, :], sb_, ALU.mult), reads=[bxn], writes=[bb])
                            P.op("dve", lambda e: TT(e, rk[:, :, 1, :], a_[:], b_[:], ALU.add), reads=[ba, bb], writes=[brk])
                            dup, bdup = dup_r.get()
                            P.op("dve", lambda e: e.tensor_copy(out=dup[:], in_=rk[:].rearrange("p k h d -> p k (h d)").unsqueeze(2).to_broadcast([128, 2, 2, 64])),
                                 reads=[brk], writes=[bdup])
                            ptr, bptr = ps.get()
                            ptb = ptr[:].bitcast(BF16)
                            for kh in range(2):
                                P.op("pe", lambda e: e.transpose(out=ptb[:, kh * 128:(kh + 1) * 128], in_=dup[:, kh].rearrange("p a d -> p (a d)"), identity=identb[:]),
                                     reads=[bdup], writes=[bptr])
                            P.op("act", lambda e: e.activation(out=kcmpT[:, :, nch * 128:(nch + 1) * 128], in_=ptb[:, 0:256].rearrange("p (k n) -> p k n", k=2), func=AF.Copy),
                                 reads=[bptr], writes=[b_k])
                    if NCH == 1:
                        P.op("pool", lambda e: e.memset(kcmpT[:, :, 128:256], 0.0), reads=[b_k], writes=[b_k])
                        P.op("pool", lambda e: e.memset(vcX[:, 1, :, 0:64], 0.0), reads=[b_k], writes=[b_k])
                    P.barrier()
                    tmpc.close()
                    qT = sbt(ph, "nqT", [128, 2, S], BF16); ksT = sbt(ph, "nksT", [128, 2, S], BF16); kwT = sbt(ph, "nkwT", [128, 2, S], BF16)
                    vS = sbt(ph, "nvS", [128, NT, 2, 66], BF16); vW = sbt(ph, "nvW", [128, NT, 2, 66], BF16)
                    gS = sbt(ph, "ngS", [128, NT, 12], F32)
                    wz = sbt(ph, "nwz", [128, NKC, 256], BF16)
                    P.op("pool", lambda e: e.memset(vS[:, :, :, 64:66], 1.0), writes=[b_k])
                    P.op("pool", lambda e: e.memset(vW[:, :, :, 64:66], 1.0), writes=[b_k])
                    tmpc = ExitStack()
                    gq = sbt(tmpc, "gq", [128, 512], F32)
                    P.dma(gq[:], qkg_row[l:l + 1, :].partition_broadcast(128), writes=[b_k])
                    wbf = Ring(nc, tmpc, "wbf", [128, NKC, 512], BF16, 1)
                    wst = Ring(nc, tmpc, "wst", [128, NKC, 512], F32, 1)
                    wbf2 = Ring(nc, tmpc, "wbf2", [128, NKC, 512], BF16, 1)
                    wt, bw = loadw(wst, wbf, w_in[l], [(C_NZ, 256)])
                    P.op("pool", lambda e: e.tensor_copy(out=wz[:], in_=wt[:, :, 0:256]), reads=[bw], writes=[b_k])
                    wtA, bwA = loadw(wst, wbf, w_in[l], [(C_NQ, 256), (C_NKV + 256, 128), (C_NKV + 512, 128)])
                    wtB, bwB = loadw(wst, wbf2, w_in[l], [(C_NKV + 384, 128), (C_NKV + 640, 128), (C_NG, 12)])
                    sq_r = Ring(nc, tmpc, "nsq", [128, 512], F32, 2); st_r = Ring(nc, tmpc, "nst", [128, 24], F32, 2)
                    xn_r = Ring(nc, tmpc, "nxn", [128, 512], F32, 2); ra_r = Ring(nc, tmpc, "nra", [128, 8, 32], F32, 8)
                    rq_r = Ring(nc, tmpc, "nrq", [128, 8, 2, 32], BF16, 4); dup_r = Ring(nc, tmpc, "ndup", [128, 4, 2, 64], BF16, 3)
                    pp1 = PR([0, 1, 2, 3]); pp2 = PR([4, 5]); ppt = PR([6, 7])
                    st_r = Ring(nc, tmpc, "nst3", [128, 24], F32, 3)
                    iters = []

                    def mk_pre(tt):
                        p1, bp1 = pp1.get(); p2, bp2 = pp2.get(); ptr, bptr = ppt.get()
                        sq, bsq = sq_r.get(); st, bst = st_r.get(); xn, bxn = xn_r.get()
                        rq, brq = rq_r.get(); dup, bdup = dup_r.get()
                        tsl = slice(tt * 128, (tt + 1) * 128)

                        def s0():
                            mm_tm(p1, bp1, wtA, bwA, 0, 512, tt)
                            mm_tm(p2, bp2, wtB, bwB, 0, 268, tt)

                        def s1():
                            P.op("act", lambda e: e.activation(out=sq[:], in_=p1[:], func=AF.Square), reads=[bp1], writes=[bsq])
                            P.op("act", lambda e: e.activation(out=vS[:, tt, :, 0:64], in_=p2[:, 0:128].rearrange("p (k d) -> p k d", k=2), func=AF.Copy), reads=[bp2], writes=[b_k])
                            P.op("dve", lambda e: e.tensor_copy(out=vW[:, tt, :, 0:64], in_=p2[:, 128:256].rearrange("p (k d) -> p k d", k=2)), reads=[bp2], writes=[b_k])
                            P.op("act", lambda e: e.activation(out=gS[:, tt, :], in_=p2[:, 256:268], func=AF.Copy), reads=[bp2], writes=[b_k])

                        def s2():
                            P.op("dve", lambda e: e.reduce_sum(out=st[:, 0:8], in_=sq[:].rearrange("p (k d) -> p k d", k=8), axis=AX.X), reads=[bsq], writes=[bst])
                            P.op("act", lambda e: e.activation(out=st[:, 8:16], in_=st[:, 0:8], func=AF.Sqrt, bias=EPS, scale=1.0 / 64), reads=[bst], writes=[bst])
                            P.op("dve", lambda e: e.reciprocal(out=st[:, 16:24], in_=st[:, 8:16]), reads=[bst], writes=[bst])

                        def s3():
                            P.op("dve", lambda e: TT(e, xn[:].rearrange("p (k d) -> p k d", k=8), p1[:].rearrange("p (k d) -> p k d", k=8),
                                                     st[:, 16:24].unsqueeze(2).to_broadcast([128, 8, 64]), ALU.mult), reads=[bp1, bst], writes=[bxn])
                            P.op("pool", lambda e: TT(e, xn[:], xn[:], gq[:], ALU.mult), reads=[bxn, b_k], writes=[bxn])

                        ra4 = [ra_r.get() for _ in range(4)]

                        def s4():
                            x4 = xn[:].rearrange("p (k h d) -> p k h d", k=8, h=2)
                            cb = cosT[:, tt, :].unsqueeze(1).to_broadcast([128, 8, 32]); sb_ = sinT[:, tt, :].unsqueeze(1).to_broadcast([128, 8, 32])
                            P.op("dve", lambda e: TT(e, ra4[0][0][:], x4[:, :, 0, :], cb, ALU.mult), reads=[bxn], writes=[ra4[0][1]])
                            P.op("pool", lambda e: TT(e, ra4[1][0][:], x4[:, :, 1, :], sb_, ALU.mult), reads=[bxn], writes=[ra4[1][1]])
                            P.op("dve", lambda e: TT(e, ra4[2][0][:], x4[:, :, 1, :], cb, ALU.mult), reads=[bxn], writes=[ra4[2][1]])
                            P.op("pool", lambda e: TT(e, ra4[3][0][:], x4[:, :, 0, :], sb_, ALU.mult), reads=[bxn], writes=[ra4[3][1]])

                        def s4b():
                            P.op("dve", lambda e: TT(e, rq[:, :, 0, :], ra4[0][0][:], ra4[1][0][:], ALU.subtract), reads=[ra4[0][1], ra4[1][1]], writes=[brq])
                            P.op("dve", lambda e: TT(e, rq[:, :, 1, :], ra4[2][0][:], ra4[3][0][:], ALU.add), reads=[ra4[2][1], ra4[3][1]], writes=[brq])

                        def s4c():
                            rqf = rq[:].rearrange("p k h d -> p k (h d)")
                            P.op("pool", lambda e: e.tensor_copy(out=dup[:], in_=rqf[:, 4:8, :].unsqueeze(2).to_broadcast([128, 4, 2, 64])), reads=[brq], writes=[bdup])

                        def s5():
                            ptb = ptr[:].bitcast(BF16)
                            rq2 = rq[:].rearrange("p k h d -> p (k h d)")
                            for c in range(2):
                                P.op("pe", lambda e: e.transpose(out=ptb[:, c * 128:(c + 1) * 128], in_=rq2[:, c * 128:(c + 1) * 128], identity=identb[:]),
                                     reads=[brq], writes=[bptr])
                            for c in range(4):
                                P.op("pe", lambda e: e.transpose(out=ptb[:, (2 + c) * 128:(3 + c) * 128], in_=dup[:, c].rearrange("p a d -> p (a d)"), identity=identb[:]),
                                     reads=[bdup], writes=[bptr])

                        def s6():
                            ptb = ptr[:].bitcast(BF16)
                            P.op("act", lambda e: e.activation(out=qT[:, :, tsl], in_=ptb[:, 0:256].rearrange("p (k n) -> p k n", k=2), func=AF.Copy), reads=[bptr], writes=[b_k])
                            P.op("act", lambda e: e.activation(out=ksT[:, :, tsl], in_=ptb[:, 256:512].rearrange("p (k n) -> p k n", k=2), func=AF.Copy), reads=[bptr], writes=[b_k])
                            P.op("dve", lambda e: e.tensor_copy(out=kwT[:, :, tsl], in_=ptb[:, 512:768].rearrange("p (k n) -> p k n", k=2)), reads=[bptr], writes=[b_k])
                        return [s0, s1, s2, s3, s4, s4b, s4c, s5, s6]

                    for tt in range(NT):
                        iters.append(mk_pre(tt))
                    run_pipeline(iters, 9)
                    P.op("act", lambda e: e.activation(out=gS[:], in_=gS[:], func=AF.Sigmoid), reads=[b_k], writes=[b_k])
                    P.barrier()
                    tmpc.close()
                    slm = sbt(ph, "nslm", [128, 4, 512], BF16); band = sbt(ph, "nband", [128, 8, 512], BF16)
                    eexp = sbt(ph, "neexp", [128, S], BF16); svm = sbt(ph, "nsvm", [128, NT, 64], BF16); sfv = sbt(ph, "nsfv", [128, NT, 64], BF16)
                    for dst, src in ((slm, c_slm), (band, c_band), (eexp, c_eexp), (svm, c_selvm), (sfv, c_selfv)):
                        P.dma(dst[:], src, writes=[b_k])
                    P.barrier()
                    pacc = [(pst[k], psb[k]) for k in range(4)]
                    psc = PR([4, 5]); pmx = PR([6, 7])
                    e_r = Ring(nc, ph, "ne", [128, 512], BF16, 3); em_r = Ring(nc, ph, "nem", [128, 512], BF16, 3)
                    mm_r = Ring(nc, ph, "nmm", [128, 512], BF16, 3)
                    oacc = sbt(ph, "noacc", [128, 4, 256], F32); b_oa = Buf("oacc")
                    imp = sbt(ph, "nimp", [128, 4, 2, 64], F32); b_imp = Buf("imp")
                    impf8 = sbt(ph, "nimpf8", [128, 8, 64], F32); impt8 = sbt(ph, "nimpt8", [128, 8, 64], F32)
                    m8a = sbt(ph, "nm8a", [128, 8, 8], F32); m8b = sbt(ph, "nm8b", [128, 8, 8], F32); selb8 = sbt(ph, "nselb8", [128, 8, 64], BF16)
                    selT = sbt(ph, "nselT", [128, 2, 512], BF16); b_selT = Buf("selT")
                    P.op("pool", lambda e: e.memset(selT[:], 0.0), writes=[b_selT])
                    qz_r = Ring(nc, ph, "nqz", [128, 4, 512], BF16, 1)
                    for qi in range(1):
                        P.op("pool", lambda e: e.memset(qz_r.t[qi][:], 0.0), writes=[qz_r.b[qi]])
                    sc_r = Ring(nc, ph, "nsc8", [128, 12], F32, 4)
                    sz_r = Ring(nc, ph, "nsz", [128, 512], F32, 1); ob_r = Ring(nc, ph, "nob", [128, 4, 256], BF16, 1)
                    oTb_r = Ring(nc, ph, "noTb", [128, 2, 512], BF16, 1)

                    ftmp_r = Ring(nc, ph, "nftmp", [128, 4, 64], F32, 1)

                    def finalize(view, bank_b, dencol, ts0, nts, tb, hq, gidx, first, imp_g=None):
                        sc8, bsc = sc_r.get()
                        T0 = 4 * tb + ts0
                        P.op("dve", lambda e: e.tensor_scalar(out=sc8[:, 0:nts].unsqueeze(2), in0=view[:, :, dencol:dencol + 1], scalar1=1e-30, scalar2=None, op0=ALU.max),
                             reads=[bank_b], writes=[bsc])
                        P.op("dve", lambda e: e.reciprocal(out=sc8[:, 4:4 + nts], in_=sc8[:, 0:nts]), reads=[bsc], writes=[bsc])
                        P.op("dve", lambda e: TT(e, sc8[:, 8:8 + nts], sc8[:, 4:4 + nts], gS[:, T0:T0 + nts, gidx], ALU.mult), reads=[bsc], writes=[bsc])
                        osl = oacc[:, ts0:ts0 + nts, hq * 64:(hq + 1) * 64]
                        sgb = sc8[:, 8:8 + nts].unsqueeze(2).to_broadcast([128, nts, 64])
                        if first:
                            P.op("dve", lambda e: TT(e, osl, view[:, :, 0:64], sgb, ALU.mult), reads=[bank_b, bsc], writes=[b_oa])
                        else:
                            ft, bft = ftmp_r.get()
                            P.op("dve", lambda e: TT(e, ft[:, 0:nts, :], view[:, :, 0:64], sgb, ALU.mult), reads=[bank_b, bsc], writes=[bft])
                            P.op("pool", lambda e: TT(e, osl, osl, ft[:, 0:nts, :], ALU.add), reads=[bft, b_oa], writes=[b_oa])
                        if imp_g is not None:
                            kh_, g_ = imp_g
                            isl = imp[:, ts0:ts0 + nts, kh_, :]
                            rb = sc8[:, 4:4 + nts].unsqueeze(2).to_broadcast([128, nts, 64])
                            if g_ == 0:
                                P.op("dve", lambda e: TT(e, isl, view[:, :, 64:128], rb, ALU.mult), reads=[bank_b, bsc], writes=[b_imp])
                            else:
                                ft, bft = ftmp_r.get()
                                P.op("dve", lambda e: TT(e, ft[:, 0:nts, :], view[:, :, 64:128], rb, ALU.mult), reads=[bank_b, bsc], writes=[bft])
                                P.op("dve", lambda e: TT(e, isl, isl, ft[:, 0:nts, :], ALU.add), reads=[bft, b_imp], writes=[b_imp])

                    for tb in range(NB if "nsa_pre" not in phases else 0):
                        t0 = tb * 512
                        qz, bqz = qz_r.get()
                        for hq in range(4):
                            kh, g = hq // 2, hq % 2
                            P.op("pool" if hq % 2 else "act",
                                 (lambda e: e.tensor_copy(out=qz[g * 64:(g + 1) * 64, hq, :], in_=qT[g * 64:(g + 1) * 64, kh, t0:t0 + 512])) if hq % 2 else
                                 (lambda e: e.activation(out=qz[g * 64:(g + 1) * 64, hq, :], in_=qT[g * 64:(g + 1) * 64, kh, t0:t0 + 512], func=AF.Copy)),
                                 writes=[bqz])
                        nchs = [n_ for n_ in range(NCH) if 16 * (n_ * 128) + 31 <= t0 + 511]
                        iters = []

                        def mk_cmp(hq, nch):
                            kh, g = hq // 2, hq % 2
                            plo, phi = g * 64, g * 64 + 64
                            pair = (pacc[0], pacc[1]) if hq % 2 == 0 else (pacc[2], pacc[3])
                            sc, bsc_ = psc.get(); e_, be = e_r.get(); em, bem = em_r.get()

                            def s0():
                                P.op("pe", lambda e: e.matmul(sc[:], lhsT=kcmpT[:, kh, nch * 128:(nch + 1) * 128], rhs=qz[:, hq, :],
                                                              start=True, stop=True), reads=[bqz], writes=[bsc_])

                            def s1():
                                P.op("act", lambda e: e.activation(out=e_[:], in_=sc[:], func=AF.Exp, scale=SCALE), reads=[bsc_], writes=[be])

                            def s2():
                                P.op("pool", lambda e: e.affine_select(out=em[:], in_=e_[:], pattern=[[1, 512]], compare_op=ALU.is_ge, fill=0.0,
                                                                       base=t0 - 16 * 128 * nch - 31, channel_multiplier=-16), reads=[be], writes=[bem])

                            def s3():
                                for ts in range(4):
                                    bank, bbank = pair[ts // 2]
                                    region = bank[:, (ts % 2) * 130:(ts % 2) * 130 + 130]
                                    P.op("pe", lambda e: e.matmul(region, lhsT=em[:, ts * 128:(ts + 1) * 128], rhs=vcX[:, nch, kh, :],
                                                                  start=(nch == nchs[0] and ts % 2 == 0), stop=(nch == nchs[-1]), skip_group_check=True),
                                         reads=[bem], writes=[bbank])
                                if nch == nchs[-1]:
                                    for hb in range(2):
                                        bank, bbank = pair[hb]
                                        finalize(bank[:, 0:260].rearrange("p (a w) -> p a w", a=2), bbank, 128, 2 * hb, 2, tb, hq, hq, True, imp_g=(kh, g))
                            return [s0, s1, s2, s3]

                        for hq in range(4):
                            for nch in nchs:
                                iters.append(mk_cmp(hq, nch))
                        run_pipeline(iters, 4)
                        bif = Buf("impf")
                        P.op("dve", lambda e: TT(e, impf8[:].rearrange("p (k t) j -> p k t j", k=2), imp[:].rearrange("p t k j -> p k t j"),
                                                 svm[:, 4 * tb:4 * tb + 4, :].unsqueeze(1).to_broadcast([128, 2, 4, 64]), ALU.mult), reads=[b_imp], writes=[bif])
                        P.op("dve", lambda e: TT(e, impf8[:].rearrange("p (k t) j -> p k t j", k=2), impf8[:].rearrange("p (k t) j -> p k t j", k=2),
                                                 sfv[:, 4 * tb:4 * tb + 4, :].unsqueeze(1).to_broadcast([128, 2, 4, 64]), ALU.add), reads=[bif], writes=[bif])
                        gb = [Buf("g%d" % q) for q in range(8)]
                        for q in range(8):
                            P.op("dve", lambda e: e.max(out=m8a[:, q, :], in_=impf8[:, q, :]), reads=[bif], writes=[gb[q]])
                        for q in range(8):
                            P.op("dve", lambda e: e.match_replace(out=impt8[:, q, :], in_to_replace=m8a[:, q, :], in_values=impf8[:, q, :], imm_value=-3.0e4),
                                 reads=[bif, gb[q]], writes=[gb[q]])
                        for q in range(8):
                            P.op("dve", lambda e: e.max(out=m8b[:, q, :], in_=impt8[:, q, :]), reads=[gb[q]], writes=[gb[q]])
                        bsl = Buf("selb")
                        P.op("dve", lambda e: TT(e, selb8[:], impf8[:], m8b[:, :, 7:8].to_broadcast([128, 8, 64]), ALU.is_ge), reads=[bif] + gb, writes=[bsl])
                        ptr, bptr = pmx.get()
                        ptb = ptr[:].bitcast(BF16)
                        for q in range(8):
                            P.op("pe", lambda e: e.transpose(out=ptb[0:64, q * 128:(q + 1) * 128], in_=selb8[:, q, :], identity=identb[:]), reads=[bsl], writes=[bptr])
                        P.op("act", lambda e: e.activation(out=selT[0:64].rearrange("j k t -> j (k t)"), in_=ptb[0:64, :], func=AF.Copy), reads=[bptr], writes=[b_selT])
                        iters = []
                        started = set()
                        state = {"mm": None}

                        def mk_att(kind, i, kh, g, dd):
                            hq = 2 * kh + g
                            plo, phi = g * 64, g * 64 + 64
                            d = i - 4 * tb
                            sc, bsc_ = psc.get(); e_, be = e_r.get(); em, bem = em_r.get()
                            bank, bbank = pacc[hq]
                            if kind == "sel":
                                kmat, vmat, gbase = ksT, vS, 4
                                ts_list = list(range(max(d, 0), 4))
                                if g == 0:
                                    mx, bmx = pmx.get(); mm, bmm = mm_r.get()
                                    state["mm"] = (mm, bmm)
                                else:
                                    mx = bmx = None
                                    mm, bmm = state["mm"]
                                last_i = 4 * tb + 3
                            else:
                                kmat, vmat, gbase = kwT, vW, 8
                                ts_list = list(range(max(dd, 0), min(dd + 4, 3) + 1))
                                mx = bmx = mm = bmm = None
                                last_i = 4 * tb + 3
                            key = (kind, hq)
                            flags = []
                            for ts in ts_list:
                                flags.append(key not in started)
                                started.add(key)

                            def s0():
                                if mx is not None:
                                    P.op("pe", lambda e: e.matmul(mx[:], lhsT=eexp[:, i * 128:(i + 1) * 128], rhs=selT[:, kh, :], start=True, stop=True),
                                         reads=[b_selT], writes=[bmx])
                                P.op("pe", lambda e: e.matmul(sc[:], lhsT=kmat[:, kh, i * 128:(i + 1) * 128], rhs=qz[:, hq, :],
                                                              start=True, stop=True), reads=[bqz], writes=[bsc_])

                            def s1():
                                if mx is not None:
                                    if d >= 0:
                                        P.op("dve", lambda e: TT(e, mm[:], mx[:], slm[:, d, :], ALU.mult), reads=[bmx], writes=[bmm])
                                    else:
                                        P.op("dve", lambda e: e.tensor_copy(out=mm[:], in_=mx[:]), reads=[bmx], writes=[bmm])
                                P.op("act", lambda e: e.activation(out=e_[:], in_=sc[:], func=AF.Exp, scale=SCALE), reads=[bsc_], writes=[be])

                            def s2():
                                if kind == "sel":
                                    P.op("dve" if g == 0 else "pool", lambda e: TT(e, em[:], e_[:], mm[:], ALU.mult), reads=[be, bmm], writes=[bem])
                                else:
                                    P.op("pool" if g == 0 else "dve", lambda e: TT(e, em[:], e_[:], band[:, dd + 4, :], ALU.mult), reads=[be], writes=[bem])

                            def s3():
                                for ts, fl in zip(ts_list, flags):
                                    P.op("pe", lambda e: e.matmul(bank[:, ts * 66:(ts + 1) * 66], lhsT=em[:, ts * 128:(ts + 1) * 128], rhs=vmat[:, i, kh, :],
                                                                  start=fl, stop=(i == 4 * tb + ts), skip_group_check=True),
                                         reads=[bem], writes=[bbank])
                                if i == last_i:
                                    finalize(bank[:, 0:264].rearrange("p (a w) -> p a w", a=4), bbank, 64, 0, 4, tb, hq, gbase + hq, False)
                            return [s0, s1, s2, s3]

                        for i in range(0, 4 * tb + 4):
                            for kh in range(2):
                                for g in range(2):
                                    iters.append(mk_att("sel", i, kh, g, None))
                        for dd in range(-4, 4):
                            i = 4 * tb + dd
                            if i < 0:
                                continue
                            for kh in range(2):
                                for g in range(2):
                                    iters.append(mk_att("win", i, kh, g, dd))
                        run_pipeline(iters, 4)
                        oTb, boT = oTb_r.get()
                        ob, bob = ob_r.get()
                        for hb in range(2):
                            pz, bpz = psc.get()
                            for t2_ in range(2):
                                T = 4 * tb + 2 * hb + t2_
                                for kc in range(NKC):
                                    P.op("pe", lambda e: e.matmul(pz[:, t2_ * 256:(t2_ + 1) * 256], lhsT=hT[:, kc, T * 128:(T + 1) * 128], rhs=wz[:, kc, :],
                                                                  start=(kc == 0 and t2_ == 0), stop=(kc == NKC - 1), skip_group_check=True), writes=[bpz])
                            sz, bsz = sz_r.get()
                            P.op("act", lambda e: e.activation(out=sz[:], in_=pz[:], func=AF.Silu), reads=[bpz], writes=[bsz])
                            P.op("pool" if hb == 0 else "dve", lambda e: TT(e, ob[:, 2 * hb:2 * hb + 2, :], oacc[:, 2 * hb:2 * hb + 2, :], sz[:].rearrange("p (a c) -> p a c", a=2), ALU.mult),
                                 reads=[b_oa, bsz], writes=[bob])
                        ptr, bptr = pmx.get()
                        ptb = ptr[:].bitcast(BF16)
                        for c in range(2):
                            for ts in range(4):
                                P.op("pe", lambda e: e.transpose(out=ptb[:, (c * 4 + ts) * 128:(c * 4 + ts + 1) * 128], in_=ob[:, ts, c * 128:(c + 1) * 128], identity=identb[:]),
                                     reads=[bob], writes=[bptr])
                        P.op("act", lambda e: e.activation(out=oTb[:].rearrange("p c n -> p (c n)"), in_=ptb[:, :], func=AF.Copy), reads=[bptr], writes=[boT])
                        for c in range(2):
                            P.dma(oT_d[0, c * 128:(c + 1) * 128, t0:t0 + 512], oTb[:, c, :], reads=[boT])
                    P.barrier()


            if "final" in phases:
                with ExitStack() as ph:
                    TT = lambda e, o, a, b, op: e.tensor_tensor(out=o, in0=a, in1=b, op=op)
                    SBW = min(S, 1024)
                    NSB = S // SBW
                    NBB = SBW // 512
                    b_k = Buf("finconst")
                    wo = sbt(ph, "fwo", [128, NKC, D], BF16)
                    wbr = sbt(ph, "fwbr", [128, 4, 2, D], BF16)
                    tmpc = ExitStack()
                    stg = sbt(tmpc, "fstg", [128, NKC, 512], F32)
                    for hf in range(2):
                        P.dma(stg[:], w_o[l, :, hf * 512:(hf + 1) * 512].rearrange("(k p) c -> p k c", p=128), reads=[b_k], writes=[b_k])
                        P.op("pool", lambda e: e.tensor_copy(out=wo[:, :, hf * 512:(hf + 1) * 512], in_=stg[:]), reads=[b_k], writes=[b_k])
                    for m in range(4):
                        for hf in range(2):
                            P.dma(stg[:, 0:2, :], w_br[l, m, :, hf * 512:(hf + 1) * 512].rearrange("(k p) c -> p k c", p=128), reads=[b_k], writes=[b_k])
                            P.op("pool", lambda e: e.tensor_copy(out=wbr[:, m, :, hf * 512:(hf + 1) * 512], in_=stg[:, 0:2, :]), reads=[b_k], writes=[b_k])
                    P.barrier()
                    tmpc.close()
                    oTs_r = Ring(nc, ph, "foTs", [128, 4, 2, SBW], BF16, 2)
                    mixT = sbt(ph, "fmixT", [128, NKC, SBW], BF16); b_mix = Buf("mixT")
                    wbf = Ring(nc, ph, "wbf", [128, NKC, 512], BF16, 2)
                    sg_r = Ring(nc, ph, "fsg", [128, 512], F32, 3); tmp_r = Ring(nc, ph, "ftmp", [128, 512], F32, 2)
                    acc_r = Ring(nc, ph, "facc", [128, 512], F32, 2)
                    xr = Ring(nc, ph, "fx", [128, D], F32, 2); yr = Ring(nc, ph, "fy", [128, D], F32, 2)
                    pg_ = PR([0, 1, 2]); pb_ = PR([3, 4, 5]); po_ = PR([6, 7])
                    units = [(sbk, cch) for sbk in range(NSB) for cch in range(NKC)]
                    wtiles = {}
                    otiles = {}

                    def load_unit(idx):
                        sbk, cch = units[idx]
                        wtiles[idx] = loadw_cast(wbf, w_in[l], [(C_MG + m * D + cch * 128, 128) for m in range(4)])

                    def load_oT(sbk):
                        oTs, b_oT = oTs_r.get()
                        s0 = sbk * SBW
                        for m in range(4):
                            for kk in range(2):
                                P.dma(oTs[:, m, kk, :], oT_d[m, kk * 128:(kk + 1) * 128, s0:s0 + SBW], writes=[b_oT])
                        otiles[sbk] = (oTs, b_oT)

                    load_oT(0)
                    load_unit(0)
                    for idx, (sbk, cch) in enumerate(units):
                        s0 = sbk * SBW
                        if idx + 1 < len(units):
                            load_unit(idx + 1)
                        if cch == 0 and sbk + 1 < NSB:
                            load_oT(sbk + 1)
                        wt, bw = wtiles.pop(idx)
                        oTs, b_oT = otiles[sbk]
                        for tbb in range(NBB):
                            tb = sbk * NBB + tbb
                            acc, bacc = acc_r.get()
                            for m in range(4):
                                pg, bpg = pg_.get(); mm_fm(pg, bpg, wt, bw, m * 128, 128, tb)
                                sg, bsg = sg_r.get()
                                P.op("act", lambda e: e.activation(out=sg[:], in_=pg[:], func=AF.Sigmoid), reads=[bpg], writes=[bsg])
                                pb, bpb = pb_.get()
                                for kk in range(2):
                                    P.op("pe", lambda e: e.matmul(pb[:], lhsT=wbr[:, m, kk, cch * 128:(cch + 1) * 128], rhs=oTs[:, m, kk, tbb * 512:(tbb + 1) * 512],
                                                                  start=(kk == 0), stop=(kk == 1)), reads=[b_oT], writes=[bpb])
                                if m == 0:
                                    P.op("dve", lambda e: TT(e, acc[:], pb[:], sg[:], ALU.mult), reads=[bpb, bsg], writes=[bacc])
                                else:
                                    tmp, btmp = tmp_r.get()
                                    P.op("dve", lambda e: TT(e, tmp[:], pb[:], sg[:], ALU.mult), reads=[bpb, bsg], writes=[btmp])
                                    if m < 3:
                                        P.op("pool", lambda e: TT(e, acc[:], acc[:], tmp[:], ALU.add), reads=[bacc, btmp], writes=[bacc])
                                    else:
                                        P.op("pool", lambda e: TT(e, mixT[:, cch, tbb * 512:(tbb + 1) * 512], acc[:], tmp[:], ALU.add),
                                             reads=[bacc, btmp], writes=[b_mix])
                        if cch != NKC - 1:
                            continue
                        for tt_ in range(SBW // 128):
                            tt = s0 // 128 + tt_
                            xt, bx = xr.get()
                            P.dma(xt[:], xsrc[tt * 128:(tt + 1) * 128, :], writes=[bx])
                            yt, by = yr.get()
                            for hf in range(2):
                                po, bpo = po_.get()
                                for c2 in range(NKC):
                                    P.op("pe", lambda e: e.matmul(po[:], lhsT=mixT[:, c2, tt_ * 128:(tt_ + 1) * 128], rhs=wo[:, c2, hf * 512:(hf + 1) * 512],
                                                                  start=(c2 == 0), stop=(c2 == NKC - 1)), reads=[b_mix], writes=[bpo])
                                P.op("dve", lambda e: TT(e, yt[:, hf * 512:(hf + 1) * 512], po[:], xt[:, hf * 512:(hf + 1) * 512], ALU.add),
                                     reads=[bpo, bx], writes=[by])
                            P.dma(xdst[tt * 128:(tt + 1) * 128, :], yt[:], reads=[by])
                    P.barrier()

            PHASES_PLACEHOLDER = None

            if debug == "oT":
                P.barrier()
                P.dma(dbg, oT_d)
                P.barrier()
                break

        P.barrier()
        print("instructions", P.nins, "waits", P.nwaits)
    return nc


def prep_inputs(inputs, S, n_layers):
    f = lambda a: np.ascontiguousarray(np.asarray(a), dtype=np.float32)
    L = n_layers
    com = {}
    com["norm_g"] = f(inputs["norm_g"])[:L]
    com["w_in"] = f(inputs["w_in"])[:L]
    com["convw"] = f(np.transpose(f(inputs["sc_conv_w"])[:L], (0, 2, 1)))
    sm = lambda a: f(np.transpose(f(a)[:L].reshape(L, 8, 128), (0, 2, 1)))
    com["s5_lr"] = sm(inputs["s5_a_re"]); com["s5_li"] = sm(inputs["s5_a_im"])
    com["s5_ldt"] = sm(np.repeat(f(inputs["s5_log_dt"])[:L, :, None], 64, axis=2))
    def expand_b(b):
        out = np.zeros((L, 128, 8, 128), np.float32)
        b = f(b)[:L]
        for j in range(8):
            for hq in range(2):
                g = 2 * j + hq
                col = 32 * (j % 4) + 16 * hq
                out[:, hq * 64:(hq + 1) * 64, j, col:col + 16] = b[:, g]
        return out
    def expand_c(c_):
        out = np.zeros((L, 128, 8, 128), np.float32)
        c_ = f(c_)[:L]
        for j in range(8):
            for hq in range(2):
                g = 2 * j + hq
                col = 32 * (j % 4) + 16 * hq
                out[:, hq * 64:(hq + 1) * 64, j, col:col + 16] = np.transpose(c_[:, g], (0, 2, 1))
        return out
    com["s5_bre"] = expand_b(inputs["s5_b_re"]); com["s5_bim"] = expand_b(inputs["s5_b_im"])
    com["s5_cre"] = expand_c(inputs["s5_c_re"]); com["s5_cim"] = expand_c(inputs["s5_c_im"])
    com["s5_d"] = f(np.transpose(f(inputs["s5_d"])[:L].reshape(L, 2, 128), (0, 2, 1)))
    com["glu_w"] = f(inputs["s5_glu_w"])[:L]
    com["glu_b"] = f(np.transpose(f(inputs["s5_glu_b"])[:L].reshape(L, 4, 128), (0, 2, 1)))
    com["w_br"] = f(inputs["w_branch"])[:L]; com["w_o"] = f(inputs["w_out"])[:L]
    g = f(inputs["nsa_qk_g"])[:L]
    com["qkg_row"] = f(np.concatenate([np.tile(g[:, 0], (1, 4)), np.tile(g[:, 2], (1, 2)), np.tile(g[:, 3], (1, 2))], axis=1))
    com["qkg1"] = f(np.tile(g[:, 1], (1, 2)))
    pe = f(inputs["nsa_cmp_pe"])[:L]
    com["pe_t"] = f(np.tile(np.transpose(pe, (0, 1, 3, 2)), (1, 1, 2, 1)))
    w1 = f(inputs["nsa_cmp_w1"])[:L].reshape(L, 2, 32, 64, 64)
    w1bd = np.zeros((L, 2, 128, 32, 128), np.float32)
    for kh in range(2):
        w1bd[:, :, kh * 64:(kh + 1) * 64, :, kh * 64:(kh + 1) * 64] = np.transpose(w1, (0, 1, 3, 2, 4))
    com["w1bd"] = w1bd.reshape(L, 2, 128, 32 * 128)
    w2 = f(inputs["nsa_cmp_w2"])[:L]
    w2bd = np.zeros((L, 2, 128, 128), np.float32)
    for kh in range(2):
        w2bd[:, :, kh * 64:(kh + 1) * 64, kh * 64:(kh + 1) * 64] = w2
    com["w2bd"] = w2bd
    com.update(host_consts(S))
    x = np.asarray(inputs["x"]); pos = np.asarray(inputs["positions"])
    maps = []
    for b in range(x.shape[0]):
        m = dict(com)
        m["x"] = np.ascontiguousarray(x[b, :S], dtype=np.float32)
        pb = np.asarray(pos[b, :S], dtype=np.int32)
        pl = np.zeros((128, S // 128 + 2), np.int32)
        pl[:, :S // 128] = pb.reshape(S // 128, 128).T
        ce = pb[31::16]
        n1 = min(128, len(ce))
        pl[:n1, S // 128] = ce[:n1]
        if len(ce) > 128:
            pl[:len(ce) - 128, S // 128 + 1] = ce[128:]
        m["pos"] = pl
        maps.append(m)
    return maps


_NC_CACHE = {}


def kernel(**inputs):
    S = 4096
    L = 2
    maps = prep_inputs(inputs, S, L)
    if "nc" not in _NC_CACHE:
        _NC_CACHE["nc"] = build(S, L)
    res = run_bass_kernel_spmd(_NC_CACHE["nc"], maps, core_ids=list(range(8)))
    return np.stack([np.asarray(r["y"], dtype=np.float32) for r in res.results], axis=0)
```

```python
import math
from contextlib import ExitStack
import numpy as np
import concourse.bass as bass
import concourse.mybir as mybir
from concourse.bass_utils import run_bass_kernel_spmd

F32 = mybir.dt.float32
BF16 = mybir.dt.bfloat16
I32 = mybir.dt.int32
AF = mybir.ActivationFunctionType
ALU = mybir.AluOpType
AX = mybir.AxisListType

D = 1024
NKC = 8
EPS = 1e-6
C_NQ, C_NKV, C_NG, C_NZ = 0, 256, 1024, 1036
C_SC, C_SCZ = 1292, 2060
C_SB, C_SBZ = 2316, 3084
C_S5U, C_S5Z = 3340, 3596
C_MG = 3852
WIN = 7948
NDS = 24
NDW = 6
SAME_ENGINE_WAIT = True
USE_CAST_DMA = True
CAST_Q = "pool"


class Buf:
    __slots__ = ("name", "lw", "rd")

    def __init__(self, name=""):
        self.name = name
        self.lw = []
        self.rd = {}


class Prog:
    def __init__(self, nc, es):
        self.nc = nc
        self.eng = {"pe": nc.tensor, "dve": nc.vector, "act": nc.scalar, "pool": nc.gpsimd, "sp": nc.sync}
        self.sem = {e: es.enter_context(nc.semaphore("s_" + e)) for e in ("pe", "dve", "act", "pool")}
        self.cnt = {e: 0 for e in self.sem}
        self.dsem = [es.enter_context(nc.semaphore("d%d" % i)) for i in range(NDS)]
        self.dsem += [es.enter_context(nc.semaphore("w%d" % i)) for i in range(NDW)]
        self.dcnt = [0] * (NDS + NDW)
        self.dnext = 0
        self.wnext = 0
        self.seen = {e: {} for e in self.eng}
        self.nwaits = 0
        self.nins = 0

    def _wait(self, e, tok):
        if tok is None:
            return
        k, v = tok
        if k == ("c", e) and (e == "pe" or not SAME_ENGINE_WAIT):
            return
        if self.seen[e].get(k, 0) >= v:
            return
        self.seen[e][k] = v
        s = self.sem[k[1]] if k[0] == "c" else self.dsem[k[1]]
        self.eng[e].wait_ge(s, v)
        self.nwaits += 1

    def _deps(self, e, reads, writes, isdma=False):
        for b in reads:
            for t in b.lw:
                self._wait(e, t)
        for b in writes:
            if isdma and not b.rd and b.lw and all(t[0][0] == "d" for t in b.lw):
                continue
            for t in b.lw:
                self._wait(e, t)
            for k, v in b.rd.items():
                self._wait(e, (k, v))

    def _done(self, tok, reads, writes, isdma=False):
        k, v = tok
        for b in reads:
            if b.rd.get(k, 0) < v:
                b.rd[k] = v
        for b in writes:
            if isdma and not b.rd and b.lw and all(t[0][0] == "d" for t in b.lw):
                b.lw.append(tok)
            else:
                b.lw = [tok]
            b.rd = {}

    def op(self, e, fn, reads=(), writes=()):
        self._deps(e, reads, writes)
        ins = fn(self.eng[e])
        self.cnt[e] += 1
        self.nins += 1
        ins.then_inc(self.sem[e], 1)
        tok = (("c", e), self.cnt[e])
        self._done(tok, reads, writes)
        return tok

    def dma(self, out, in_, reads=(), writes=(), q="sp"):
        if q == "pool":
            i = NDS + self.wnext
            self.wnext = (self.wnext + 1) % NDW
        else:
            i = self.dnext
            self.dnext = (i + 1) % NDS
        if self.dcnt[i] > 0:
            self._wait(q, (("d", i), self.dcnt[i]))
        self._deps(q, reads, writes, True)
        ins = self.eng[q].dma_start(out=out, in_=in_)
        self.dcnt[i] += 16
        self.nins += 1
        ins.then_inc(self.dsem[i], 16)
        tok = (("d", i), self.dcnt[i])
        self._done(tok, reads, writes, True)
        return tok

    def barrier(self):
        toks = [(("c", e), self.cnt[e]) for e in self.cnt if self.cnt[e] > 0]
        toks += [(("d", i), self.dcnt[i]) for i in range(NDS + NDW) if self.dcnt[i] > 0]
        for e in self.eng:
            for t in toks:
                self._wait(e, t)


_UID = [0]


def run_pipeline(iters, nst, young_first=True):
    n = len(iters)
    for step in range(n + nst - 1):
        for st in (range(nst) if young_first else range(nst - 1, -1, -1)):
            k = step - st
            if 0 <= k < n and iters[k][st] is not None:
                iters[k][st]()


class Ring:
    def __init__(self, nc, es, name, shape, dtype, n, psum=False):
        self.t = []
        self.b = []
        for i in range(n):
            _UID[0] += 1
            if psum:
                t = es.enter_context(nc.psum_tensor("rp_%s%d_%d" % (name, i, _UID[0]), shape, dtype))
            else:
                t = es.enter_context(nc.sbuf_tensor("rs_%s%d_%d" % (name, i, _UID[0]), shape, dtype))
            self.t.append(t)
            self.b.append(Buf("%s%d" % (name, i)))
        self.i = 0
        self.n = n

    def get(self):
        i = self.i
        self.i = (i + 1) % self.n
        return self.t[i], self.b[i]


def host_consts(S):
    import ml_dtypes
    bf = ml_dtypes.bfloat16
    NT = S // 128
    c = {}
    c["ident"] = np.eye(128, dtype=np.float32)
    half = 32
    invf = np.power(np.float32(10000.0), -np.arange(half, dtype=np.float32) / np.float32(half)).astype(np.float32)
    c["invf"] = np.tile(invf[None, :], (128, 1)).astype(np.float32)
    s_ = np.arange(128)[:, None]
    t_ = np.arange(512)[None, :]
    c["sbm"] = np.stack([(128 * d + s_ < t_) for d in range(4)], 1).astype(bf)
    c["slm"] = np.stack([(128 * d + s_ <= t_) for d in range(4)], 1).astype(bf)
    band = []
    for d in range(-4, 4):
        sa = 128 * d + s_
        band.append((sa <= t_) & (sa > t_ - 512))
    c["band"] = np.stack(band, 1).astype(bf)
    c["ustrict"] = (np.arange(128)[:, None] > np.arange(128)[None, :]).astype(bf)
    c["onesb"] = np.ones((128, 128), dtype=bf)
    ee = np.zeros((128, S), np.float32)
    ee[:64] = (np.arange(S)[None, :] // 64 == np.arange(64)[:, None])
    c["eexp"] = ee.astype(bf)
    t = np.arange(S)
    j = np.arange(64)[None, :]
    cur = (t // 64)[:, None]
    forced = (j == 0) | (j == cur) | (j == cur - 1)
    valid = (j * 64) <= t[:, None]
    vm = (valid & ~forced).astype(np.float32)
    fv = np.where(forced, 1.0e4, np.where(valid, 0.0, -1.0e4)).astype(np.float32)
    c["selvm"] = vm.reshape(NT, 128, 64).transpose(1, 0, 2).astype(bf)
    c["selfv"] = fv.reshape(NT, 128, 64).transpose(1, 0, 2).astype(bf)
    n = np.arange(256)[:, None]
    ovl = ((16 * n < 64 * j + 64) & (16 * n + 32 > 64 * j)).astype(np.float32)
    ext = np.concatenate([ovl, np.ones((256, 1), np.float32), np.zeros((256, 1), np.float32)], 1)
    ext[255, :] = 0
    c["ovl"] = ext.reshape(2, 128, 66).transpose(1, 0, 2).copy()
    return c


def build(S, n_layers, debug=None, phases=("conv", "s5", "sb", "nsa", "final")):
    NT = S // 128
    NB = S // 512
    NCMP = (S - 32) // 16 + 1
    SCALE = 0.125
    nc = bass.Bass("TRN2", target_bir_lowering=False)

    def din(name, shape, dtype=F32):
        return nc.dram_tensor(name, list(shape), dtype, kind="ExternalInput").ap()

    L = n_layers
    x_in = din("x", [S, D])
    pos_in = din("pos", [128, NT + 2], I32)
    norm_g = din("norm_g", [L, D])
    w_in = din("w_in", [L, D, WIN])
    convw = din("convw", [L, 256, 3])
    s5_lr = din("s5_lr", [L, 128, 8]); s5_li = din("s5_li", [L, 128, 8]); s5_ldt = din("s5_ldt", [L, 128, 8])
    s5_bre = din("s5_bre", [L, 128, 8, 128]); s5_bim = din("s5_bim", [L, 128, 8, 128])
    s5_cre = din("s5_cre", [L, 128, 8, 128]); s5_cim = din("s5_cim", [L, 128, 8, 128])
    s5_d = din("s5_d", [L, 128, 2]); glu_w = din("glu_w", [L, 256, 512]); glu_b = din("glu_b", [L, 128, 4])
    w_br = din("w_br", [L, 4, 256, D]); w_o = din("w_o", [L, D, D])
    qkg_row = din("qkg_row", [L, 512]); qkg1 = din("qkg1", [L, 128])
    pe_t = din("pe_t", [L, 2, 128, 32]); w1bd = din("w1bd", [L, 2, 128, 32 * 128]); w2bd = din("w2bd", [L, 2, 128, 128])
    c_ident = din("ident", [128, 128]); c_invf = din("invf", [128, 32])
    c_sbm = din("sbm", [128, 4, 512], BF16); c_slm = din("slm", [128, 4, 512], BF16); c_band = din("band", [128, 8, 512], BF16)
    c_ustrict = din("ustrict", [128, 128], BF16); c_onesb = din("onesb", [128, 128], BF16); c_eexp = din("eexp", [128, S], BF16)
    c_selvm = din("selvm", [128, NT, 64], BF16); c_selfv = din("selfv", [128, NT, 64], BF16); c_ovl = din("ovl", [128, 2, 66])
    y_out = nc.dram_tensor("y", [S, D], F32, kind="ExternalOutput").ap()
    oT_d = nc.dram_tensor("oT_d", [4, 256, S], BF16, kind="Internal").ap()
    xmid = nc.dram_tensor("xmid", [S, D], F32, kind="Internal").ap()
    dbg = None
    if debug == "oT":
        dbg = nc.dram_tensor("dbg", [4, 256, S], BF16, kind="ExternalOutput").ap()

    with ExitStack() as es:
        P = Prog(nc, es)

        def sbt(ctx, name, shape, dtype):
            _UID[0] += 1
            return ctx.enter_context(nc.sbuf_tensor("sb_%s_%d" % (name, _UID[0]), list(shape), dtype))

        identf = sbt(es, "identf", [128, 128], F32)
        identb = sbt(es, "identb", [128, 128], BF16)
        onesb = sbt(es, "onesb", [128, 128], BF16)
        b_c = Buf("const")
        P.dma(identf[:], c_ident, writes=[b_c])
        P.dma(onesb[:], c_onesb, writes=[b_c])
        P.op("dve", lambda e: e.tensor_copy(out=identb[:], in_=identf[:]), reads=[b_c], writes=[b_c])
        pst = [es.enter_context(nc.psum_tensor("ps%d" % i, [128, 512], F32)) for i in range(8)]
        psb = [Buf("ps%d" % i) for i in range(8)]

        class PR:
            def __init__(self, idx):
                self.idx = list(idx); self.i = 0
            def get(self):
                k = self.idx[self.i]; self.i = (self.i + 1) % len(self.idx)
                return pst[k], psb[k]
        ps = PR(range(8))

        hT = sbt(es, "hT", [128, NKC, S], BF16)
        cosT = sbt(es, "cosT", [128, NT, 32], F32); sinT = sbt(es, "sinT", [128, NT, 32], F32)
        cosC = sbt(es, "cosC", [128, 2, 32], F32); sinC = sbt(es, "sinC", [128, 2, 32], F32)

        def sincos(ctx, ang, shape, out_sin, out_cos, tag, rd=()):
            b = Buf(tag)
            ki = sbt(ctx, tag + "ki", shape, I32); kf = sbt(ctx, tag + "kf", shape, F32)
            r = sbt(ctx, tag + "r", shape, F32); c1 = sbt(ctx, tag + "c1", shape, F32)
            TWO_PI = 2.0 * math.pi
            for shift, dst in ((0.0, out_sin), (0.5 * math.pi, out_cos)):
                P.op("dve", lambda e: e.tensor_scalar(out=ki[:], in0=ang, scalar1=1.0 / TWO_PI, scalar2=shift / TWO_PI + 0.5,
                                                      op0=ALU.mult, op1=ALU.add), reads=list(rd), writes=[b])
                P.op("dve", lambda e: e.tensor_copy(out=kf[:], in_=ki[:]), reads=[b], writes=[b])
                P.op("dve", lambda e: e.scalar_tensor_tensor(out=r[:], in0=kf[:], scalar=-TWO_PI, in1=ang, op0=ALU.mult, op1=ALU.add),
                     reads=[b] + list(rd), writes=[b])
                if shift != 0.0:
                    P.op("dve", lambda e: e.tensor_scalar(out=r[:], in0=r[:], scalar1=shift, scalar2=None, op0=ALU.add), reads=[b], writes=[b])
                P.op("dve", lambda e: e.tensor_scalar(out=c1[:], in0=r[:], scalar1=math.pi, scalar2=-TWO_PI, op0=ALU.is_gt, op1=ALU.mult),
                     reads=[b], writes=[b])
                P.op("dve", lambda e: e.tensor_tensor(out=r[:], in0=r[:], in1=c1[:], op=ALU.add), reads=[b], writes=[b])
                P.op("dve", lambda e: e.tensor_scalar(out=c1[:], in0=r[:], scalar1=-math.pi, scalar2=TWO_PI, op0=ALU.is_lt, op1=ALU.mult),
                     reads=[b], writes=[b])
                P.op("dve", lambda e: e.tensor_tensor(out=r[:], in0=r[:], in1=c1[:], op=ALU.add), reads=[b], writes=[b])
                P.op("dve", lambda e: e.tensor_scalar(out=r[:], in0=r[:], scalar1=3.1415925, scalar2=-3.1415925, op0=ALU.min, op1=ALU.max),
                     reads=[b], writes=[b])
                P.op("act", lambda e: e.activation(out=dst, in_=r[:], func=AF.Sin), reads=[b], writes=[b])

        with ExitStack() as ph:
            invf = sbt(ph, "invf", [128, 32], F32)
            posi = sbt(ph, "posi", [128, NT + 2], I32); posf = sbt(ph, "posf", [128, NT + 2], F32)
            ang = sbt(ph, "ang", [128, NT + 2, 32], F32)
            b0 = Buf("rope")
            P.dma(invf[:], c_invf, writes=[b0])
            P.dma(posi[:], pos_in, writes=[b0])
            P.op("dve", lambda e: e.tensor_copy(out=posf[:], in_=posi[:]), reads=[b0], writes=[b0])
            P.op("dve", lambda e: e.tensor_tensor(out=ang[:], in0=posf[:].unsqueeze(2).to_broadcast([128, NT + 2, 32]),
                                                  in1=invf[:].unsqueeze(1).to_broadcast([128, NT + 2, 32]), op=ALU.mult),
                 reads=[b0], writes=[b0])
            sn = sbt(ph, "sn_all", [128, NT + 2, 32], F32); cs = sbt(ph, "cs_all", [128, NT + 2, 32], F32)
            sincos(ph, ang[:], [128, NT + 2, 32], sn[:], cs[:], "rp", rd=[b0])
            P.barrier()
            P.op("dve", lambda e: e.tensor_copy(out=cosT[:], in_=cs[:, 0:NT, :]))
            P.op("dve", lambda e: e.tensor_copy(out=sinT[:], in_=sn[:, 0:NT, :]))
            P.op("dve", lambda e: e.tensor_copy(out=cosC[:], in_=cs[:, NT:NT + 2, :]))
            P.op("dve", lambda e: e.tensor_copy(out=sinC[:], in_=sn[:, NT:NT + 2, :]))
            P.barrier()

        def loadw(wst, wbf, src, cols, eng="pool"):
            if USE_CAST_DMA:
                return loadw_cast(wbf, src, cols)
            ws, bws = wst.get(); wt, bw = wbf.get()
            off = 0
            for (c0, n) in cols:
                P.dma(ws[:, :, off:off + n], src[:, c0:c0 + n].rearrange("(k p) c -> p k c", p=128), writes=[bws])
                off += n
            P.op(eng, lambda e: e.tensor_copy(out=wt[:, :, 0:off], in_=ws[:, :, 0:off]), reads=[bws], writes=[bw])
            return wt, bw

        _simstg = {}
        if CAST_Q != "pool":
            _simstg["t"] = sbt(es, "simstg", [128, NKC * 512], F32)
            _simstg["b"] = Buf("simstg")

        def cast_dma(out, in_, bw):
            if CAST_Q == "pool":
                P.dma(out, in_, writes=[bw], q="pool")
                return
            n = 1
            for d_ in out.shape[1:]:
                n *= d_
            stg = _simstg["t"][:, 0:n]
            if len(out.shape) == 3:
                stg = stg.rearrange("p (a b) -> p a b", a=out.shape[1])
            P.dma(stg, in_, reads=[_simstg["b"]], writes=[_simstg["b"]])
            P.op("dve", lambda e: e.tensor_copy(out=out, in_=stg), reads=[_simstg["b"]], writes=[bw, _simstg["b"]])

        def loadw_cast(wbf, src, cols):
            wt, bw = wbf.get()
            off = 0
            for (c0, n) in cols:
                cast_dma(wt[:, :, off:off + n], src[:, c0:c0 + n].rearrange("(k p) c -> p k c", p=128), bw)
                off += n
            return wt, bw

        def mm_fm(pt, bp, wt, bw, c0, cw, tb):
            for kc in range(NKC):
                P.op("pe", lambda e: e.matmul(pt[0:cw, :], lhsT=wt[:, kc, c0:c0 + cw], rhs=hT[:, kc, tb * 512:(tb + 1) * 512],
                                              start=(kc == 0), stop=(kc == NKC - 1)), reads=[bw], writes=[bp])

        def mm_tm(pt, bp, wt, bw, c0, cw, tt, po=0):
            for kc in range(NKC):
                P.op("pe", lambda e: e.matmul(pt[:, po:po + cw], lhsT=hT[:, kc, tt * 128:(tt + 1) * 128], rhs=wt[:, kc, c0:c0 + cw],
                                              start=(kc == 0), stop=(kc == NKC - 1)), reads=[bw], writes=[bp])

        for l in range(n_layers):
            xsrc = x_in if l == 0 else xmid
            xdst = y_out if l == n_layers - 1 else xmid
            with ExitStack() as ph:
                gB = sbt(ph, "gB", [128, D], F32)
                b_g = Buf("gB")
                P.dma(gB[:], norm_g[l:l + 1, :].partition_broadcast(128), writes=[b_g])
                xr = Ring(nc, ph, "xt", [128, D], F32, 5)
                junk = Ring(nc, ph, "junk", [128, D], F32, 1)
                xnr = Ring(nc, ph, "xn", [128, D], BF16, 2)
                st = Ring(nc, ph, "st", [128, 4], F32, 4)
                b_h = Buf("hT")
                pp0 = PR([0, 1, 2, 3])
                iters = []

                def mk_p0(i):
                    xt, bx = xr.get(); s_, bs = st.get(); xn, bn = xnr.get(); pt, bp = pp0.get()

                    def s0():
                        P.dma(xt[:], xsrc[i * 128:(i + 1) * 128, :], writes=[bx])

                    def s1():
                        jt, bj = junk.get()
                        P.op("act", lambda e: e.activation(out=jt[:], in_=xt[:], func=AF.Square, accum_out=s_[:, 0:1]), reads=[bx], writes=[bj, bs])
                        P.op("act", lambda e: e.activation(out=s_[:, 1:2], in_=s_[:, 0:1], func=AF.Sqrt, bias=EPS, scale=1.0 / D), reads=[bs], writes=[bs])

                    def s2():
                        P.op("dve", lambda e: e.reciprocal(out=s_[:, 2:3], in_=s_[:, 1:2]), reads=[bs], writes=[bs])
                        P.op("dve", lambda e: e.scalar_tensor_tensor(out=xn[:], in0=xt[:], scalar=s_[:, 2:3], in1=gB[:], op0=ALU.mult, op1=ALU.mult),
                             reads=[bx, bs, b_g], writes=[bn])

                    def s3():
                        ptb = pt[:].bitcast(BF16)
                        for kc in range(NKC):
                            P.op("pe", lambda e: e.transpose(out=ptb[:, kc * 128:(kc + 1) * 128], in_=xn[:, kc * 128:(kc + 1) * 128], identity=identb[:]),
                                 reads=[bn, b_c], writes=[bp])

                    def s4():
                        ptb = pt[:].bitcast(BF16)
                        P.op("act" if i % 2 == 0 else "pool_or_dve", None) if False else None
                        if i % 3 == 2:
                            P.op("dve", lambda e: e.tensor_copy(out=hT[:, :, i * 128:(i + 1) * 128], in_=ptb.rearrange("p (k t) -> p k t", k=NKC)), reads=[bp], writes=[b_h])
                        else:
                            P.op("act", lambda e: e.activation(out=hT[:, :, i * 128:(i + 1) * 128], in_=ptb.rearrange("p (k t) -> p k t", k=NKC), func=AF.Copy),
                                 reads=[bp], writes=[b_h])
                    return [s0, s1, s2, s3, s4]

                for i in range(NT):
                    iters.append(mk_p0(i))
                run_pipeline(iters, 5)
                P.barrier()

            if "conv" in phases:
                with ExitStack() as ph:
                    wst = Ring(nc, ph, "wst", [128, NKC, 512], F32, 2); wbf = Ring(nc, ph, "wbf", [128, NKC, 512], BF16, 2)
                    cwt = sbt(ph, "cwt", [128, 2, 3], F32); b_cw = Buf()
                    P.dma(cwt[:], convw[l].rearrange("(c p) j -> p c j", p=128), writes=[b_cw])
                    u = sbt(ph, "u", [128, S + 2], F32); b_u = Buf()
                    xs_r = Ring(nc, ph, "xs", [128, 512], F32, 2); sz_r = Ring(nc, ph, "sz", [128, 512], F32, 2)
                    bz_r = Ring(nc, ph, "bz", [128, 512], F32, 2); y_r = Ring(nc, ph, "y", [128, 512], F32, 2)
                    o_r = Ring(nc, ph, "o", [128, 512], BF16, 2)
                    for cc in range(2):
                        wt, bw = loadw(wst, wbf, w_in[l], [(C_SC + cc * 128, 128), (C_SC + 256 + cc * 128, 128),
                                                           (C_SC + 512 + cc * 128, 128), (C_SCZ + cc * 128, 128)])
                        P.op("pool", lambda e: e.memset(u[:, 0:2], 0.0), writes=[b_u])
                        for tb in range(NB):
                            t0 = tb * 512
                            pB, bB = ps.get(); mm_fm(pB, bB, wt, bw, 0, 128, tb)
                            pC, bC = ps.get(); mm_fm(pC, bC, wt, bw, 128, 128, tb)
                            pX, bX = ps.get(); mm_fm(pX, bX, wt, bw, 256, 128, tb)
                            pZ, bZ = ps.get(); mm_fm(pZ, bZ, wt, bw, 384, 128, tb)
                            xs, bxs = xs_r.get()
                            P.op("act", lambda e: e.activation(out=xs[:], in_=pX[:], func=AF.Copy), reads=[bX], writes=[bxs])
                            P.op("dve", lambda e: e.tensor_tensor(out=u[:, 2 + t0:2 + t0 + 512], in0=pC[:], in1=xs[:], op=ALU.mult),
                                 reads=[bC, bxs], writes=[b_u])
                            sz, bsz = sz_r.get()
                            P.op("act", lambda e: e.activation(out=sz[:], in_=pZ[:], func=AF.Silu), reads=[bZ], writes=[bsz])
                            bz, bbz = bz_r.get()
                            P.op("dve", lambda e: e.tensor_tensor(out=bz[:], in0=pB[:], in1=sz[:], op=ALU.mult), reads=[bB, bsz], writes=[bbz])
                            y, by = y_r.get()
                            P.op("dve", lambda e: e.tensor_scalar(out=y[:], in0=u[:, 2 + t0:2 + t0 + 512], scalar1=cwt[:, cc, 2:3], scalar2=None,
                                                                  op0=ALU.mult), reads=[b_u, b_cw], writes=[by])
                            P.op("dve", lambda e: e.scalar_tensor_tensor(out=y[:], in0=u[:, 1 + t0:1 + t0 + 512], scalar=cwt[:, cc, 1:2], in1=y[:],
                                                                         op0=ALU.mult, op1=ALU.add), reads=[b_u, by], writes=[by])
                            P.op("dve", lambda e: e.scalar_tensor_tensor(out=y[:], in0=u[:, t0:t0 + 512], scalar=cwt[:, cc, 0:1], in1=y[:],
                                                                         op0=ALU.mult, op1=ALU.add), reads=[b_u, by], writes=[by])
                            o, bo = o_r.get()
                            P.op("pool", lambda e: e.tensor_tensor(out=o[:], in0=y[:], in1=bz[:], op=ALU.mult), reads=[by, bbz], writes=[bo])
                            P.dma(oT_d[1, cc * 128:(cc + 1) * 128, t0:t0 + 512], o[:], reads=[bo])
                    P.barrier()


            if "s5" in phases:
                with ExitStack() as ph:
                    TC = 512
                    sp_ = {}
                    bp_ = Buf("s5par")
                    for nm, src in (("lr", s5_lr), ("li", s5_li), ("ldt", s5_ldt)):
                        sp_[nm] = sbt(ph, "s5" + nm, [128, 8], F32)
                        P.dma(sp_[nm][:], src[l], writes=[bp_])
                    def t8(nm, w=8):
                        sp_[nm] = sbt(ph, "s5" + nm, [128, w], F32)
                        return sp_[nm]
                    V = lambda nm: sp_[nm][:]
                    def dv(fn, eng="dve"):
                        P.op(eng, fn, reads=[bp_], writes=[bp_])
                    t8("dt"); t8("lrd"); t8("mag"); t8("ang"); t8("sn"); t8("cs"); t8("abre"); t8("abim"); t8("den"); t8("rden")
                    t8("abm1"); t8("t1"); t8("t2"); t8("cre"); t8("cim")
                    dv(lambda e: e.activation(out=V("dt"), in_=V("ldt"), func=AF.Exp), "act")
                    dv(lambda e: e.tensor_tensor(out=V("lrd"), in0=V("lr"), in1=V("dt"), op=ALU.mult))
                    dv(lambda e: e.activation(out=V("mag"), in_=V("lrd"), func=AF.Exp), "act")
                    dv(lambda e: e.tensor_tensor(out=V("ang"), in0=V("li"), in1=V("dt"), op=ALU.mult))
                    P.barrier()
                    sincos(ph, V("ang"), [128, 8], V("sn"), V("cs"), "s5sc")
                    P.barrier()
                    dv(lambda e: e.tensor_tensor(out=V("abre"), in0=V("mag"), in1=V("cs"), op=ALU.mult))
                    dv(lambda e: e.tensor_tensor(out=V("abim"), in0=V("mag"), in1=V("sn"), op=ALU.mult))
                    dv(lambda e: e.tensor_tensor(out=V("t1"), in0=V("lr"), in1=V("lr"), op=ALU.mult))
                    dv(lambda e: e.tensor_tensor(out=V("t2"), in0=V("li"), in1=V("li"), op=ALU.mult))
                    dv(lambda e: e.tensor_tensor(out=V("den"), in0=V("t1"), in1=V("t2"), op=ALU.add))
                    dv(lambda e: e.reciprocal(out=V("rden"), in_=V("den")))
                    dv(lambda e: e.tensor_scalar(out=V("abm1"), in0=V("abre"), scalar1=-1.0, scalar2=None, op0=ALU.add))
                    dv(lambda e: e.tensor_tensor(out=V("t1"), in0=V("abm1"), in1=V("lr"), op=ALU.mult))
                    dv(lambda e: e.tensor_tensor(out=V("t2"), in0=V("abim"), in1=V("li"), op=ALU.mult))
                    dv(lambda e: e.tensor_tensor(out=V("t1"), in0=V("t1"), in1=V("t2"), op=ALU.add))
                    dv(lambda e: e.tensor_tensor(out=V("cre"), in0=V("t1"), in1=V("rden"), op=ALU.mult))
                    dv(lambda e: e.tensor_tensor(out=V("t1"), in0=V("abim"), in1=V("lr"), op=ALU.mult))
                    dv(lambda e: e.tensor_tensor(out=V("t2"), in0=V("abm1"), in1=V("li"), op=ALU.mult))
                    dv(lambda e: e.tensor_tensor(out=V("t1"), in0=V("t1"), in1=V("t2"), op=ALU.subtract))
                    dv(lambda e: e.tensor_tensor(out=V("cim"), in0=V("t1"), in1=V("rden"), op=ALU.mult))
                    pc = sbt(ph, "s5pc", [128, 8, 10], F32); pn = sbt(ph, "s5pn", [128, 8, 10], F32)
                    dv(lambda e: e.tensor_copy(out=pc[:, :, 0], in_=V("cs")))
                    dv(lambda e: e.tensor_copy(out=pn[:, :, 0], in_=V("sn")))
                    for m in range(1, 10):
                        dv(lambda e: e.tensor_tensor(out=V("t1"), in0=pc[:, :, m - 1], in1=pc[:, :, m - 1], op=ALU.mult))
                        dv(lambda e: e.tensor_tensor(out=V("t2"), in0=pn[:, :, m - 1], in1=pn[:, :, m - 1], op=ALU.mult))
                        dv(lambda e: e.tensor_tensor(out=pc[:, :, m], in0=V("t1"), in1=V("t2"), op=ALU.subtract))
                        dv(lambda e: e.tensor_tensor(out=V("t1"), in0=pc[:, :, m - 1], in1=pn[:, :, m - 1], op=ALU.mult))
                        dv(lambda e: e.tensor_scalar(out=pn[:, :, m], in0=V("t1"), scalar1=2.0, scalar2=None, op0=ALU.mult))
                    Ec = sbt(ph, "s5Ec", [128, 8, TC], F32); Es = sbt(ph, "s5Es", [128, 8, TC], F32)
                    bbT = sbt(ph, "s5bbT", [128, 2, 8, 128], BF16)
                    CTr = sbt(ph, "s5CTr", [128, 8, 128], BF16); CTi = sbt(ph, "s5CTi", [128, 8, 128], BF16)
                    dsk = sbt(ph, "s5dsk", [128, 2], F32); glb = sbt(ph, "s5glb", [128, 4], F32)
                    glw = sbt(ph, "s5glw", [128, 2, 512], BF16)
                    uT = sbt(ph, "s5uT", [128, 2, S], BF16); b_uT = Buf("uT")
                    wbf = Ring(nc, ph, "wbf", [128, NKC, 512], BF16, 1)
                    tmpc = ExitStack()
                    tA = sbt(tmpc, "s5tA", [128, 8, TC // 2], F32); tB = sbt(tmpc, "s5tB", [128, 8, TC // 2], F32)
                    dv(lambda e: e.memset(Ec[:, :, 0:1], 1.0)); dv(lambda e: e.memset(Es[:, :, 0:1], 0.0))
                    k = 1
                    m = 0
                    while k < TC:
                        ck = pc[:, :, m:m + 1].to_broadcast([128, 8, k]); sk = pn[:, :, m:m + 1].to_broadcast([128, 8, k])
                        dv(lambda e: e.tensor_tensor(out=tA[:, :, 0:k], in0=Ec[:, :, 0:k], in1=ck, op=ALU.mult))
                        dv(lambda e: e.tensor_tensor(out=tB[:, :, 0:k], in0=Es[:, :, 0:k], in1=sk, op=ALU.mult))
                        dv(lambda e: e.tensor_tensor(out=Ec[:, :, k:2 * k], in0=tA[:, :, 0:k], in1=tB[:, :, 0:k], op=ALU.subtract))
                        dv(lambda e: e.tensor_tensor(out=tA[:, :, 0:k], in0=Ec[:, :, 0:k], in1=sk, op=ALU.mult))
                        dv(lambda e: e.tensor_tensor(out=tB[:, :, 0:k], in0=Es[:, :, 0:k], in1=ck, op=ALU.mult))
                        dv(lambda e: e.tensor_tensor(out=Es[:, :, k:2 * k], in0=tA[:, :, 0:k], in1=tB[:, :, 0:k], op=ALU.add))
                        k *= 2; m += 1
                    P.barrier()
                    tmpc.close()
                    tmpc = ExitStack()
                    bex_r = sbt(tmpc, "s5bexr", [128, 8, 128], F32); bex_i = sbt(tmpc, "s5bexi", [128, 8, 128], F32)
                    P.dma(bex_r[:], s5_bre[l], writes=[bp_]); P.dma(bex_i[:], s5_bim[l], writes=[bp_])
                    bbs = Ring(nc, tmpc, "s5bbs", [128, 128], F32, 2); bbt = Ring(nc, tmpc, "s5bbt", [128, 128], F32, 2)
                    for j in range(8):
                        for ri in range(2):
                            tt_, btt = bbt.get(); bb_, bbb = bbs.get()
                            if ri == 0:
                                P.op("dve", lambda e: e.tensor_scalar(out=tt_[:], in0=bex_i[:, j, :], scalar1=sp_["cim"][:, j:j + 1], scalar2=None, op0=ALU.mult),
                                     reads=[bp_], writes=[btt])
                                P.op("dve", lambda e: e.scalar_tensor_tensor(out=bb_[:], in0=bex_r[:, j, :], scalar=sp_["cre"][:, j:j + 1], in1=tt_[:],
                                                                             op0=ALU.mult, op1=ALU.subtract), reads=[bp_, btt], writes=[bbb])
                            else:
                                P.op("dve", lambda e: e.tensor_scalar(out=tt_[:], in0=bex_r[:, j, :], scalar1=sp_["cim"][:, j:j + 1], scalar2=None, op0=ALU.mult),
                                     reads=[bp_], writes=[btt])
                                P.op("dve", lambda e: e.scalar_tensor_tensor(out=bb_[:], in0=bex_i[:, j, :], scalar=sp_["cre"][:, j:j + 1], in1=tt_[:],
                                                                             op0=ALU.mult, op1=ALU.add), reads=[bp_, btt], writes=[bbb])
                            pt, bpt = ps.get()
                            P.op("pe", lambda e: e.transpose(out=pt[:, 0:128], in_=bb_[:], identity=identf[:]), reads=[bbb, b_c], writes=[bpt])
                            P.op("act", lambda e: e.activation(out=bbT[:, ri, j, :], in_=pt[:, 0:128], func=AF.Copy), reads=[bpt], writes=[bp_])
                    cst = sbt(tmpc, "s5cst", [128, 8, 128], F32)
                    P.dma(cst[:], s5_cre[l], writes=[bp_])
                    dv(lambda e: e.tensor_copy(out=CTr[:], in_=cst[:]))
                    P.dma(cst[:], s5_cim[l], reads=[bp_], writes=[bp_])
                    dv(lambda e: e.tensor_scalar(out=CTi[:], in0=cst[:], scalar1=-1.0, scalar2=None, op0=ALU.mult))
                    P.dma(dsk[:], s5_d[l], writes=[bp_]); P.dma(glb[:], glu_b[l], writes=[bp_])
                    gst = sbt(tmpc, "s5gst", [128, 2, 512], F32)
                    P.dma(gst[:], glu_w[l].rearrange("(k p) c -> p k c", p=128), writes=[bp_])
                    dv(lambda e: e.tensor_copy(out=glw[:], in_=gst[:]))
                    wst = Ring(nc, tmpc, "wst", [128, NKC, 512], F32, 1)
                    wt, bw = loadw(wst, wbf, w_in[l], [(C_S5U, 256), (C_S5Z, 256)])
                    P.barrier()
                    tmpc.close()
                    for tb in range(NB):
                        for cc in range(2):
                            pt, bpt = ps.get(); mm_fm(pt, bpt, wt, bw, cc * 128, 128, tb)
                            P.op("act", lambda e: e.activation(out=uT[:, cc, tb * 512:(tb + 1) * 512], in_=pt[:], func=AF.Copy), reads=[bpt], writes=[b_uT])
                    ini = sbt(ph, "s5ini", [128, 2, 8], F32); b_ini = Buf("ini")
                    P.op("dve", lambda e: e.memset(ini[:], 0.0), writes=[b_ini])
                    itmp = sbt(ph, "s5itmp", [128, 8, 4], F32)
                    b_inis = [b_ini] * 8
                    b_it = [Buf("it%d" % q) for q in range(8)]
                    nsT9 = sbt(ph, "s5nsT9", [128, 8], F32)
                    P.op("dve", lambda e: e.tensor_scalar(out=nsT9[:], in0=pn[:, :, 9], scalar1=-1.0, scalar2=None, op0=ALU.mult), writes=[b_ini])
                    b_inis = [Buf("ini%d" % q) for q in range(8)]
                    P.barrier()
                    ab_r = Ring(nc, ph, "s5ab", [128, 4, 512], F32, 1)
                    mre = Ring(nc, ph, "s5mre", [128, 2, 512], F32, 2)
                    xre = Ring(nc, ph, "s5xre", [128, 2, 512], F32, 2)
                    dd_r = Ring(nc, ph, "s5dd", [128, 4, 512], F32, 1)
                    xs_r = Ring(nc, ph, "s5xs", [128, 2, 8, 512], BF16, 1)
                    yT_r = Ring(nc, ph, "s5yT", [128, 2, 512], BF16, 1)
                    sg_r = Ring(nc, ph, "s5sg", [128, 512], F32, 1); rs_r = Ring(nc, ph, "s5rs", [128, 512], F32, 1)
                    sz_r = Ring(nc, ph, "s5sz", [128, 512], F32, 1); o_r = Ring(nc, ph, "s5o", [128, 512], BF16, 2)
                    TT = lambda e, o, a, b, op: e.tensor_tensor(out=o, in0=a, in1=b, op=op)
                    ppr = PR([0, 1]); ppi = PR([2, 3]); pep = PR([4, 5, 6, 7])
                    iters = []

                    def mk_s5(tb, j, xs, bxs):
                        t0 = tb * 512
                        pr, bpr = ppr.get(); pi_, bpi = ppi.get()
                        ab, bab = ab_r.get(); mm_, bmm = mre.get(); xx, bxx = xre.get(); dd4, bdd = dd_r.get()
                        cT = pc[:, j, 9:10]; sT = pn[:, j, 9:10]

                        def s0():
                            P.op("pe", lambda e: e.matmul(pr[:], lhsT=bbT[:, 0, j, :], rhs=uT[:, j // 4, t0:t0 + 512], start=True, stop=True),
                                 reads=[b_uT], writes=[bpr])
                            P.op("pe", lambda e: e.matmul(pi_[:], lhsT=bbT[:, 1, j, :], rhs=uT[:, j // 4, t0:t0 + 512], start=True, stop=True),
                                 reads=[b_uT], writes=[bpi])

                        def s1():
                            P.op("dve", lambda e: TT(e, ab[:, 0, :], pr[:], Ec[:, j, :], ALU.mult), reads=[bpr], writes=[bab])
                            P.op("dve", lambda e: TT(e, ab[:, 1, :], pi_[:], Es[:, j, :], ALU.mult), reads=[bpi], writes=[bab])
                            P.op("dve", lambda e: TT(e, ab[:, 2, :], pi_[:], Ec[:, j, :], ALU.mult), reads=[bpi], writes=[bab])
                            P.op("dve", lambda e: TT(e, ab[:, 3, :], pr[:], Es[:, j, :], ALU.mult), reads=[bpr], writes=[bab])
                            P.op("pool", lambda e: TT(e, mm_[:, 0, :], ab[:, 0, :], ab[:, 1, :], ALU.add), reads=[bab], writes=[bmm])
                            P.op("pool", lambda e: TT(e, mm_[:, 1, :], ab[:, 2, :], ab[:, 3, :], ALU.subtract), reads=[bab], writes=[bmm])

                        def s2():
                            P.op("dve", lambda e: e.tensor_tensor_scan(out=xx[:, 0, :], data0=sp_["mag"][:, j:j + 1].to_broadcast([128, 512]), data1=mm_[:, 0, :],
                                                                       initial=ini[:, 0, j:j + 1], op0=ALU.mult, op1=ALU.add), reads=[bmm, b_inis[j]], writes=[bxx])
                            P.op("dve", lambda e: e.tensor_tensor_scan(out=xx[:, 1, :], data0=sp_["mag"][:, j:j + 1].to_broadcast([128, 512]), data1=mm_[:, 1, :],
                                                                       initial=ini[:, 1, j:j + 1], op0=ALU.mult, op1=ALU.add), reads=[bmm, b_inis[j]], writes=[bxx])
                            it_ = itmp[:, j, :]
                            P.op("act", lambda e: e.activation(out=it_[:, 0:1], in_=xx[:, 1, 511:512], func=AF.Copy, scale=nsT9[:, j:j + 1]), reads=[bxx], writes=[b_it[j]])
                            P.op("act", lambda e: e.activation(out=it_[:, 1:2], in_=xx[:, 1, 511:512], func=AF.Copy, scale=cT), reads=[bxx], writes=[b_it[j]])
                            P.op("act", lambda e: e.activation(out=ini[:, 0, j:j + 1], in_=xx[:, 0, 511:512], func=AF.Identity, scale=cT, bias=it_[:, 0:1]),
                                 reads=[bxx, b_it[j]], writes=[b_inis[j]])
                            P.op("act", lambda e: e.activation(out=ini[:, 1, j:j + 1], in_=xx[:, 0, 511:512], func=AF.Identity, scale=sT, bias=it_[:, 1:2]),
                                 reads=[bxx, b_it[j]], writes=[b_inis[j]])

                        def s3():
                            P.op("dve", lambda e: TT(e, dd4[:, 0, :], xx[:, 0, :], Ec[:, j, :], ALU.mult), reads=[bxx], writes=[bdd])
                            P.op("dve", lambda e: TT(e, dd4[:, 1, :], xx[:, 1, :], Es[:, j, :], ALU.mult), reads=[bxx], writes=[bdd])
                            P.op("dve", lambda e: TT(e, dd4[:, 2, :], xx[:, 0, :], Es[:, j, :], ALU.mult), reads=[bxx], writes=[bdd])
                            P.op("pool", lambda e: TT(e, dd4[:, 3, :], xx[:, 1, :], Ec[:, j, :], ALU.mult), reads=[bxx], writes=[bdd])
                            P.op("pool", lambda e: TT(e, xs[:, 0, j, :], dd4[:, 0, :], dd4[:, 1, :], ALU.subtract), reads=[bdd], writes=[bxs])
                            P.op("pool", lambda e: TT(e, xs[:, 1, j, :], dd4[:, 2, :], dd4[:, 3, :], ALU.add), reads=[bdd], writes=[bxs])
                            if j != 7:
                                return
                            yT, byT = yT_r.get()
                            for cc in range(2):
                                py, bpy = pep.get()
                                for jj in range(4):
                                    j2 = cc * 4 + jj
                                    P.op("pe", lambda e: e.matmul(py[:], lhsT=CTr[:, j2, :], rhs=xs[:, 0, j2, :], start=(jj == 0), stop=False), reads=[bxs], writes=[bpy])
                                    P.op("pe", lambda e: e.matmul(py[:], lhsT=CTi[:, j2, :], rhs=xs[:, 1, j2, :], start=False, stop=(jj == 3)), reads=[bxs], writes=[bpy])
                                P.op("dve", lambda e: e.scalar_tensor_tensor(out=yT[:, cc, :], in0=uT[:, cc, t0:t0 + 512], scalar=dsk[:, cc:cc + 1], in1=py[:],
                                                                             op0=ALU.mult, op1=ALU.add), reads=[bpy, b_uT], writes=[byT])
                            for c in range(2):
                                pa, bpa = pep.get(); pg, bpg = pep.get()
                                for kk in range(2):
                                    P.op("pe", lambda e: e.matmul(pa[:], lhsT=glw[:, kk, c * 128:(c + 1) * 128], rhs=yT[:, kk, :], start=(kk == 0), stop=(kk == 1)),
                                         reads=[byT], writes=[bpa])
                                for kk in range(2):
                                    P.op("pe", lambda e: e.matmul(pg[:], lhsT=glw[:, kk, (c + 2) * 128:(c + 3) * 128], rhs=yT[:, kk, :], start=(kk == 0), stop=(kk == 1)),
                                         reads=[byT], writes=[bpg])
                                sg, bsg = sg_r.get()
                                P.op("act", lambda e: e.activation(out=sg[:], in_=pg[:], func=AF.Sigmoid, bias=glb[:, c + 2:c + 3]), reads=[bpg], writes=[bsg])
                                rs, brs = rs_r.get()
                                P.op("dve", lambda e: e.scalar_tensor_tensor(out=rs[:], in0=pa[:], scalar=glb[:, c:c + 1], in1=sg[:], op0=ALU.add, op1=ALU.mult),
                                     reads=[bpa, bsg], writes=[brs])
                                pz, bpz = pep.get(); mm_fm(pz, bpz, wt, bw, 256 + c * 128, 128, tb)
                                sz, bsz = sz_r.get()
                                P.op("act", lambda e: e.activation(out=sz[:], in_=pz[:], func=AF.Silu), reads=[bpz], writes=[bsz])
                                o, bo = o_r.get()
                                P.op("pool", lambda e: TT(e, o[:], rs[:], sz[:], ALU.mult), reads=[brs, bsz], writes=[bo])
                                P.dma(oT_d[3, c * 128:(c + 1) * 128, t0:t0 + 512], o[:], reads=[bo])
                        return [s0, s1, s2, s3]

                    for tb in range(NB):
                        xs, bxs = xs_r.get()
                        for j in range(8):
                            iters.append(mk_s5(tb, j, xs, bxs))
                    run_pipeline(iters, 4)
                    P.barrier()

            if "sb" in phases:
                with ExitStack() as ph:
                    TT = lambda e, o, a, b, op: e.tensor_tensor(out=o, in0=a, in1=b, op=op)
                    sbm = sbt(ph, "sbm", [128, 4, 512], BF16); ustr = sbt(ph, "ustr", [128, 128], BF16)
                    b_k = Buf("sbconst")
                    P.dma(sbm[:], c_sbm, writes=[b_k]); P.dma(ustr[:], c_ustrict, writes=[b_k])
                    F32R = mybir.dt.float32r
                    ustrf = sbt(ph, "ustrf", [128, 128], F32); onesf = sbt(ph, "onesf", [128, 128], F32)
                    P.op("dve", lambda e: e.tensor_copy(out=ustrf[:].bitcast(F32R), in_=ustr[:]), reads=[b_k], writes=[b_k])
                    P.op("dve", lambda e: e.tensor_copy(out=onesf[:].bitcast(F32R), in_=onesb[:]), reads=[b_k, b_c], writes=[b_k])
                    qT = sbt(ph, "sbqT", [128, 2, S], BF16); kT = sbt(ph, "sbkT", [128, 2, S], BF16)
                    vz = sbt(ph, "sbvz", [128, NT, 4, 128], BF16)
                    P.op("pool", lambda e: e.memset(vz[:], 0.0), writes=[b_k])
                    szT = sbt(ph, "sbszT", [128, 2, S], BF16)
                    tmpc = ExitStack()
                    wbf = Ring(nc, tmpc, "wbf", [128, NKC, 512], BF16, 1)
                    wst = Ring(nc, tmpc, "wst", [128, NKC, 512], F32, 1)
                    wt, bw = loadw(wst, wbf, w_in[l], [(C_SB, 512)])
                    for tb in range(NB):
                        for c4 in range(4):
                            pt, bpt = ps.get(); mm_fm(pt, bpt, wt, bw, c4 * 128, 128, tb)
                            bsl = slice(tb * 512, (tb + 1) * 512)
                            if c4 < 2:
                                P.op("act", lambda e: e.activation(out=qT[:, c4, bsl], in_=pt[:], func=AF.Copy), reads=[bpt], writes=[b_k])
                            else:
                                P.op("dve", lambda e: e.tensor_copy(out=kT[:, c4 - 2, bsl], in_=pt[:]), reads=[bpt], writes=[b_k])
                    wt, bw = loadw(wst, wbf, w_in[l], [(C_SB + 512, 256), (C_SBZ, 256)])
                    for tt in range(NT):
                        pt, bpt = ps.get(); mm_tm(pt, bpt, wt, bw, 0, 256, tt)
                        vz4 = vz[:, tt].rearrange("p (a b) c -> p a b c", b=2)
                        pt4 = pt[:, 0:256].rearrange("p (a b d) -> p a b d", b=2, d=64)
                        P.op("act", lambda e: e.activation(out=vz4[:, :, 0, 0:64], in_=pt4[:, :, 0, :], func=AF.Copy), reads=[bpt], writes=[b_k])
                        P.op("dve", lambda e: e.tensor_copy(out=vz4[:, :, 1, 64:128], in_=pt4[:, :, 1, :]), reads=[bpt], writes=[b_k])
                    for tb in range(NB):
                        for hc in range(2):
                            pt, bpt = ps.get(); mm_fm(pt, bpt, wt, bw, 256 + hc * 128, 128, tb)
                            P.op("act", lambda e: e.activation(out=szT[:, hc, tb * 512:(tb + 1) * 512], in_=pt[:], func=AF.Silu), reads=[bpt], writes=[b_k])
                    P.barrier()
                    tmpc.close()
                    pz_ = PR([0, 1, 2]); pc_ = PR([3, 4]); pk_ = PR([5, 6]); pa_ = PR([7])
                    e1_r = Ring(nc, ph, "sbe1", [128, 512], F32, 2); sp_r = Ring(nc, ph, "sbsp", [128, 512], F32, 3)
                    spb_r = Ring(nc, ph, "sbspb", [128, 512], F32, 3); t1_r = Ring(nc, ph, "sbt1", [128, 512], F32, 3)
                    t2_r = Ring(nc, ph, "sbt2", [128, 512], F32, 3); w_r = Ring(nc, ph, "sbw", [128, 512], BF16, 3)
                    spp_r = Ring(nc, ph, "sbspp", [128, 512], F32, 2)
                    MBIG = 200.0
                    mbias = sbt(ph, "sbmbias", [128, 2], F32)
                    P.op("pool", lambda e: e.memset(mbias[:], -MBIG), writes=[b_k])
                    R_r = Ring(nc, ph, "sbR", [128, 512], F32, 3)
                    rstate = {}
                    sz_r = Ring(nc, ph, "sbsz", [128, 512], F32, 1); o_r = Ring(nc, ph, "sbo", [128, 512], BF16, 2)
                    iters = []

                    qz_r = Ring(nc, ph, "sbqz", [128, 2, 512], BF16, 2)
                    for qi in range(2):
                        P.op("pool", lambda e: e.memset(qz_r.t[qi][:], 0.0), writes=[qz_r.b[qi]])

                    def mk_iter(tb, hc, hh, i, pacc, bacc, qz, bqz):
                        t0 = tb * 512
                        h = hc * 2 + hh
                        plo, phi = hh * 64, hh * 64 + 64
                        imax = 4 * tb + 3
                        d = i - 4 * tb
                        first = (i == imax)
                        pz, bpz = pz_.get(); e1, be1 = e1_r.get(); sp, bsp = sp_r.get(); spb, bspb = spb_r.get()
                        pcm, bpcm = pc_.get(); pcl, bpcl = pk_.get(); t1, bt1 = t1_r.get(); t2, bt2 = t2_r.get()
                        w, bw_ = w_r.get()
                        wm, bwm = w, bw_
                        spp = bspp = None
                        if d >= 0:
                            spp, bspp = spp_r.get()
                        Rin = None if first else rstate["R"]
                        Rout = None
                        if i > 0:
                            Rout = R_r.get()
                            rstate["R"] = Rout

                        def stA0():
                            P.op("pe", lambda e: e.matmul(pz[:], lhsT=kT[:, hc, i * 128:(i + 1) * 128], rhs=qz[:, hh, :],
                                                          start=True, stop=True), reads=[bqz], writes=[bpz])

                        def stA():
                            P.op("act", lambda e: e.activation(out=e1[:], in_=pz[:], func=AF.Exp, scale=SCALE), reads=[bpz], writes=[be1])
                            P.op("act", lambda e: e.activation(out=sp[:].bitcast(F32R), in_=e1[:], func=AF.Ln, bias=1.0), reads=[be1], writes=[bsp])
                            if d >= 0:
                                P.op("pool", lambda e: TT(e, spb[:].bitcast(F32R), sp[:], sbm[:, d, :], ALU.mult), reads=[bsp], writes=[bspb])
                                P.op("dve", lambda e: e.scalar_tensor_tensor(out=spp[:], in0=sbm[:, d, :], scalar=-MBIG, in1=sp[:], op0=ALU.mult, op1=ALU.add),
                                     reads=[bsp], writes=[bspp])

                        def stB0():
                            sp1, bsp1 = (spp, bspp) if d >= 0 else (sp, bsp)
                            P.op("dve", lambda e: e.scalar_tensor_tensor(out=t1[:], in0=pz[:], scalar=SCALE, in1=sp1[:], op0=ALU.mult, op1=ALU.subtract),
                                 reads=[bpz, bsp1], writes=[bt1])
                            rsp, brsp = (spb, bspb) if d >= 0 else (sp, bsp)
                            P.op("pe", lambda e: e.matmul(pcm[:], lhsT=ustrf[:].bitcast(F32R), rhs=rsp[:].bitcast(F32R), start=True, stop=True), reads=[brsp], writes=[bpcm])
                            if i > 0:
                                P.op("pe", lambda e: e.matmul(pcl[:], lhsT=onesf[:].bitcast(F32R), rhs=rsp[:].bitcast(F32R), start=True, stop=True), reads=[brsp], writes=[bpcl])

                        def stB():
                            P.op("dve", lambda e: TT(e, t2[:], t1[:], pcm[:], ALU.subtract), reads=[bt1, bpcm], writes=[bt2])
                            if Rout is not None:
                                if Rin is None:
                                    P.op("dve", lambda e: e.tensor_copy(out=Rout[0][:], in_=pcl[:]), reads=[bpcl], writes=[Rout[1]])
                                else:
                                    P.op("dve", lambda e: TT(e, Rout[0][:], Rin[0][:], pcl[:], ALU.add), reads=[bpcl, Rin[1]], writes=[Rout[1]])

                        def stB2():
                            if Rin is not None:
                                P.op("pool", lambda e: TT(e, t2[:], t2[:], Rin[0][:], ALU.subtract), reads=[bt2, Rin[1]], writes=[bt2])

                        def stC0():
                            if d >= 0:
                                P.op("act", lambda e: e.activation(out=w[:], in_=t2[:], func=AF.Exp, bias=mbias[:, 0:1]), reads=[bt2], writes=[bw_])
                            else:
                                P.op("act", lambda e: e.activation(out=w[:], in_=t2[:], func=AF.Exp), reads=[bt2], writes=[bw_])

                        def stC():
                            P.op("pe", lambda e: e.matmul(pacc[:], lhsT=vz[:, i, h, :], rhs=wm[:],
                                                          start=(first and hh == 0), stop=(i == 0 and hh == 1), skip_group_check=True), reads=[bwm], writes=[bacc])
                            if i == 0 and hh == 1:
                                o, bo = o_r.get()
                                P.op("dve", lambda e: TT(e, o[:], pacc[:], szT[:, hc, t0:t0 + 512], ALU.mult), reads=[bacc], writes=[bo])
                                P.dma(oT_d[2, hc * 128:(hc + 1) * 128, t0:t0 + 512], o[:], reads=[bo])
                        return [stA0, stA, stB0, stB, stB2, stC0, stC]

                    def mk_qz(tb, hc, qz, bqz):
                        def s():
                            t0 = tb * 512
                            P.op("pool", lambda e: e.tensor_copy(out=qz[0:64, 0, :], in_=qT[0:64, hc, t0:t0 + 512]), writes=[bqz])
                            P.op("pool", lambda e: e.tensor_copy(out=qz[64:128, 1, :], in_=qT[64:128, hc, t0:t0 + 512]), writes=[bqz])
                        return [s, None, None, None, None, None, None]

                    for tb in range(NB):
                        for hc in range(2):
                            pacc, bacc = pa_.get()
                            qz, bqz = qz_r.get()
                            iters.append(mk_qz(tb, hc, qz, bqz))
                            for hh in range(2):
                                for i in range(4 * tb + 3, -1, -1):
                                    iters.append(mk_iter(tb, hc, hh, i, pacc, bacc, qz, bqz))
                    run_pipeline(iters, 7)
                    P.barrier()

            if "nsa" in phases:
                with ExitStack() as ph:
                    TT = lambda e, o, a, b, op: e.tensor_tensor(out=o, in0=a, in1=b, op=op)
                    NCH = (NCMP + 127) // 128
                    b_k = Buf("nsaconst")
                    kcmpT = sbt(ph, "kcmpT", [128, 2, 256], BF16)
                    vcX = sbt(ph, "vcX", [128, 2, 2, 130], BF16)
                    tmpc = ExitStack()
                    g1 = sbt(tmpc, "g1", [128, 128], F32)
                    P.dma(g1[:], qkg1[l:l + 1, :].partition_broadcast(128), writes=[b_k])
                    wbf = Ring(nc, tmpc, "wbf", [128, NKC, 512], BF16, 1)
                    wst = Ring(nc, tmpc, "wst", [128, NKC, 512], F32, 1)
                    ovs = sbt(tmpc, "ovs", [128, 2, 66], F32)
                    P.dma(ovs[:], c_ovl, writes=[b_k])
                    for kh in range(2):
                        P.op("dve", lambda e: e.tensor_copy(out=vcX[:, :, kh, 64:130], in_=ovs[:]), reads=[b_k], writes=[b_k])
                    pet = sbt(tmpc, "pet", [128, 2, 32], F32)
                    P.dma(pet[:], pe_t[l].rearrange("j p l -> p j l"), writes=[b_k])
                    w1b = sbt(tmpc, "w1b", [128, 2, 32, 128], BF16)
                    for j in range(2):
                        for hf in range(2):
                            cast_dma(w1b[:, j, hf * 16:(hf + 1) * 16, :].rearrange("p a b -> p (a b)"), w1bd[l, j][:, hf * 2048:(hf + 1) * 2048], b_k)
                    w2s = sbt(tmpc, "w2s", [128, 2, 128], F32); w2b = sbt(tmpc, "w2b", [128, 2, 128], BF16)
                    P.dma(w2s[:], w2bd[l].rearrange("j p c -> p j c"), writes=[b_k])
                    P.op("dve", lambda e: e.tensor_copy(out=w2b[:], in_=w2s[:]), reads=[b_k], writes=[b_k])
                    kAB = sbt(tmpc, "kAB", [128, 2, 2, S], BF16)
                    hidT = sbt(tmpc, "hidT", [128, 2, 256], BF16)
                    P.op("pool", lambda e: e.memset(hidT[:], 0.0), writes=[b_k])
                    wt, bw = loadw(wst, wbf, w_in[l], [(C_NKV, 256)])
                    for tb in range(NB):
                        for j in range(2):
                            pt, bpt = ps.get(); mm_fm(pt, bpt, wt, bw, j * 128, 128, tb)
                            for ab in range(2):
                                P.op("dve", lambda e: TT(e, kAB[:, j, ab, tb * 512:(tb + 1) * 512].rearrange("p (a b) -> p a b", b=16),
                                                         pt[:].rearrange("p (a b) -> p a b", b=16),
                                                         pet[:, j, ab * 16:(ab + 1) * 16].unsqueeze(1).to_broadcast([128, 32, 16]), ALU.add),
                                     reads=[bpt, b_k], writes=[b_k])
                    sq_r = Ring(nc, tmpc, "csq", [128, 128], F32, 2); st_r = Ring(nc, tmpc, "cst", [128, 8], F32, 2)
                    xn_r = Ring(nc, tmpc, "cxn", [128, 128], F32, 2); ra_r = Ring(nc, tmpc, "cra", [128, 2, 32], F32, 4)
                    rk_r = Ring(nc, tmpc, "crk", [128, 2, 2, 32], BF16, 2); dup_r = Ring(nc, tmpc, "cdup", [128, 2, 2, 64], BF16, 2)
                    for j in range(2):
                        phd, bphd = ps.get()
                        for ll in range(32):
                            ab = ll // 16
                            rhs = kAB[:, j, ab, :].rearrange("p (a b) -> p a b", b=16)[:, ab:ab + NCMP, ll % 16]
                            P.op("pe", lambda e: e.matmul(phd[:, 0:NCMP], lhsT=w1b[:, j, ll, :], rhs=rhs, start=(ll == 0), stop=(ll == 31)),
                                 reads=[b_k], writes=[bphd])
                        P.op("act", lambda e: e.activation(out=hidT[:, j, 0:NCMP], in_=phd[:, 0:NCMP], func=AF.Silu), reads=[bphd, b_k], writes=[b_k])
                        for nch in range(NCH):
                            po, bpo = ps.get()
                            P.op("pe", lambda e: e.matmul(po[:, 0:128], lhsT=hidT[:, j, nch * 128:(nch + 1) * 128], rhs=w2b[:, j, :], start=True, stop=True),
                                 reads=[b_k], writes=[bpo])
                            if j == 1:
                                P.op("act", lambda e: e.activation(out=vcX[:, nch, :, 0:64], in_=po[:, 0:128].rearrange("p (k d) -> p k d", k=2), func=AF.Copy),
                                     reads=[bpo, b_k], writes=[b_k])
                                continue
                            sq, bsq = sq_r.get(); st, bst = st_r.get()
                            P.op("act", lambda e: e.activation(out=sq[:], in_=po[:, 0:128], func=AF.Square), reads=[bpo], writes=[bsq])
                            P.op("dve", lambda e: e.reduce_sum(out=st[:, 0:2], in_=sq[:].rearrange("p (k d) -> p k d", k=2), axis=AX.X), reads=[bsq], writes=[bst])
                            P.op("act", lambda e: e.activation(out=st[:, 2:4], in_=st[:, 0:2], func=AF.Sqrt, bias=EPS, scale=1.0 / 64), reads=[bst], writes=[bst])
                            P.op("dve", lambda e: e.reciprocal(out=st[:, 4:6], in_=st[:, 2:4]), reads=[bst], writes=[bst])
                            xn, bxn = xn_r.get()
                            P.op("dve", lambda e: TT(e, xn[:].rearrange("p (k d) -> p k d", k=2), po[:, 0:128].rearrange("p (k d) -> p k d", k=2),
                                                     st[:, 4:6].unsqueeze(2).to_broadcast([128, 2, 64]), ALU.mult), reads=[bpo, bst], writes=[bxn])
                            P.op("dve", lambda e: TT(e, xn[:], xn[:], g1[:], ALU.mult), reads=[bxn, b_k], writes=[bxn])
                            x4 = xn[:].rearrange("p (k h d) -> p k h d", k=2, h=2)
                            cb = cosC[:, nch, :].unsqueeze(1).to_broadcast([128, 2, 32]); sb_ = sinC[:, nch, :].unsqueeze(1).to_broadcast([128, 2, 32])
                            rk, brk = rk_r.get()
                            a_, ba = ra_r.get(); b_, bb = ra_r.get()
                            P.op("dve", lambda e: TT(e, a_[:], x4[:, :, 0, :], cb, ALU.mult), reads=[bxn], writes=[ba])
                            P.op("dve", lambda e: TT(e, b_[:], x4[:, :, 1, :], sb_, ALU.mult), reads=[bxn], writes=[bb])
                            P.op("dve", lambda e: TT(e, rk[:, :, 0, :], a_[:], b_[:], ALU.subtract), reads=[ba, bb], writes=[brk])
                            a_, ba = ra_r.get(); b_, bb = ra_r.get()
                            P.op("dve", lambda e: TT(e, a_[:], x4[:, :, 1, :], cb, ALU.mult), reads=[bxn], writes=[ba])
                            P.op("dve", lambda e: TT(e, b_[:], x4[:, :, 0, :], sb_, ALU.mult), reads=[bxn], writes=[bb])
                            P.op("dve", lambda e: TT(e, rk[:, :, 1, :], a_[:], b_[:], ALU.add), reads=[ba, bb], writes=[brk])
                            dup, bdup = dup_r.get()
                            P.op("dve", lambda e: e.tensor_copy(out=dup[:], in_=rk[:].rearrange("p k h d -> p k (h d)").unsqueeze(2).to_broadcast([128, 2, 2, 64])),
                                 reads=[brk], writes=[bdup])
                            ptr, bptr = ps.get()
                            ptb = ptr[:].bitcast(BF16)
                            for kh in range(2):
                                P.op("pe", lambda e: e.transpose(out=ptb[:, kh * 128:(kh + 1) * 128], in_=dup[:, kh].rearrange("p a d -> p (a d)"), identity=identb[:]),
                                     reads=[bdup], writes=[bptr])
                            P.op("act", lambda e: e.activation(out=kcmpT[:, :, nch * 128:(nch + 1) * 128], in_=ptb[:, 0:256].rearrange("p (k n) -> p k n", k=2), func=AF.Copy),
                                 reads=[bptr], writes=[b_k])
                    if NCH == 1:
                        P.op("pool", lambda e: e.memset(kcmpT[:, :, 128:256], 0.0), reads=[b_k], writes=[b_k])
                        P.op("pool", lambda e: e.memset(vcX[:, 1, :, 0:64], 0.0), reads=[b_k], writes=[b_k])
                    P.barrier()
                    tmpc.close()
                    qT = sbt(ph, "nqT", [128, 2, S], BF16); ksT = sbt(ph, "nksT", [128, 2, S], BF16); kwT = sbt(ph, "nkwT", [128, 2, S], BF16)
                    vS = sbt(ph, "nvS", [128, NT, 2, 66], BF16); vW = sbt(ph, "nvW", [128, NT, 2, 66], BF16)
                    gS = sbt(ph, "ngS", [128, NT, 12], F32)
                    wz = sbt(ph, "nwz", [128, NKC, 256], BF16)
                    P.op("pool", lambda e: e.memset(vS[:, :, :, 64:66], 1.0), writes=[b_k])
                    P.op("pool", lambda e: e.memset(vW[:, :, :, 64:66], 1.0), writes=[b_k])
                    tmpc = ExitStack()
                    gq = sbt(tmpc, "gq", [128, 512], F32)
                    P.dma(gq[:], qkg_row[l:l + 1, :].partition_broadcast(128), writes=[b_k])
                    wbf = Ring(nc, tmpc, "wbf", [128, NKC, 512], BF16, 1)
                    wst = Ring(nc, tmpc, "wst", [128, NKC, 512], F32, 1)
                    wbf2 = Ring(nc, tmpc, "wbf2", [128, NKC, 512], BF16, 1)
                    wt, bw = loadw(wst, wbf, w_in[l], [(C_NZ, 256)])
                    P.op("pool", lambda e: e.tensor_copy(out=wz[:], in_=wt[:, :, 0:256]), reads=[bw], writes=[b_k])
                    wtA, bwA = loadw(wst, wbf, w_in[l], [(C_NQ, 256), (C_NKV + 256, 128), (C_NKV + 512, 128)])
                    wtB, bwB = loadw(wst, wbf2, w_in[l], [(C_NKV + 384, 128), (C_NKV + 640, 128), (C_NG, 12)])
                    sq_r = Ring(nc, tmpc, "nsq", [128, 512], F32, 2); st_r = Ring(nc, tmpc, "nst", [128, 24], F32, 2)
                    xn_r = Ring(nc, tmpc, "nxn", [128, 512], F32, 2); ra_r = Ring(nc, tmpc, "nra", [128, 8, 32], F32, 8)
                    rq_r = Ring(nc, tmpc, "nrq", [128, 8, 2, 32], BF16, 4); dup_r = Ring(nc, tmpc, "ndup", [128, 4, 2, 64], BF16, 3)
                    pp1 = PR([0, 1, 2, 3]); pp2 = PR([4, 5]); ppt = PR([6, 7])
                    st_r = Ring(nc, tmpc, "nst3", [128, 24], F32, 3)
                    iters = []

                    def mk_pre(tt):
                        p1, bp1 = pp1.get(); p2, bp2 = pp2.get(); ptr, bptr = ppt.get()
                        sq, bsq = sq_r.get(); st, bst = st_r.get(); xn, bxn = xn_r.get()
                        rq, brq = rq_r.get(); dup, bdup = dup_r.get()
                        tsl = slice(tt * 128, (tt + 1) * 128)

                        def s0():
                            mm_tm(p1, bp1, wtA, bwA, 0, 512, tt)
                            mm_tm(p2, bp2, wtB, bwB, 0, 268, tt)

                        def s1():
                            P.op("act", lambda e: e.activation(out=sq[:], in_=p1[:], func=AF.Square), reads=[bp1], writes=[bsq])
                            P.op("act", lambda e: e.activation(out=vS[:, tt, :, 0:64], in_=p2[:, 0:128].rearrange("p (k d) -> p k d", k=2), func=AF.Copy), reads=[bp2], writes=[b_k])
                            P.op("dve", lambda e: e.tensor_copy(out=vW[:, tt, :, 0:64], in_=p2[:, 128:256].rearrange("p (k d) -> p k d", k=2)), reads=[bp2], writes=[b_k])
                            P.op("act", lambda e: e.activation(out=gS[:, tt, :], in_=p2[:, 256:268], func=AF.Copy), reads=[bp2], writes=[b_k])

                        def s2():
                            P.op("dve", lambda e: e.reduce_sum(out=st[:, 0:8], in_=sq[:].rearrange("p (k d) -> p k d", k=8), axis=AX.X), reads=[bsq], writes=[bst])
                            P.op("act", lambda e: e.activation(out=st[:, 8:16], in_=st[:, 0:8], func=AF.Sqrt, bias=EPS, scale=1.0 / 64), reads=[bst], writes=[bst])
                            P.op("dve", lambda e: e.reciprocal(out=st[:, 16:24], in_=st[:, 8:16]), reads=[bst], writes=[bst])

                        def s3():
                            P.op("dve", lambda e: TT(e, xn[:].rearrange("p (k d) -> p k d", k=8), p1[:].rearrange("p (k d) -> p k d", k=8),
                                                     st[:, 16:24].unsqueeze(2).to_broadcast([128, 8, 64]), ALU.mult), reads=[bp1, bst], writes=[bxn])
                            P.op("pool", lambda e: TT(e, xn[:], xn[:], gq[:], ALU.mult), reads=[bxn, b_k], writes=[bxn])

                        ra4 = [ra_r.get() for _ in range(4)]

                        def s4():
                            x4 = xn[:].rearrange("p (k h d) -> p k h d", k=8, h=2)
                            cb = cosT[:, tt, :].unsqueeze(1).to_broadcast([128, 8, 32]); sb_ = sinT[:, tt, :].unsqueeze(1).to_broadcast([128, 8, 32])
                            P.op("dve", lambda e: TT(e, ra4[0][0][:], x4[:, :, 0, :], cb, ALU.mult), reads=[bxn], writes=[ra4[0][1]])
                            P.op("pool", lambda e: TT(e, ra4[1][0][:], x4[:, :, 1, :], sb_, ALU.mult), reads=[bxn], writes=[ra4[1][1]])
                            P.op("dve", lambda e: TT(e, ra4[2][0][:], x4[:, :, 1, :], cb, ALU.mult), reads=[bxn], writes=[ra4[2][1]])
                            P.op("pool", lambda e: TT(e, ra4[3][0][:], x4[:, :, 0, :], sb_, ALU.mult), reads=[bxn], writes=[ra4[3][1]])

                        def s4b():
                            P.op("dve", lambda e: TT(e, rq[:, :, 0, :], ra4[0][0][:], ra4[1][0][:], ALU.subtract), reads=[ra4[0][1], ra4[1][1]], writes=[brq])
                            P.op("dve", lambda e: TT(e, rq[:, :, 1, :], ra4[2][0][:], ra4[3][0][:], ALU.add), reads=[ra4[2][1], ra4[3][1]], writes=[brq])

                        def s4c():
                            rqf = rq[:].rearrange("p k h d -> p k (h d)")
                            P.op("pool", lambda e: e.tensor_copy(out=dup[:], in_=rqf[:, 4:8, :].unsqueeze(2).to_broadcast([128, 4, 2, 64])), reads=[brq], writes=[bdup])

                        def s5():
                            ptb = ptr[:].bitcast(BF16)
                            rq2 = rq[:].rearrange("p k h d -> p (k h d)")
                            for c in range(2):
                                P.op("pe", lambda e: e.transpose(out=ptb[:, c * 128:(c + 1) * 128], in_=rq2[:, c * 128:(c + 1) * 128], identity=identb[:]),
                                     reads=[brq], writes=[bptr])
                            for c in range(4):
                                P.op("pe", lambda e: e.transpose(out=ptb[:, (2 + c) * 128:(3 + c) * 128], in_=dup[:, c].rearrange("p a d -> p (a d)"), identity=identb[:]),
                                     reads=[bdup], writes=[bptr])

                        def s6():
                            ptb = ptr[:].bitcast(BF16)
                            P.op("act", lambda e: e.activation(out=qT[:, :, tsl], in_=ptb[:, 0:256].rearrange("p (k n) -> p k n", k=2), func=AF.Copy), reads=[bptr], writes=[b_k])
                            P.op("act", lambda e: e.activation(out=ksT[:, :, tsl], in_=ptb[:, 256:512].rearrange("p (k n) -> p k n", k=2), func=AF.Copy), reads=[bptr], writes=[b_k])
                            P.op("dve", lambda e: e.tensor_copy(out=kwT[:, :, tsl], in_=ptb[:, 512:768].rearrange("p (k n) -> p k n", k=2)), reads=[bptr], writes=[b_k])
                        return [s0, s1, s2, s3, s4, s4b, s4c, s5, s6]

                    for tt in range(NT):
                        iters.append(mk_pre(tt))
                    run_pipeline(iters, 9)
                    P.op("act", lambda e: e.activation(out=gS[:], in_=gS[:], func=AF.Sigmoid), reads=[b_k], writes=[b_k])
                    P.barrier()
                    tmpc.close()
                    slm = sbt(ph, "nslm", [128, 4, 512], BF16); band = sbt(ph, "nband", [128, 8, 512], BF16)
                    eexp = sbt(ph, "neexp", [128, S], BF16); sfv = sbt(ph, "nsfv", [128, NT, 64], BF16)
                    for dst, src in ((slm, c_slm), (band, c_band), (eexp, c_eexp), (sfv, c_selfv)):
                        P.dma(dst[:], src, writes=[b_k])
                    P.barrier()
                    pacc = [(pst[k], psb[k]) for k in range(4)]
                    psc = PR([4, 5]); pmx = PR([6, 7])
                    e_r = Ring(nc, ph, "ne", [128, 512], BF16, 3); em_r = Ring(nc, ph, "nem", [128, 512], BF16, 3)
                    mm_r = Ring(nc, ph, "nmm", [128, 512], BF16, 3)
                    oacc_r = Ring(nc, ph, "noacc", [128, 4, 256], F32, 2)
                    cur = {}
                    imp = sbt(ph, "nimp", [128, 4, 2, 64], F32); b_imp = Buf("imp")
                    impf8 = sbt(ph, "nimpf8", [128, 8, 64], F32); impt8 = sbt(ph, "nimpt8", [128, 8, 64], F32)
                    m8a = sbt(ph, "nm8a", [128, 8, 8], F32); m8b = sbt(ph, "nm8b", [128, 8, 8], F32); selb8 = sbt(ph, "nselb8", [128, 8, 64], BF16)
                    selT = sbt(ph, "nselT", [128, 2, 512], BF16); b_selT = Buf("selT")
                    P.op("pool", lambda e: e.memset(selT[:], 0.0), writes=[b_selT])
                    qz_r = Ring(nc, ph, "nqz", [128, 4, 512], BF16, 1)
                    for qi in range(1):
                        P.op("pool", lambda e: e.memset(qz_r.t[qi][:], 0.0), writes=[qz_r.b[qi]])
                    sc_r = Ring(nc, ph, "nsc8", [128, 12], F32, 4)
                    sz_r = Ring(nc, ph, "nsz", [128, 512], F32, 1); ob_r = Ring(nc, ph, "nob", [128, 4, 256], BF16, 1)
                    oTb_r = Ring(nc, ph, "noTb", [128, 2, 512], BF16, 1)

                    ftmp_r = Ring(nc, ph, "nftmp", [128, 4, 64], F32, 1)

                    def finalize(view, bank_b, dencol, ts0, nts, tb, hq, gidx, first, imp_g=None):
                        sc8, bsc = sc_r.get()
                        T0 = 4 * tb + ts0
                        P.op("dve", lambda e: e.tensor_scalar(out=sc8[:, 0:nts].unsqueeze(2), in0=view[:, :, dencol:dencol + 1], scalar1=1e-30, scalar2=None, op0=ALU.max),
                             reads=[bank_b], writes=[bsc])
                        P.op("dve", lambda e: e.reciprocal(out=sc8[:, 4:4 + nts], in_=sc8[:, 0:nts]), reads=[bsc], writes=[bsc])
                        P.op("dve", lambda e: TT(e, sc8[:, 8:8 + nts], sc8[:, 4:4 + nts], gS[:, T0:T0 + nts, gidx], ALU.mult), reads=[bsc], writes=[bsc])
                        oacc, b_oa = cur["oacc"]
                        osl = oacc[:, ts0:ts0 + nts, hq * 64:(hq + 1) * 64]
                        sgb = sc8[:, 8:8 + nts].unsqueeze(2).to_broadcast([128, nts, 64])
                        if first:
                            P.op("dve", lambda e: TT(e, osl, view[:, :, 0:64], sgb, ALU.mult), reads=[bank_b, bsc], writes=[b_oa])
                        else:
                            ft, bft = ftmp_r.get()
                            P.op("dve", lambda e: TT(e, ft[:, 0:nts, :], view[:, :, 0:64], sgb, ALU.mult), reads=[bank_b, bsc], writes=[bft])
                            P.op("pool", lambda e: TT(e, osl, osl, ft[:, 0:nts, :], ALU.add), reads=[bft, b_oa], writes=[b_oa])
                        if imp_g is not None:
                            kh_, g_ = imp_g
                            isl = imp[:, ts0:ts0 + nts, kh_, :]
                            rb = sc8[:, 4:4 + nts].unsqueeze(2).to_broadcast([128, nts, 64])
                            if g_ == 0:
                                P.op("dve", lambda e: TT(e, isl, view[:, :, 64:128], rb, ALU.mult), reads=[bank_b, bsc], writes=[b_imp])
                            else:
                                ft, bft = ftmp_r.get()
                                P.op("dve", lambda e: TT(e, ft[:, 0:nts, :], view[:, :, 64:128], rb, ALU.mult), reads=[bank_b, bsc], writes=[bft])
                                P.op("dve", lambda e: TT(e, isl, isl, ft[:, 0:nts, :], ALU.add), reads=[bft, b_imp], writes=[b_imp])

                    def emit_E(tb, oacc, b_oa):
                        t0 = tb * 512
                        oTb, boT = oTb_r.get()
                        ob, bob = ob_r.get()
                        for hb in range(2):
                            pz, bpz = psc.get()
                            for t2_ in range(2):
                                T = 4 * tb + 2 * hb + t2_
                                for kc in range(NKC):
                                    P.op("pe", lambda e: e.matmul(pz[:, t2_ * 256:(t2_ + 1) * 256], lhsT=hT[:, kc, T * 128:(T + 1) * 128], rhs=wz[:, kc, :],
                                                                  start=(kc == 0 and t2_ == 0), stop=(kc == NKC - 1), skip_group_check=True), writes=[bpz])
                            sz, bsz = sz_r.get()
                            P.op("act", lambda e: e.activation(out=sz[:], in_=pz[:], func=AF.Silu), reads=[bpz], writes=[bsz])
                            P.op("pool" if hb == 0 else "dve", lambda e: TT(e, ob[:, 2 * hb:2 * hb + 2, :], oacc[:, 2 * hb:2 * hb + 2, :], sz[:].rearrange("p (a c) -> p a c", a=2), ALU.mult),
                                 reads=[b_oa, bsz], writes=[bob])
                        ptr, bptr = pmx.get()
                        ptb = ptr[:].bitcast(BF16)
                        for c in range(2):
                            for ts in range(4):
                                P.op("pe", lambda e: e.transpose(out=ptb[:, (c * 4 + ts) * 128:(c * 4 + ts + 1) * 128], in_=ob[:, ts, c * 128:(c + 1) * 128], identity=identb[:]),
                                     reads=[bob], writes=[bptr])
                        P.op("act", lambda e: e.activation(out=oTb[:].rearrange("p c n -> p (c n)"), in_=ptb[:, :], func=AF.Copy), reads=[bptr], writes=[boT])
                        for c in range(2):
                            P.dma(oT_d[0, c * 128:(c + 1) * 128, t0:t0 + 512], oTb[:, c, :], reads=[boT])

                    pending_E = None
                    for tb in range(NB if "nsa_pre" not in phases else 0):
                        t0 = tb * 512
                        cur["oacc"] = oacc_r.get()
                        qz, bqz = qz_r.get()
                        for hq in range(4):
                            kh, g = hq // 2, hq % 2
                            P.op("pool" if hq % 2 else "act",
                                 (lambda e: e.tensor_copy(out=qz[g * 64:(g + 1) * 64, hq, :], in_=qT[g * 64:(g + 1) * 64, kh, t0:t0 + 512])) if hq % 2 else
                                 (lambda e: e.activation(out=qz[g * 64:(g + 1) * 64, hq, :], in_=qT[g * 64:(g + 1) * 64, kh, t0:t0 + 512], func=AF.Copy)),
                                 writes=[bqz])
                        nchs = [n_ for n_ in range(NCH) if 16 * (n_ * 128) + 31 <= t0 + 511]
                        iters = []

                        def mk_cmp(hq, nch):
                            kh, g = hq // 2, hq % 2
                            plo, phi = g * 64, g * 64 + 64
                            pair = (pacc[0], pacc[1]) if hq % 2 == 0 else (pacc[2], pacc[3])
                            sc, bsc_ = psc.get(); e_, be = e_r.get(); em, bem = em_r.get()

                            def s0():
                                P.op("pe", lambda e: e.matmul(sc[:], lhsT=kcmpT[:, kh, nch * 128:(nch + 1) * 128], rhs=qz[:, hq, :],
                                                              start=True, stop=True), reads=[bqz], writes=[bsc_])

                            def s1():
                                P.op("act", lambda e: e.activation(out=e_[:], in_=sc[:], func=AF.Exp, scale=SCALE), reads=[bsc_], writes=[be])

                            def s2():
                                P.op("pool", lambda e: e.affine_select(out=em[:], in_=e_[:], pattern=[[1, 512]], compare_op=ALU.is_ge, fill=0.0,
                                                                       base=t0 - 16 * 128 * nch - 31, channel_multiplier=-16), reads=[be], writes=[bem])

                            def s3():
                                for ts in range(4):
                                    bank, bbank = pair[ts // 2]
                                    region = bank[:, (ts % 2) * 130:(ts % 2) * 130 + 130]
                                    P.op("pe", lambda e: e.matmul(region, lhsT=em[:, ts * 128:(ts + 1) * 128], rhs=vcX[:, nch, kh, :],
                                                                  start=(nch == nchs[0] and ts % 2 == 0), stop=(nch == nchs[-1]), skip_group_check=True),
                                         reads=[bem], writes=[bbank])
                                if nch == nchs[-1]:
                                    for hb in range(2):
                                        bank, bbank = pair[hb]
                                        finalize(bank[:, 0:260].rearrange("p (a w) -> p a w", a=2), bbank, 128, 2 * hb, 2, tb, hq, hq, True, imp_g=(kh, g))
                            return [s0, s1, s2, s3]

                        for hq in range(4):
                            for nch in nchs:
                                iters.append(mk_cmp(hq, nch))
                        run_pipeline(iters, 4)
                        if pending_E is not None:
                            emit_E(*pending_E)
                            pending_E = None
                        bif = Buf("impf")
                        P.op("dve", lambda e: TT(e, impf8[:].rearrange("p (k t) j -> p k t j", k=2), imp[:].rearrange("p t k j -> p k t j"),
                                                 sfv[:, 4 * tb:4 * tb + 4, :].unsqueeze(1).to_broadcast([128, 2, 4, 64]), ALU.add), reads=[b_imp], writes=[bif])
                        gb = [Buf("g%d" % q) for q in range(8)]
                        for q in range(8):
                            P.op("dve", lambda e: e.max(out=m8a[:, q, :], in_=impf8[:, q, :]), reads=[bif], writes=[gb[q]])
                        for q in range(8):
                            P.op("dve", lambda e: e.match_replace(out=impt8[:, q, :], in_to_replace=m8a[:, q, :], in_values=impf8[:, q, :], imm_value=-3.0e4),
                                 reads=[bif, gb[q]], writes=[gb[q]])
                        for q in range(8):
                            P.op("dve", lambda e: e.max(out=m8b[:, q, :], in_=impt8[:, q, :]), reads=[gb[q]], writes=[gb[q]])
                        bsl = Buf("selb")
                        P.op("dve", lambda e: TT(e, selb8[:], impf8[:], m8b[:, :, 7:8].to_broadcast([128, 8, 64]), ALU.is_ge), reads=[bif] + gb, writes=[bsl])
                        ptr, bptr = pmx.get()
                        ptb = ptr[:].bitcast(BF16)
                        for q in range(8):
                            P.op("pe", lambda e: e.transpose(out=ptb[0:64, q * 128:(q + 1) * 128], in_=selb8[:, q, :], identity=identb[:]), reads=[bsl], writes=[bptr])
                        P.op("act", lambda e: e.activation(out=selT[0:64].rearrange("j k t -> j (k t)"), in_=ptb[0:64, :], func=AF.Copy), reads=[bptr], writes=[b_selT])
                        iters = []
                        started = set()
                        state = {"mm": None}

                        def mk_att(kind, i, kh, g, dd):
                            hq = 2 * kh + g
                            plo, phi = g * 64, g * 64 + 64
                            d = i - 4 * tb
                            sc, bsc_ = psc.get(); e_, be = e_r.get(); em, bem = em_r.get()
                            bank, bbank = pacc[hq]
                            if kind == "sel":
                                kmat, vmat, gbase = ksT, vS, 4
                                ts_list = list(range(max(d, 0), 4))
                                if g == 0:
                                    mx, bmx = pmx.get(); mm, bmm = mm_r.get()
                                    state["mm"] = (mm, bmm)
                                else:
                                    mx = bmx = None
                                    mm, bmm = state["mm"]
                                last_i = 4 * tb + 3
                            else:
                                kmat, vmat, gbase = kwT, vW, 8
                                ts_list = list(range(max(dd, 0), min(dd + 4, 3) + 1))
                                mx = bmx = mm = bmm = None
                                last_i = 4 * tb + 3
                            key = (kind, hq)
                            flags = []
                            for ts in ts_list:
                                flags.append(key not in started)
                                started.add(key)

                            def s0():
                                if mx is not None:
                                    P.op("pe", lambda e: e.matmul(mx[:], lhsT=eexp[:, i * 128:(i + 1) * 128], rhs=selT[:, kh, :], start=True, stop=True),
                                         reads=[b_selT], writes=[bmx])
                                P.op("pe", lambda e: e.matmul(sc[:], lhsT=kmat[:, kh, i * 128:(i + 1) * 128], rhs=qz[:, hq, :],
                                                              start=True, stop=True), reads=[bqz], writes=[bsc_])

                            def s1():
                                if mx is not None:
                                    if d >= 0:
                                        P.op("dve", lambda e: TT(e, mm[:], mx[:], slm[:, d, :], ALU.mult), reads=[bmx], writes=[bmm])
                                    else:
                                        P.op("dve", lambda e: e.tensor_copy(out=mm[:], in_=mx[:]), reads=[bmx], writes=[bmm])
                                P.op("act", lambda e: e.activation(out=e_[:], in_=sc[:], func=AF.Exp, scale=SCALE), reads=[bsc_], writes=[be])

                            def s2():
                                if kind == "sel":
                                    P.op("dve" if g == 0 else "pool", lambda e: TT(e, em[:], e_[:], mm[:], ALU.mult), reads=[be, bmm], writes=[bem])
                                else:
                                    P.op("pool" if g == 0 else "dve", lambda e: TT(e, em[:], e_[:], band[:, dd + 4, :], ALU.mult), reads=[be], writes=[bem])

                            def s3():
                                for ts, fl in zip(ts_list, flags):
                                    P.op("pe", lambda e: e.matmul(bank[:, ts * 66:(ts + 1) * 66], lhsT=em[:, ts * 128:(ts + 1) * 128], rhs=vmat[:, i, kh, :],
                                                                  start=fl, stop=(i == 4 * tb + ts), skip_group_check=True),
                                         reads=[bem], writes=[bbank])
                                if i == last_i:
                                    finalize(bank[:, 0:264].rearrange("p (a w) -> p a w", a=4), bbank, 64, 0, 4, tb, hq, gbase + hq, False)
                            return [s0, s1, s2, s3]

                        for i in range(0, 4 * tb + 4):
                            for kh in range(2):
                                for g in range(2):
                                    iters.append(mk_att("sel", i, kh, g, None))
                        for dd in range(-4, 4):
                            i = 4 * tb + dd
                            if i < 0:
                                continue
                            for kh in range(2):
                                for g in range(2):
                                    iters.append(mk_att("win", i, kh, g, dd))
                        run_pipeline(iters, 4)
                        pending_E = (tb, cur["oacc"][0], cur["oacc"][1])
                    emit_E(*pending_E)
                    P.barrier()


            if "final" in phases:
                with ExitStack() as ph:
                    TT = lambda e, o, a, b, op: e.tensor_tensor(out=o, in0=a, in1=b, op=op)
                    SBW = min(S, 1024)
                    NSB = S // SBW
                    NBB = SBW // 512
                    b_k = Buf("finconst")
                    wo = sbt(ph, "fwo", [128, NKC, D], BF16)
                    wbr = sbt(ph, "fwbr", [128, 4, 2, D], BF16)
                    tmpc = ExitStack()
                    stg = sbt(tmpc, "fstg", [128, NKC, 512], F32)
                    for hf in range(2):
                        P.dma(stg[:], w_o[l, :, hf * 512:(hf + 1) * 512].rearrange("(k p) c -> p k c", p=128), reads=[b_k], writes=[b_k])
                        P.op("pool", lambda e: e.tensor_copy(out=wo[:, :, hf * 512:(hf + 1) * 512], in_=stg[:]), reads=[b_k], writes=[b_k])
                    for m in range(4):
                        for hf in range(2):
                            P.dma(stg[:, 0:2, :], w_br[l, m, :, hf * 512:(hf + 1) * 512].rearrange("(k p) c -> p k c", p=128), reads=[b_k], writes=[b_k])
                            P.op("pool", lambda e: e.tensor_copy(out=wbr[:, m, :, hf * 512:(hf + 1) * 512], in_=stg[:, 0:2, :]), reads=[b_k], writes=[b_k])
                    P.barrier()
                    tmpc.close()
                    oTs_r = Ring(nc, ph, "foTs", [128, 4, 2, SBW], BF16, 2)
                    mixT = sbt(ph, "fmixT", [128, NKC, SBW], BF16); b_mix = Buf("mixT")
                    wbf = Ring(nc, ph, "wbf", [128, NKC, 512], BF16, 2)
                    sg_r = Ring(nc, ph, "fsg", [128, 512], F32, 3); tmp_r = Ring(nc, ph, "ftmp", [128, 512], F32, 2)
                    acc_r = Ring(nc, ph, "facc", [128, 512], F32, 2)
                    xr = Ring(nc, ph, "fx", [128, D], F32, 2); yr = Ring(nc, ph, "fy", [128, D], F32, 2)
                    pg_ = PR([0, 1, 2]); pb_ = PR([3, 4, 5]); po_ = PR([6, 7])
                    units = [(sbk, cch) for sbk in range(NSB) for cch in range(NKC)]
                    wtiles = {}
                    otiles = {}

                    def load_unit(idx):
                        sbk, cch = units[idx]
                        wtiles[idx] = loadw_cast(wbf, w_in[l], [(C_MG + m * D + cch * 128, 128) for m in range(4)])

                    def load_oT(sbk):
                        oTs, b_oT = oTs_r.get()
                        s0 = sbk * SBW
                        for m in range(4):
                            for kk in range(2):
                                P.dma(oTs[:, m, kk, :], oT_d[m, kk * 128:(kk + 1) * 128, s0:s0 + SBW], writes=[b_oT])
                        otiles[sbk] = (oTs, b_oT)

                    load_oT(0)
                    load_unit(0)
                    for idx, (sbk, cch) in enumerate(units):
                        s0 = sbk * SBW
                        if idx + 1 < len(units):
                            load_unit(idx + 1)
                        if cch == 0 and sbk + 1 < NSB:
                            load_oT(sbk + 1)
                        wt, bw = wtiles.pop(idx)
                        oTs, b_oT = otiles[sbk]
                        for tbb in range(NBB):
                            tb = sbk * NBB + tbb
                            acc, bacc = acc_r.get()
                            for m in range(4):
                                pg, bpg = pg_.get(); mm_fm(pg, bpg, wt, bw, m * 128, 128, tb)
                                sg, bsg = sg_r.get()
                                P.op("act", lambda e: e.activation(out=sg[:], in_=pg[:], func=AF.Sigmoid), reads=[bpg], writes=[bsg])
                                pb, bpb = pb_.get()
                                for kk in range(2):
                                    P.op("pe", lambda e: e.matmul(pb[:], lhsT=wbr[:, m, kk, cch * 128:(cch + 1) * 128], rhs=oTs[:, m, kk, tbb * 512:(tbb + 1) * 512],
                                                                  start=(kk == 0), stop=(kk == 1)), reads=[b_oT], writes=[bpb])
                                if m == 0:
                                    P.op("dve", lambda e: TT(e, acc[:], pb[:], sg[:], ALU.mult), reads=[bpb, bsg], writes=[bacc])
                                else:
                                    tmp, btmp = tmp_r.get()
                                    P.op("dve", lambda e: TT(e, tmp[:], pb[:], sg[:], ALU.mult), reads=[bpb, bsg], writes=[btmp])
                                    if m < 3:
                                        P.op("pool", lambda e: TT(e, acc[:], acc[:], tmp[:], ALU.add), reads=[bacc, btmp], writes=[bacc])
                                    else:
                                        P.op("pool", lambda e: TT(e, mixT[:, cch, tbb * 512:(tbb + 1) * 512], acc[:], tmp[:], ALU.add),
                                             reads=[bacc, btmp], writes=[b_mix])
                        if cch != NKC - 1:
                            continue
                        for tt_ in range(SBW // 128):
                            tt = s0 // 128 + tt_
                            xt, bx = xr.get()
                            P.dma(xt[:], xsrc[tt * 128:(tt + 1) * 128, :], writes=[bx])
                            yt, by = yr.get()
                            for hf in range(2):
                                po, bpo = po_.get()
                                for c2 in range(NKC):
                                    P.op("pe", lambda e: e.matmul(po[:], lhsT=mixT[:, c2, tt_ * 128:(tt_ + 1) * 128], rhs=wo[:, c2, hf * 512:(hf + 1) * 512],
                                                                  start=(c2 == 0), stop=(c2 == NKC - 1)), reads=[b_mix], writes=[bpo])
                                P.op("dve", lambda e: TT(e, yt[:, hf * 512:(hf + 1) * 512], po[:], xt[:, hf * 512:(hf + 1) * 512], ALU.add),
                                     reads=[bpo, bx], writes=[by])
                            P.dma(xdst[tt * 128:(tt + 1) * 128, :], yt[:], reads=[by])
                    P.barrier()

            PHASES_PLACEHOLDER = None

            if debug == "oT":
                P.barrier()
                P.dma(dbg, oT_d)
                P.barrier()
                break

        P.barrier()
        print("instructions", P.nins, "waits", P.nwaits)
    return nc


def prep_inputs(inputs, S, n_layers):
    f = lambda a: np.ascontiguousarray(np.asarray(a), dtype=np.float32)
    L = n_layers
    com = {}
    com["norm_g"] = f(inputs["norm_g"])[:L]
    com["w_in"] = f(inputs["w_in"])[:L]
    com["convw"] = f(np.transpose(f(inputs["sc_conv_w"])[:L], (0, 2, 1)))
    sm = lambda a: f(np.transpose(f(a)[:L].reshape(L, 8, 128), (0, 2, 1)))
    com["s5_lr"] = sm(inputs["s5_a_re"]); com["s5_li"] = sm(inputs["s5_a_im"])
    com["s5_ldt"] = sm(np.repeat(f(inputs["s5_log_dt"])[:L, :, None], 64, axis=2))
    def expand_b(b):
        out = np.zeros((L, 128, 8, 128), np.float32)
        b = f(b)[:L]
        for j in range(8):
            for hq in range(2):
                g = 2 * j + hq
                col = 32 * (j % 4) + 16 * hq
                out[:, hq * 64:(hq + 1) * 64, j, col:col + 16] = b[:, g]
        return out
    def expand_c(c_):
        out = np.zeros((L, 128, 8, 128), np.float32)
        c_ = f(c_)[:L]
        for j in range(8):
            for hq in range(2):
                g = 2 * j + hq
                col = 32 * (j % 4) + 16 * hq
                out[:, hq * 64:(hq + 1) * 64, j, col:col + 16] = np.transpose(c_[:, g], (0, 2, 1))
        return out
    com["s5_bre"] = expand_b(inputs["s5_b_re"]); com["s5_bim"] = expand_b(inputs["s5_b_im"])
    com["s5_cre"] = expand_c(inputs["s5_c_re"]); com["s5_cim"] = expand_c(inputs["s5_c_im"])
    com["s5_d"] = f(np.transpose(f(inputs["s5_d"])[:L].reshape(L, 2, 128), (0, 2, 1)))
    com["glu_w"] = f(inputs["s5_glu_w"])[:L]
    com["glu_b"] = f(np.transpose(f(inputs["s5_glu_b"])[:L].reshape(L, 4, 128), (0, 2, 1)))
    com["w_br"] = f(inputs["w_branch"])[:L]; com["w_o"] = f(inputs["w_out"])[:L]
    g = f(inputs["nsa_qk_g"])[:L]
    com["qkg_row"] = f(np.concatenate([np.tile(g[:, 0], (1, 4)), np.tile(g[:, 2], (1, 2)), np.tile(g[:, 3], (1, 2))], axis=1))
    com["qkg1"] = f(np.tile(g[:, 1], (1, 2)))
    pe = f(inputs["nsa_cmp_pe"])[:L]
    com["pe_t"] = f(np.tile(np.transpose(pe, (0, 1, 3, 2)), (1, 1, 2, 1)))
    w1 = f(inputs["nsa_cmp_w1"])[:L].reshape(L, 2, 32, 64, 64)
    w1bd = np.zeros((L, 2, 128, 32, 128), np.float32)
    for kh in range(2):
        w1bd[:, :, kh * 64:(kh + 1) * 64, :, kh * 64:(kh + 1) * 64] = np.transpose(w1, (0, 1, 3, 2, 4))
    com["w1bd"] = w1bd.reshape(L, 2, 128, 32 * 128)
    w2 = f(inputs["nsa_cmp_w2"])[:L]
    w2bd = np.zeros((L, 2, 128, 128), np.float32)
    for kh in range(2):
        w2bd[:, :, kh * 64:(kh + 1) * 64, kh * 64:(kh + 1) * 64] = w2
    com["w2bd"] = w2bd
    com.update(host_consts(S))
    x = np.asarray(inputs["x"]); pos = np.asarray(inputs["positions"])
    maps = []
    for b in range(x.shape[0]):
        m = dict(com)
        m["x"] = np.ascontiguousarray(x[b, :S], dtype=np.float32)
        pb = np.asarray(pos[b, :S], dtype=np.int32)
        pl = np.zeros((128, S // 128 + 2), np.int32)
        pl[:, :S // 128] = pb.reshape(S // 128, 128).T
        ce = pb[31::16]
        n1 = min(128, len(ce))
        pl[:n1, S // 128] = ce[:n1]
        if len(ce) > 128:
            pl[:len(ce) - 128, S // 128 + 1] = ce[128:]
        m["pos"] = pl
        maps.append(m)
    return maps


_NC_CACHE = {}


def kernel(**inputs):
    S = 4096
    L = 2
    maps = prep_inputs(inputs, S, L)
    if "nc" not in _NC_CACHE:
        _NC_CACHE["nc"] = build(S, L)
    res = run_bass_kernel_spmd(_NC_CACHE["nc"], maps, core_ids=list(range(8)))
    return np.stack([np.asarray(r["y"], dtype=np.float32) for r in res.results], axis=0)
```

```python
import math
from contextlib import ExitStack
import numpy as np
import concourse.bass as bass
import concourse.mybir as mybir
from concourse.bass_utils import run_bass_kernel_spmd

F32 = mybir.dt.float32
BF16 = mybir.dt.bfloat16
I32 = mybir.dt.int32
AF = mybir.ActivationFunctionType
ALU = mybir.AluOpType
AX = mybir.AxisListType

D = 1024
NKC = 8
EPS = 1e-6
C_NQ, C_NKV, C_NG, C_NZ = 0, 256, 1024, 1036
C_SC, C_SCZ = 1292, 2060
C_SB, C_SBZ = 2316, 3084
C_S5U, C_S5Z = 3340, 3596
C_MG = 3852
WIN = 7948
NDS = 24
NDW = 6
SAME_ENGINE_WAIT = True
USE_CAST_DMA = True
CAST_Q = "pool"


class Buf:
    __slots__ = ("name", "lw", "rd")

    def __init__(self, name=""):
        self.name = name
        self.lw = []
        self.rd = {}


class Prog:
    def __init__(self, nc, es):
        self.nc = nc
        self.eng = {"pe": nc.tensor, "dve": nc.vector, "act": nc.scalar, "pool": nc.gpsimd, "sp": nc.sync}
        self.sem = {e: es.enter_context(nc.semaphore("s_" + e)) for e in ("pe", "dve", "act", "pool")}
        self.cnt = {e: 0 for e in self.sem}
        self.dsem = [es.enter_context(nc.semaphore("d%d" % i)) for i in range(NDS)]
        self.dsem += [es.enter_context(nc.semaphore("w%d" % i)) for i in range(NDW)]
        self.dcnt = [0] * (NDS + NDW)
        self.dnext = 0
        self.wnext = 0
        self.seen = {e: {} for e in self.eng}
        self.nwaits = 0
        self.nins = 0

    def _wait(self, e, tok):
        if tok is None:
            return
        k, v = tok
        if k == ("c", e) and (e == "pe" or not SAME_ENGINE_WAIT):
            return
        if self.seen[e].get(k, 0) >= v:
            return
        self.seen[e][k] = v
        s = self.sem[k[1]] if k[0] == "c" else self.dsem[k[1]]
        self.eng[e].wait_ge(s, v)
        self.nwaits += 1

    def _deps(self, e, reads, writes, isdma=False):
        for b in reads:
            for t in b.lw:
                self._wait(e, t)
        for b in writes:
            if isdma and not b.rd and b.lw and all(t[0][0] == "d" for t in b.lw):
                continue
            for t in b.lw:
                self._wait(e, t)
            for k, v in b.rd.items():
                self._wait(e, (k, v))

    def _done(self, tok, reads, writes, isdma=False):
        k, v = tok
        for b in reads:
            if b.rd.get(k, 0) < v:
                b.rd[k] = v
        for b in writes:
            if isdma and not b.rd and b.lw and all(t[0][0] == "d" for t in b.lw):
                b.lw.append(tok)
            else:
                b.lw = [tok]
            b.rd = {}

    def op(self, e, fn, reads=(), writes=()):
        self._deps(e, reads, writes)
        ins = fn(self.eng[e])
        self.cnt[e] += 1
        self.nins += 1
        ins.then_inc(self.sem[e], 1)
        tok = (("c", e), self.cnt[e])
        self._done(tok, reads, writes)
        return tok

    def dma(self, out, in_, reads=(), writes=(), q="sp"):
        if q == "pool":
            i = NDS + self.wnext
            self.wnext = (self.wnext + 1) % NDW
        else:
            i = self.dnext
            self.dnext = (i + 1) % NDS
        if self.dcnt[i] > 0:
            self._wait(q, (("d", i), self.dcnt[i]))
        self._deps(q, reads, writes, True)
        ins = self.eng[q].dma_start(out=out, in_=in_)
        self.dcnt[i] += 16
        self.nins += 1
        ins.then_inc(self.dsem[i], 16)
        tok = (("d", i), self.dcnt[i])
        self._done(tok, reads, writes, True)
        return tok

    def barrier(self):
        toks = [(("c", e), self.cnt[e]) for e in self.cnt if self.cnt[e] > 0]
        toks += [(("d", i), self.dcnt[i]) for i in range(NDS + NDW) if self.dcnt[i] > 0]
        for e in self.eng:
            for t in toks:
                self._wait(e, t)


_UID = [0]


def run_pipeline(iters, nst, young_first=True):
    n = len(iters)
    for step in range(n + nst - 1):
        for st in (range(nst) if young_first else range(nst - 1, -1, -1)):
            k = step - st
            if 0 <= k < n and iters[k][st] is not None:
                iters[k][st]()


class Ring:
    def __init__(self, nc, es, name, shape, dtype, n, psum=False):
        self.t = []
        self.b = []
        for i in range(n):
            _UID[0] += 1
            if psum:
                t = es.enter_context(nc.psum_tensor("rp_%s%d_%d" % (name, i, _UID[0]), shape, dtype))
            else:
                t = es.enter_context(nc.sbuf_tensor("rs_%s%d_%d" % (name, i, _UID[0]), shape, dtype))
            self.t.append(t)
            self.b.append(Buf("%s%d" % (name, i)))
        self.i = 0
        self.n = n

    def get(self):
        i = self.i
        self.i = (i + 1) % self.n
        return self.t[i], self.b[i]


def host_consts(S):
    import ml_dtypes
    bf = ml_dtypes.bfloat16
    NT = S // 128
    c = {}
    c["ident"] = np.eye(128, dtype=np.float32)
    half = 32
    invf = np.power(np.float32(10000.0), -np.arange(half, dtype=np.float32) / np.float32(half)).astype(np.float32)
    c["invf"] = np.tile(invf[None, :], (128, 1)).astype(np.float32)
    s_ = np.arange(128)[:, None]
    t_ = np.arange(512)[None, :]
    c["sbm"] = np.stack([(128 * d + s_ < t_) for d in range(4)], 1).astype(bf)
    c["slm"] = np.stack([(128 * d + s_ <= t_) for d in range(4)], 1).astype(bf)
    band = []
    for d in range(-4, 4):
        sa = 128 * d + s_
        band.append((sa <= t_) & (sa > t_ - 512))
    c["band"] = np.stack(band, 1).astype(bf)
    c["ustrict"] = (np.arange(128)[:, None] > np.arange(128)[None, :]).astype(bf)
    c["onesb"] = np.ones((128, 128), dtype=bf)
    ee = np.zeros((128, S), np.float32)
    ee[:64] = (np.arange(S)[None, :] // 64 == np.arange(64)[:, None])
    c["eexp"] = ee.astype(bf)
    t = np.arange(S)
    j = np.arange(64)[None, :]
    cur = (t // 64)[:, None]
    forced = (j == 0) | (j == cur) | (j == cur - 1)
    valid = (j * 64) <= t[:, None]
    vm = (valid & ~forced).astype(np.float32)
    fv = np.where(forced, 1.0e4, np.where(valid, 0.0, -1.0e4)).astype(np.float32)
    c["selvm"] = vm.reshape(NT, 128, 64).transpose(1, 0, 2).astype(bf)
    c["selfv"] = fv.reshape(NT, 128, 64).transpose(1, 0, 2).astype(bf)
    n = np.arange(256)[:, None]
    ovl = ((16 * n < 64 * j + 64) & (16 * n + 32 > 64 * j)).astype(np.float32)
    ext = np.concatenate([ovl, np.ones((256, 1), np.float32), np.zeros((256, 1), np.float32)], 1)
    ext[255, :] = 0
    c["ovl"] = ext.reshape(2, 128, 66).transpose(1, 0, 2).copy()
    return c


def build(S, n_layers, debug=None, phases=("conv", "s5", "sb", "nsa", "final")):
    NT = S // 128
    NB = S // 512
    NCMP = (S - 32) // 16 + 1
    SCALE = 0.125
    nc = bass.Bass("TRN2", target_bir_lowering=False)

    def din(name, shape, dtype=F32):
        return nc.dram_tensor(name, list(shape), dtype, kind="ExternalInput").ap()

    L = n_layers
    x_in = din("x", [S, D])
    pos_in = din("pos", [128, NT + 2], I32)
    norm_g = din("norm_g", [L, D])
    w_in = din("w_in", [L, D, WIN])
    convw = din("convw", [L, 256, 3])
    s5_lr = din("s5_lr", [L, 128, 8]); s5_li = din("s5_li", [L, 128, 8]); s5_ldt = din("s5_ldt", [L, 128, 8])
    s5_bre = din("s5_bre", [L, 128, 8, 128]); s5_bim = din("s5_bim", [L, 128, 8, 128])
    s5_cre = din("s5_cre", [L, 128, 8, 128]); s5_cim = din("s5_cim", [L, 128, 8, 128])
    s5_d = din("s5_d", [L, 128, 2]); glu_w = din("glu_w", [L, 256, 512]); glu_b = din("glu_b", [L, 128, 4])
    w_br = din("w_br", [L, 4, 256, D]); w_o = din("w_o", [L, D, D])
    qkg_row = din("qkg_row", [L, 512]); qkg1 = din("qkg1", [L, 128])
    pe_t = din("pe_t", [L, 2, 128, 32]); w1bd = din("w1bd", [L, 2, 128, 32 * 128]); w2bd = din("w2bd", [L, 2, 128, 128])
    c_ident = din("ident", [128, 128]); c_invf = din("invf", [128, 32])
    c_sbm = din("sbm", [128, 4, 512], BF16); c_slm = din("slm", [128, 4, 512], BF16); c_band = din("band", [128, 8, 512], BF16)
    c_ustrict = din("ustrict", [128, 128], BF16); c_onesb = din("onesb", [128, 128], BF16); c_eexp = din("eexp", [128, S], BF16)
    c_selvm = din("selvm", [128, NT, 64], BF16); c_selfv = din("selfv", [128, NT, 64], BF16); c_ovl = din("ovl", [128, 2, 66])
    y_out = nc.dram_tensor("y", [S, D], F32, kind="ExternalOutput").ap()
    oT_d = nc.dram_tensor("oT_d", [4, 256, S], BF16, kind="Internal").ap()
    xmid = nc.dram_tensor("xmid", [S, D], F32, kind="Internal").ap()
    dbg = None
    if debug == "oT":
        dbg = nc.dram_tensor("dbg", [4, 256, S], BF16, kind="ExternalOutput").ap()

    with ExitStack() as es:
        P = Prog(nc, es)

        def sbt(ctx, name, shape, dtype):
            _UID[0] += 1
            return ctx.enter_context(nc.sbuf_tensor("sb_%s_%d" % (name, _UID[0]), list(shape), dtype))

        identf = sbt(es, "identf", [128, 128], F32)
        identb = sbt(es, "identb", [128, 128], BF16)
        onesb = sbt(es, "onesb", [128, 128], BF16)
        b_c = Buf("const")
        P.dma(identf[:], c_ident, writes=[b_c])
        P.dma(onesb[:], c_onesb, writes=[b_c])
        P.op("dve", lambda e: e.tensor_copy(out=identb[:], in_=identf[:]), reads=[b_c], writes=[b_c])
        pst = [es.enter_context(nc.psum_tensor("ps%d" % i, [128, 512], F32)) for i in range(8)]
        psb = [Buf("ps%d" % i) for i in range(8)]

        class PR:
            def __init__(self, idx):
                self.idx = list(idx); self.i = 0
            def get(self):
                k = self.idx[self.i]; self.i = (self.i + 1) % len(self.idx)
                return pst[k], psb[k]
        ps = PR(range(8))

        hT = sbt(es, "hT", [128, NKC, S], BF16)
        cosT = sbt(es, "cosT", [128, NT, 32], F32); sinT = sbt(es, "sinT", [128, NT, 32], F32)
        cosC = sbt(es, "cosC", [128, 2, 32], F32); sinC = sbt(es, "sinC", [128, 2, 32], F32)

        def sincos(ctx, ang, shape, out_sin, out_cos, tag, rd=()):
            b = Buf(tag)
            ki = sbt(ctx, tag + "ki", shape, I32); kf = sbt(ctx, tag + "kf", shape, F32)
            r = sbt(ctx, tag + "r", shape, F32); c1 = sbt(ctx, tag + "c1", shape, F32)
            TWO_PI = 2.0 * math.pi
            for shift, dst in ((0.0, out_sin), (0.5 * math.pi, out_cos)):
                P.op("dve", lambda e: e.tensor_scalar(out=ki[:], in0=ang, scalar1=1.0 / TWO_PI, scalar2=shift / TWO_PI + 0.5,
                                                      op0=ALU.mult, op1=ALU.add), reads=list(rd), writes=[b])
                P.op("dve", lambda e: e.tensor_copy(out=kf[:], in_=ki[:]), reads=[b], writes=[b])
                P.op("dve", lambda e: e.scalar_tensor_tensor(out=r[:], in0=kf[:], scalar=-TWO_PI, in1=ang, op0=ALU.mult, op1=ALU.add),
                     reads=[b] + list(rd), writes=[b])
                if shift != 0.0:
                    P.op("dve", lambda e: e.tensor_scalar(out=r[:], in0=r[:], scalar1=shift, scalar2=None, op0=ALU.add), reads=[b], writes=[b])
                P.op("dve", lambda e: e.tensor_scalar(out=c1[:], in0=r[:], scalar1=math.pi, scalar2=-TWO_PI, op0=ALU.is_gt, op1=ALU.mult),
                     reads=[b], writes=[b])
                P.op("dve", lambda e: e.tensor_tensor(out=r[:], in0=r[:], in1=c1[:], op=ALU.add), reads=[b], writes=[b])
                P.op("dve", lambda e: e.tensor_scalar(out=c1[:], in0=r[:], scalar1=-math.pi, scalar2=TWO_PI, op0=ALU.is_lt, op1=ALU.mult),
                     reads=[b], writes=[b])
                P.op("dve", lambda e: e.tensor_tensor(out=r[:], in0=r[:], in1=c1[:], op=ALU.add), reads=[b], writes=[b])
                P.op("dve", lambda e: e.tensor_scalar(out=r[:], in0=r[:], scalar1=3.1415925, scalar2=-3.1415925, op0=ALU.min, op1=ALU.max),
                     reads=[b], writes=[b])
                P.op("act", lambda e: e.activation(out=dst, in_=r[:], func=AF.Sin), reads=[b], writes=[b])

        with ExitStack() as ph:
            invf = sbt(ph, "invf", [128, 32], F32)
            posi = sbt(ph, "posi", [128, NT + 2], I32); posf = sbt(ph, "posf", [128, NT + 2], F32)
            ang = sbt(ph, "ang", [128, NT + 2, 32], F32)
            b0 = Buf("rope")
            P.dma(invf[:], c_invf, writes=[b0])
            P.dma(posi[:], pos_in, writes=[b0])
            P.op("dve", lambda e: e.tensor_copy(out=posf[:], in_=posi[:]), reads=[b0], writes=[b0])
            P.op("dve", lambda e: e.tensor_tensor(out=ang[:], in0=posf[:].unsqueeze(2).to_broadcast([128, NT + 2, 32]),
                                                  in1=invf[:].unsqueeze(1).to_broadcast([128, NT + 2, 32]), op=ALU.mult),
                 reads=[b0], writes=[b0])
            sn = sbt(ph, "sn_all", [128, NT + 2, 32], F32); cs = sbt(ph, "cs_all", [128, NT + 2, 32], F32)
            sincos(ph, ang[:], [128, NT + 2, 32], sn[:], cs[:], "rp", rd=[b0])
            P.barrier()
            P.op("dve", lambda e: e.tensor_copy(out=cosT[:], in_=cs[:, 0:NT, :]))
            P.op("dve", lambda e: e.tensor_copy(out=sinT[:], in_=sn[:, 0:NT, :]))
            P.op("dve", lambda e: e.tensor_copy(out=cosC[:], in_=cs[:, NT:NT + 2, :]))
            P.op("dve", lambda e: e.tensor_copy(out=sinC[:], in_=sn[:, NT:NT + 2, :]))
            P.barrier()

        def loadw(wst, wbf, src, cols, eng="pool"):
            if USE_CAST_DMA:
                return loadw_cast(wbf, src, cols)
            ws, bws = wst.get(); wt, bw = wbf.get()
            off = 0
            for (c0, n) in cols:
                P.dma(ws[:, :, off:off + n], src[:, c0:c0 + n].rearrange("(k p) c -> p k c", p=128), writes=[bws])
                off += n
            P.op(eng, lambda e: e.tensor_copy(out=wt[:, :, 0:off], in_=ws[:, :, 0:off]), reads=[bws], writes=[bw])
            return wt, bw

        _simstg = {}
        if CAST_Q != "pool":
            _simstg["t"] = sbt(es, "simstg", [128, NKC * 512], F32)
            _simstg["b"] = Buf("simstg")

        def cast_dma(out, in_, bw):
            if CAST_Q == "pool":
                P.dma(out, in_, writes=[bw], q="pool")
                return
            n = 1
            for d_ in out.shape[1:]:
                n *= d_
            stg = _simstg["t"][:, 0:n]
            if len(out.shape) == 3:
                stg = stg.rearrange("p (a b) -> p a b", a=out.shape[1])
            P.dma(stg, in_, reads=[_simstg["b"]], writes=[_simstg["b"]])
            P.op("dve", lambda e: e.tensor_copy(out=out, in_=stg), reads=[_simstg["b"]], writes=[bw, _simstg["b"]])

        def loadw_cast(wbf, src, cols):
            wt, bw = wbf.get()
            off = 0
            for (c0, n) in cols:
                cast_dma(wt[:, :, off:off + n], src[:, c0:c0 + n].rearrange("(k p) c -> p k c", p=128), bw)
                off += n
            return wt, bw

        def mm_fm(pt, bp, wt, bw, c0, cw, tb):
            for kc in range(NKC):
                P.op("pe", lambda e: e.matmul(pt[0:cw, :], lhsT=wt[:, kc, c0:c0 + cw], rhs=hT[:, kc, tb * 512:(tb + 1) * 512],
                                              start=(kc == 0), stop=(kc == NKC - 1)), reads=[bw], writes=[bp])

        def mm_tm(pt, bp, wt, bw, c0, cw, tt, po=0):
            for kc in range(NKC):
                P.op("pe", lambda e: e.matmul(pt[:, po:po + cw], lhsT=hT[:, kc, tt * 128:(tt + 1) * 128], rhs=wt[:, kc, c0:c0 + cw],
                                              start=(kc == 0), stop=(kc == NKC - 1)), reads=[bw], writes=[bp])

        for l in range(n_layers):
            xsrc = x_in if l == 0 else xmid
            xdst = y_out if l == n_layers - 1 else xmid
            with ExitStack() as ph:
                gB = sbt(ph, "gB", [128, D], F32)
                b_g = Buf("gB")
                P.dma(gB[:], norm_g[l:l + 1, :].partition_broadcast(128), writes=[b_g])
                xr = Ring(nc, ph, "xt", [128, D], F32, 5)
                junk = Ring(nc, ph, "junk", [128, D], F32, 1)
                xnr = Ring(nc, ph, "xn", [128, D], BF16, 2)
                st = Ring(nc, ph, "st", [128, 4], F32, 4)
                b_h = Buf("hT")
                pp0 = PR([0, 1, 2, 3])
                iters = []

                def mk_p0(i):
                    xt, bx = xr.get(); s_, bs = st.get(); xn, bn = xnr.get(); pt, bp = pp0.get()

                    def s0():
                        P.dma(xt[:], xsrc[i * 128:(i + 1) * 128, :], writes=[bx])

                    def s1():
                        jt, bj = junk.get()
                        P.op("act", lambda e: e.activation(out=jt[:], in_=xt[:], func=AF.Square, accum_out=s_[:, 0:1]), reads=[bx], writes=[bj, bs])
                        P.op("act", lambda e: e.activation(out=s_[:, 1:2], in_=s_[:, 0:1], func=AF.Sqrt, bias=EPS, scale=1.0 / D), reads=[bs], writes=[bs])

                    def s2():
                        P.op("dve", lambda e: e.reciprocal(out=s_[:, 2:3], in_=s_[:, 1:2]), reads=[bs], writes=[bs])
                        P.op("dve", lambda e: e.scalar_tensor_tensor(out=xn[:], in0=xt[:], scalar=s_[:, 2:3], in1=gB[:], op0=ALU.mult, op1=ALU.mult),
                             reads=[bx, bs, b_g], writes=[bn])

                    def s3():
                        ptb = pt[:].bitcast(BF16)
                        for kc in range(NKC):
                            P.op("pe", lambda e: e.transpose(out=ptb[:, kc * 128:(kc + 1) * 128], in_=xn[:, kc * 128:(kc + 1) * 128], identity=identb[:]),
                                 reads=[bn, b_c], writes=[bp])

                    def s4():
                        ptb = pt[:].bitcast(BF16)
                        P.op("act" if i % 2 == 0 else "pool_or_dve", None) if False else None
                        if i % 3 == 2:
                            P.op("dve", lambda e: e.tensor_copy(out=hT[:, :, i * 128:(i + 1) * 128], in_=ptb.rearrange("p (k t) -> p k t", k=NKC)), reads=[bp], writes=[b_h])
                        else:
                            P.op("act", lambda e: e.activation(out=hT[:, :, i * 128:(i + 1) * 128], in_=ptb.rearrange("p (k t) -> p k t", k=NKC), func=AF.Copy),
                                 reads=[bp], writes=[b_h])
                    return [s0, s1, s2, s3, s4]

                for i in range(NT):
                    iters.append(mk_p0(i))
                run_pipeline(iters, 5)
                P.barrier()

            if "conv" in phases:
                with ExitStack() as ph:
                    wst = Ring(nc, ph, "wst", [128, NKC, 512], F32, 2); wbf = Ring(nc, ph, "wbf", [128, NKC, 512], BF16, 2)
                    cwt = sbt(ph, "cwt", [128, 2, 3], F32); b_cw = Buf()
                    P.dma(cwt[:], convw[l].rearrange("(c p) j -> p c j", p=128), writes=[b_cw])
                    u = sbt(ph, "u", [128, S + 2], F32); b_u = Buf()
                    xs_r = Ring(nc, ph, "xs", [128, 512], F32, 2); sz_r = Ring(nc, ph, "sz", [128, 512], F32, 2)
                    bz_r = Ring(nc, ph, "bz", [128, 512], F32, 2); y_r = Ring(nc, ph, "y", [128, 512], F32, 2)
                    o_r = Ring(nc, ph, "o", [128, 512], BF16, 2)
                    for cc in range(2):
                        wt, bw = loadw(wst, wbf, w_in[l], [(C_SC + cc * 128, 128), (C_SC + 256 + cc * 128, 128),
                                                           (C_SC + 512 + cc * 128, 128), (C_SCZ + cc * 128, 128)])
                        P.op("pool", lambda e: e.memset(u[:, 0:2], 0.0), writes=[b_u])
                        for tb in range(NB):
                            t0 = tb * 512
                            pB, bB = ps.get(); mm_fm(pB, bB, wt, bw, 0, 128, tb)
                            pC, bC = ps.get(); mm_fm(pC, bC, wt, bw, 128, 128, tb)
                            pX, bX = ps.get(); mm_fm(pX, bX, wt, bw, 256, 128, tb)
                            pZ, bZ = ps.get(); mm_fm(pZ, bZ, wt, bw, 384, 128, tb)
                            xs, bxs = xs_r.get()
                            P.op("act", lambda e: e.activation(out=xs[:], in_=pX[:], func=AF.Copy), reads=[bX], writes=[bxs])
                            P.op("dve", lambda e: e.tensor_tensor(out=u[:, 2 + t0:2 + t0 + 512], in0=pC[:], in1=xs[:], op=ALU.mult),
                                 reads=[bC, bxs], writes=[b_u])
                            sz, bsz = sz_r.get()
                            P.op("act", lambda e: e.activation(out=sz[:], in_=pZ[:], func=AF.Silu), reads=[bZ], writes=[bsz])
                            bz, bbz = bz_r.get()
                            P.op("dve", lambda e: e.tensor_tensor(out=bz[:], in0=pB[:], in1=sz[:], op=ALU.mult), reads=[bB, bsz], writes=[bbz])
                            y, by = y_r.get()
                            P.op("dve", lambda e: e.tensor_scalar(out=y[:], in0=u[:, 2 + t0:2 + t0 + 512], scalar1=cwt[:, cc, 2:3], scalar2=None,
                                                                  op0=ALU.mult), reads=[b_u, b_cw], writes=[by])
                            P.op("dve", lambda e: e.scalar_tensor_tensor(out=y[:], in0=u[:, 1 + t0:1 + t0 + 512], scalar=cwt[:, cc, 1:2], in1=y[:],
                                                                         op0=ALU.mult, op1=ALU.add), reads=[b_u, by], writes=[by])
                            P.op("dve", lambda e: e.scalar_tensor_tensor(out=y[:], in0=u[:, t0:t0 + 512], scalar=cwt[:, cc, 0:1], in1=y[:],
                                                                         op0=ALU.mult, op1=ALU.add), reads=[b_u, by], writes=[by])
                            o, bo = o_r.get()
                            P.op("pool", lambda e: e.tensor_tensor(out=o[:], in0=y[:], in1=bz[:], op=ALU.mult), reads=[by, bbz], writes=[bo])
                            P.dma(oT_d[1, cc * 128:(cc + 1) * 128, t0:t0 + 512], o[:], reads=[bo])
                    P.barrier()


            if "s5" in phases:
                with ExitStack() as ph:
                    TC = 512
                    sp_ = {}
                    bp_ = Buf("s5par")
                    for nm, src in (("lr", s5_lr), ("li", s5_li), ("ldt", s5_ldt)):
                        sp_[nm] = sbt(ph, "s5" + nm, [128, 8], F32)
                        P.dma(sp_[nm][:], src[l], writes=[bp_])
                    def t8(nm, w=8):
                        sp_[nm] = sbt(ph, "s5" + nm, [128, w], F32)
                        return sp_[nm]
                    V = lambda nm: sp_[nm][:]
                    def dv(fn, eng="dve"):
                        P.op(eng, fn, reads=[bp_], writes=[bp_])
                    t8("dt"); t8("lrd"); t8("mag"); t8("ang"); t8("sn"); t8("cs"); t8("abre"); t8("abim"); t8("den"); t8("rden")
                    t8("abm1"); t8("t1"); t8("t2"); t8("cre"); t8("cim")
                    dv(lambda e: e.activation(out=V("dt"), in_=V("ldt"), func=AF.Exp), "act")
                    dv(lambda e: e.tensor_tensor(out=V("lrd"), in0=V("lr"), in1=V("dt"), op=ALU.mult))
                    dv(lambda e: e.activation(out=V("mag"), in_=V("lrd"), func=AF.Exp), "act")
                    dv(lambda e: e.tensor_tensor(out=V("ang"), in0=V("li"), in1=V("dt"), op=ALU.mult))
                    P.barrier()
                    sincos(ph, V("ang"), [128, 8], V("sn"), V("cs"), "s5sc")
                    P.barrier()
                    dv(lambda e: e.tensor_tensor(out=V("abre"), in0=V("mag"), in1=V("cs"), op=ALU.mult))
                    dv(lambda e: e.tensor_tensor(out=V("abim"), in0=V("mag"), in1=V("sn"), op=ALU.mult))
                    dv(lambda e: e.tensor_tensor(out=V("t1"), in0=V("lr"), in1=V("lr"), op=ALU.mult))
                    dv(lambda e: e.tensor_tensor(out=V("t2"), in0=V("li"), in1=V("li"), op=ALU.mult))
                    dv(lambda e: e.tensor_tensor(out=V("den"), in0=V("t1"), in1=V("t2"), op=ALU.add))
                    dv(lambda e: e.reciprocal(out=V("rden"), in_=V("den")))
                    dv(lambda e: e.tensor_scalar(out=V("abm1"), in0=V("abre"), scalar1=-1.0, scalar2=None, op0=ALU.add))
                    dv(lambda e: e.tensor_tensor(out=V("t1"), in0=V("abm1"), in1=V("lr"), op=ALU.mult))
                    dv(lambda e: e.tensor_tensor(out=V("t2"), in0=V("abim"), in1=V("li"), op=ALU.mult))
                    dv(lambda e: e.tensor_tensor(out=V("t1"), in0=V("t1"), in1=V("t2"), op=ALU.add))
                    dv(lambda e: e.tensor_tensor(out=V("cre"), in0=V("t1"), in1=V("rden"), op=ALU.mult))
                    dv(lambda e: e.tensor_tensor(out=V("t1"), in0=V("abim"), in1=V("lr"), op=ALU.mult))
                    dv(lambda e: e.tensor_tensor(out=V("t2"), in0=V("abm1"), in1=V("li"), op=ALU.mult))
                    dv(lambda e: e.tensor_tensor(out=V("t1"), in0=V("t1"), in1=V("t2"), op=ALU.subtract))
                    dv(lambda e: e.tensor_tensor(out=V("cim"), in0=V("t1"), in1=V("rden"), op=ALU.mult))
                    pc = sbt(ph, "s5pc", [128, 8, 10], F32); pn = sbt(ph, "s5pn", [128, 8, 10], F32)
                    dv(lambda e: e.tensor_copy(out=pc[:, :, 0], in_=V("cs")))
                    dv(lambda e: e.tensor_copy(out=pn[:, :, 0], in_=V("sn")))
                    for m in range(1, 10):
                        dv(lambda e: e.tensor_tensor(out=V("t1"), in0=pc[:, :, m - 1], in1=pc[:, :, m - 1], op=ALU.mult))
                        dv(lambda e: e.tensor_tensor(out=V("t2"), in0=pn[:, :, m - 1], in1=pn[:, :, m - 1], op=ALU.mult))
                        dv(lambda e: e.tensor_tensor(out=pc[:, :, m], in0=V("t1"), in1=V("t2"), op=ALU.subtract))
                        dv(lambda e: e.tensor_tensor(out=V("t1"), in0=pc[:, :, m - 1], in1=pn[:, :, m - 1], op=ALU.mult))
                        dv(lambda e: e.tensor_scalar(out=pn[:, :, m], in0=V("t1"), scalar1=2.0, scalar2=None, op0=ALU.mult))
                    Ec = sbt(ph, "s5Ec", [128, 8, TC], F32); Es = sbt(ph, "s5Es", [128, 8, TC], F32)
                    bbT = sbt(ph, "s5bbT", [128, 2, 8, 128], BF16)
                    CTr = sbt(ph, "s5CTr", [128, 8, 128], BF16); CTi = sbt(ph, "s5CTi", [128, 8, 128], BF16)
                    dsk = sbt(ph, "s5dsk", [128, 2], F32); glb = sbt(ph, "s5glb", [128, 4], F32)
                    glw = sbt(ph, "s5glw", [128, 2, 512], BF16)
                    uT = sbt(ph, "s5uT", [128, 2, S], BF16); b_uT = Buf("uT")
                    wbf = Ring(nc, ph, "wbf", [128, NKC, 512], BF16, 1)
                    tmpc = ExitStack()
                    tA = sbt(tmpc, "s5tA", [128, 8, TC // 2], F32); tB = sbt(tmpc, "s5tB", [128, 8, TC // 2], F32)
                    dv(lambda e: e.memset(Ec[:, :, 0:1], 1.0)); dv(lambda e: e.memset(Es[:, :, 0:1], 0.0))
                    k = 1
                    m = 0
                    while k < TC:
                        ck = pc[:, :, m:m + 1].to_broadcast([128, 8, k]); sk = pn[:, :, m:m + 1].to_broadcast([128, 8, k])
                        dv(lambda e: e.tensor_tensor(out=tA[:, :, 0:k], in0=Ec[:, :, 0:k], in1=ck, op=ALU.mult))
                        dv(lambda e: e.tensor_tensor(out=tB[:, :, 0:k], in0=Es[:, :, 0:k], in1=sk, op=ALU.mult))
                        dv(lambda e: e.tensor_tensor(out=Ec[:, :, k:2 * k], in0=tA[:, :, 0:k], in1=tB[:, :, 0:k], op=ALU.subtract))
                        dv(lambda e: e.tensor_tensor(out=tA[:, :, 0:k], in0=Ec[:, :, 0:k], in1=sk, op=ALU.mult))
                        dv(lambda e: e.tensor_tensor(out=tB[:, :, 0:k], in0=Es[:, :, 0:k], in1=ck, op=ALU.mult))
                        dv(lambda e: e.tensor_tensor(out=Es[:, :, k:2 * k], in0=tA[:, :, 0:k], in1=tB[:, :, 0:k], op=ALU.add))
                        k *= 2; m += 1
                    P.barrier()
                    tmpc.close()
                    tmpc = ExitStack()
                    bex_r = sbt(tmpc, "s5bexr", [128, 8, 128], F32); bex_i = sbt(tmpc, "s5bexi", [128, 8, 128], F32)
                    P.dma(bex_r[:], s5_bre[l], writes=[bp_]); P.dma(bex_i[:], s5_bim[l], writes=[bp_])
                    bbs = Ring(nc, tmpc, "s5bbs", [128, 128], F32, 2); bbt = Ring(nc, tmpc, "s5bbt", [128, 128], F32, 2)
                    for j in range(8):
                        for ri in range(2):
                            tt_, btt = bbt.get(); bb_, bbb = bbs.get()
                            if ri == 0:
                                P.op("dve", lambda e: e.tensor_scalar(out=tt_[:], in0=bex_i[:, j, :], scalar1=sp_["cim"][:, j:j + 1], scalar2=None, op0=ALU.mult),
                                     reads=[bp_], writes=[btt])
                                P.op("dve", lambda e: e.scalar_tensor_tensor(out=bb_[:], in0=bex_r[:, j, :], scalar=sp_["cre"][:, j:j + 1], in1=tt_[:],
                                                                             op0=ALU.mult, op1=ALU.subtract), reads=[bp_, btt], writes=[bbb])
                            else:
                                P.op("dve", lambda e: e.tensor_scalar(out=tt_[:], in0=bex_r[:, j, :], scalar1=sp_["cim"][:, j:j + 1], scalar2=None, op0=ALU.mult),
                                     reads=[bp_], writes=[btt])
                                P.op("dve", lambda e: e.scalar_tensor_tensor(out=bb_[:], in0=bex_i[:, j, :], scalar=sp_["cre"][:, j:j + 1], in1=tt_[:],
                                                                             op0=ALU.mult, op1=ALU.add), reads=[bp_, btt], writes=[bbb])
                            pt, bpt = ps.get()
                            P.op("pe", lambda e: e.transpose(out=pt[:, 0:128], in_=bb_[:], identity=identf[:]), reads=[bbb, b_c], writes=[bpt])
                            P.op("act", lambda e: e.activation(out=bbT[:, ri, j, :], in_=pt[:, 0:128], func=AF.Copy), reads=[bpt], writes=[bp_])
                    cst = sbt(tmpc, "s5cst", [128, 8, 128], F32)
                    P.dma(cst[:], s5_cre[l], writes=[bp_])
                    dv(lambda e: e.tensor_copy(out=CTr[:], in_=cst[:]))
                    P.dma(cst[:], s5_cim[l], reads=[bp_], writes=[bp_])
                    dv(lambda e: e.tensor_scalar(out=CTi[:], in0=cst[:], scalar1=-1.0, scalar2=None, op0=ALU.mult))
                    P.dma(dsk[:], s5_d[l], writes=[bp_]); P.dma(glb[:], glu_b[l], writes=[bp_])
                    gst = sbt(tmpc, "s5gst", [128, 2, 512], F32)
                    P.dma(gst[:], glu_w[l].rearrange("(k p) c -> p k c", p=128), writes=[bp_])
                    dv(lambda e: e.tensor_copy(out=glw[:], in_=gst[:]))
                    wst = Ring(nc, tmpc, "wst", [128, NKC, 512], F32, 1)
                    wt, bw = loadw(wst, wbf, w_in[l], [(C_S5U, 256), (C_S5Z, 256)])
                    P.barrier()
                    tmpc.close()
                    for tb in range(NB):
                        for cc in range(2):
                            pt, bpt = ps.get(); mm_fm(pt, bpt, wt, bw, cc * 128, 128, tb)
                            P.op("act", lambda e: e.activation(out=uT[:, cc, tb * 512:(tb + 1) * 512], in_=pt[:], func=AF.Copy), reads=[bpt], writes=[b_uT])
                    ini = sbt(ph, "s5ini", [128, 2, 8], F32); b_ini = Buf("ini")
                    P.op("dve", lambda e: e.memset(ini[:], 0.0), writes=[b_ini])
                    itmp = sbt(ph, "s5itmp", [128, 8, 4], F32)
                    b_inis = [b_ini] * 8
                    b_it = [Buf("it%d" % q) for q in range(8)]
                    nsT9 = sbt(ph, "s5nsT9", [128, 8], F32)
                    P.op("dve", lambda e: e.tensor_scalar(out=nsT9[:], in0=pn[:, :, 9], scalar1=-1.0, scalar2=None, op0=ALU.mult), writes=[b_ini])
                    b_inis = [Buf("ini%d" % q) for q in range(8)]
                    P.barrier()
                    ab_r = Ring(nc, ph, "s5ab", [128, 4, 512], F32, 1)
                    mre = Ring(nc, ph, "s5mre", [128, 2, 512], F32, 2)
                    xre = Ring(nc, ph, "s5xre", [128, 2, 512], F32, 2)
                    dd_r = Ring(nc, ph, "s5dd", [128, 4, 512], F32, 1)
                    xs_r = Ring(nc, ph, "s5xs", [128, 2, 8, 512], BF16, 1)
                    yT_r = Ring(nc, ph, "s5yT", [128, 2, 512], BF16, 1)
                    sg_r = Ring(nc, ph, "s5sg", [128, 512], F32, 1); rs_r = Ring(nc, ph, "s5rs", [128, 512], F32, 1)
                    sz_r = Ring(nc, ph, "s5sz", [128, 512], F32, 1); o_r = Ring(nc, ph, "s5o", [128, 512], BF16, 2)
                    TT = lambda e, o, a, b, op: e.tensor_tensor(out=o, in0=a, in1=b, op=op)
                    ppr = PR([0, 1]); ppi = PR([2, 3]); pep = PR([4, 5, 6, 7])
                    iters = []

                    def mk_s5(tb, j, xs, bxs):
                        t0 = tb * 512
                        pr, bpr = ppr.get(); pi_, bpi = ppi.get()
                        ab, bab = ab_r.get(); mm_, bmm = mre.get(); xx, bxx = xre.get(); dd4, bdd = dd_r.get()
                        cT = pc[:, j, 9:10]; sT = pn[:, j, 9:10]

                        def s0():
                            P.op("pe", lambda e: e.matmul(pr[:], lhsT=bbT[:, 0, j, :], rhs=uT[:, j // 4, t0:t0 + 512], start=True, stop=True),
                                 reads=[b_uT], writes=[bpr])
                            P.op("pe", lambda e: e.matmul(pi_[:], lhsT=bbT[:, 1, j, :], rhs=uT[:, j // 4, t0:t0 + 512], start=True, stop=True),
                                 reads=[b_uT], writes=[bpi])

                        def s1():
                            P.op("dve", lambda e: TT(e, ab[:, 0, :], pr[:], Ec[:, j, :], ALU.mult), reads=[bpr], writes=[bab])
                            P.op("dve", lambda e: TT(e, ab[:, 1, :], pi_[:], Es[:, j, :], ALU.mult), reads=[bpi], writes=[bab])
                            P.op("dve", lambda e: TT(e, ab[:, 2, :], pi_[:], Ec[:, j, :], ALU.mult), reads=[bpi], writes=[bab])
                            P.op("dve", lambda e: TT(e, ab[:, 3, :], pr[:], Es[:, j, :], ALU.mult), reads=[bpr], writes=[bab])
                            P.op("pool", lambda e: TT(e, mm_[:, 0, :], ab[:, 0, :], ab[:, 1, :], ALU.add), reads=[bab], writes=[bmm])
                            P.op("pool", lambda e: TT(e, mm_[:, 1, :], ab[:, 2, :], ab[:, 3, :], ALU.subtract), reads=[bab], writes=[bmm])

                        def s2():
                            P.op("dve", lambda e: e.tensor_tensor_scan(out=xx[:, 0, :], data0=sp_["mag"][:, j:j + 1].to_broadcast([128, 512]), data1=mm_[:, 0, :],
                                                                       initial=ini[:, 0, j:j + 1], op0=ALU.mult, op1=ALU.add), reads=[bmm, b_inis[j]], writes=[bxx])
                            P.op("dve", lambda e: e.tensor_tensor_scan(out=xx[:, 1, :], data0=sp_["mag"][:, j:j + 1].to_broadcast([128, 512]), data1=mm_[:, 1, :],
                                                                       initial=ini[:, 1, j:j + 1], op0=ALU.mult, op1=ALU.add), reads=[bmm, b_inis[j]], writes=[bxx])
                            it_ = itmp[:, j, :]
                            P.op("act", lambda e: e.activation(out=it_[:, 0:1], in_=xx[:, 1, 511:512], func=AF.Copy, scale=nsT9[:, j:j + 1]), reads=[bxx], writes=[b_it[j]])
                            P.op("act", lambda e: e.activation(out=it_[:, 1:2], in_=xx[:, 1, 511:512], func=AF.Copy, scale=cT), reads=[bxx], writes=[b_it[j]])
                            P.op("act", lambda e: e.activation(out=ini[:, 0, j:j + 1], in_=xx[:, 0, 511:512], func=AF.Identity, scale=cT, bias=it_[:, 0:1]),
                                 reads=[bxx, b_it[j]], writes=[b_inis[j]])
                            P.op("act", lambda e: e.activation(out=ini[:, 1, j:j + 1], in_=xx[:, 0, 511:512], func=AF.Identity, scale=sT, bias=it_[:, 1:2]),
                                 reads=[bxx, b_it[j]], writes=[b_inis[j]])

                        def s3():
                            P.op("dve", lambda e: TT(e, dd4[:, 0, :], xx[:, 0, :], Ec[:, j, :], ALU.mult), reads=[bxx], writes=[bdd])
                            P.op("dve", lambda e: TT(e, dd4[:, 1, :], xx[:, 1, :], Es[:, j, :], ALU.mult), reads=[bxx], writes=[bdd])
                            P.op("dve", lambda e: TT(e, dd4[:, 2, :], xx[:, 0, :], Es[:, j, :], ALU.mult), reads=[bxx], writes=[bdd])
                            P.op("pool", lambda e: TT(e, dd4[:, 3, :], xx[:, 1, :], Ec[:, j, :], ALU.mult), reads=[bxx], writes=[bdd])
                            P.op("pool", lambda e: TT(e, xs[:, 0, j, :], dd4[:, 0, :], dd4[:, 1, :], ALU.subtract), reads=[bdd], writes=[bxs])
                            P.op("pool", lambda e: TT(e, xs[:, 1, j, :], dd4[:, 2, :], dd4[:, 3, :], ALU.add), reads=[bdd], writes=[bxs])
                            if j != 7:
                                return
                            yT, byT = yT_r.get()
                            for cc in range(2):
                                py, bpy = pep.get()
                                for jj in range(4):
                                    j2 = cc * 4 + jj
                                    P.op("pe", lambda e: e.matmul(py[:], lhsT=CTr[:, j2, :], rhs=xs[:, 0, j2, :], start=(jj == 0), stop=False), reads=[bxs], writes=[bpy])
                                    P.op("pe", lambda e: e.matmul(py[:], lhsT=CTi[:, j2, :], rhs=xs[:, 1, j2, :], start=False, stop=(jj == 3)), reads=[bxs], writes=[bpy])
                                P.op("dve", lambda e: e.scalar_tensor_tensor(out=yT[:, cc, :], in0=uT[:, cc, t0:t0 + 512], scalar=dsk[:, cc:cc + 1], in1=py[:],
                                                                             op0=ALU.mult, op1=ALU.add), reads=[bpy, b_uT], writes=[byT])
                            for c in range(2):
                                pa, bpa = pep.get(); pg, bpg = pep.get()
                                for kk in range(2):
                                    P.op("pe", lambda e: e.matmul(pa[:], lhsT=glw[:, kk, c * 128:(c + 1) * 128], rhs=yT[:, kk, :], start=(kk == 0), stop=(kk == 1)),
                                         reads=[byT], writes=[bpa])
                                for kk in range(2):
                                    P.op("pe", lambda e: e.matmul(pg[:], lhsT=glw[:, kk, (c + 2) * 128:(c + 3) * 128], rhs=yT[:, kk, :], start=(kk == 0), stop=(kk == 1)),
                                         reads=[byT], writes=[bpg])
                                sg, bsg = sg_r.get()
                                P.op("act", lambda e: e.activation(out=sg[:], in_=pg[:], func=AF.Sigmoid, bias=glb[:, c + 2:c + 3]), reads=[bpg], writes=[bsg])
                                rs, brs = rs_r.get()
                                P.op("dve", lambda e: e.scalar_tensor_tensor(out=rs[:], in0=pa[:], scalar=glb[:, c:c + 1], in1=sg[:], op0=ALU.add, op1=ALU.mult),
                                     reads=[bpa, bsg], writes=[brs])
                                pz, bpz = pep.get(); mm_fm(pz, bpz, wt, bw, 256 + c * 128, 128, tb)
                                sz, bsz = sz_r.get()
                                P.op("act", lambda e: e.activation(out=sz[:], in_=pz[:], func=AF.Silu), reads=[bpz], writes=[bsz])
                                o, bo = o_r.get()
                                P.op("pool", lambda e: TT(e, o[:], rs[:], sz[:], ALU.mult), reads=[brs, bsz], writes=[bo])
                                P.dma(oT_d[3, c * 128:(c + 1) * 128, t0:t0 + 512], o[:], reads=[bo])
                        return [s0, s1, s2, s3]

                    for tb in range(NB):
                        xs, bxs = xs_r.get()
                        for j in range(8):
                            iters.append(mk_s5(tb, j, xs, bxs))
                    run_pipeline(iters, 4)
                    P.barrier()

            if "sb" in phases:
                with ExitStack() as ph:
                    TT = lambda e, o, a, b, op: e.tensor_tensor(out=o, in0=a, in1=b, op=op)
                    sbm = sbt(ph, "sbm", [128, 4, 512], BF16); ustr = sbt(ph, "ustr", [128, 128], BF16)
                    b_k = Buf("sbconst")
                    P.dma(sbm[:], c_sbm, writes=[b_k]); P.dma(ustr[:], c_ustrict, writes=[b_k])
                    F32R = mybir.dt.float32r
                    ustrf = sbt(ph, "ustrf", [128, 128], F32); onesf = sbt(ph, "onesf", [128, 128], F32)
                    P.op("dve", lambda e: e.tensor_copy(out=ustrf[:].bitcast(F32R), in_=ustr[:]), reads=[b_k], writes=[b_k])
                    P.op("dve", lambda e: e.tensor_copy(out=onesf[:].bitcast(F32R), in_=onesb[:]), reads=[b_k, b_c], writes=[b_k])
                    qT = sbt(ph, "sbqT", [128, 2, S], BF16); kT = sbt(ph, "sbkT", [128, 2, S], BF16)
                    vz = sbt(ph, "sbvz", [128, NT, 4, 128], BF16)
                    P.op("pool", lambda e: e.memset(vz[:], 0.0), writes=[b_k])
                    szT = sbt(ph, "sbszT", [128, 2, S], BF16)
                    tmpc = ExitStack()
                    wbf = Ring(nc, tmpc, "wbf", [128, NKC, 512], BF16, 1)
                    wst = Ring(nc, tmpc, "wst", [128, NKC, 512], F32, 1)
                    wt, bw = loadw(wst, wbf, w_in[l], [(C_SB, 512)])
                    for tb in range(NB):
                        for c4 in range(4):
                            pt, bpt = ps.get(); mm_fm(pt, bpt, wt, bw, c4 * 128, 128, tb)
                            bsl = slice(tb * 512, (tb + 1) * 512)
                            if c4 < 2:
                                P.op("act", lambda e: e.activation(out=qT[:, c4, bsl], in_=pt[:], func=AF.Copy), reads=[bpt], writes=[b_k])
                            else:
                                P.op("dve", lambda e: e.tensor_copy(out=kT[:, c4 - 2, bsl], in_=pt[:]), reads=[bpt], writes=[b_k])
                    wt, bw = loadw(wst, wbf, w_in[l], [(C_SB + 512, 256), (C_SBZ, 256)])
                    for tt in range(NT):
                        pt, bpt = ps.get(); mm_tm(pt, bpt, wt, bw, 0, 256, tt)
                        vz4 = vz[:, tt].rearrange("p (a b) c -> p a b c", b=2)
                        pt4 = pt[:, 0:256].rearrange("p (a b d) -> p a b d", b=2, d=64)
                        P.op("act", lambda e: e.activation(out=vz4[:, :, 0, 0:64], in_=pt4[:, :, 0, :], func=AF.Copy), reads=[bpt], writes=[b_k])
                        P.op("dve", lambda e: e.tensor_copy(out=vz4[:, :, 1, 64:128], in_=pt4[:, :, 1, :]), reads=[bpt], writes=[b_k])
                    for tb in range(NB):
                        for hc in range(2):
                            pt, bpt = ps.get(); mm_fm(pt, bpt, wt, bw, 256 + hc * 128, 128, tb)
                            P.op("act", lambda e: e.activation(out=szT[:, hc, tb * 512:(tb + 1) * 512], in_=pt[:], func=AF.Silu), reads=[bpt], writes=[b_k])
                    P.barrier()
                    tmpc.close()
                    pz_ = PR([0, 1, 2]); pc_ = PR([3, 4]); pk_ = PR([5, 6]); pa_ = PR([7])
                    e1_r = Ring(nc, ph, "sbe1", [128, 512], F32, 2); sp_r = Ring(nc, ph, "sbsp", [128, 512], F32, 3)
                    spb_r = Ring(nc, ph, "sbspb", [128, 512], F32, 3); t1_r = Ring(nc, ph, "sbt1", [128, 512], F32, 3)
                    t2_r = Ring(nc, ph, "sbt2", [128, 512], F32, 3); w_r = Ring(nc, ph, "sbw", [128, 512], BF16, 3)
                    spp_r = Ring(nc, ph, "sbspp", [128, 512], F32, 2)
                    MBIG = 200.0
                    mbias = sbt(ph, "sbmbias", [128, 2], F32)
                    P.op("pool", lambda e: e.memset(mbias[:], -MBIG), writes=[b_k])
                    R_r = Ring(nc, ph, "sbR", [128, 512], F32, 3)
                    rstate = {}
                    sz_r = Ring(nc, ph, "sbsz", [128, 512], F32, 1); o_r = Ring(nc, ph, "sbo", [128, 512], BF16, 2)
                    iters = []

                    qz_r = Ring(nc, ph, "sbqz", [128, 2, 512], BF16, 2)
                    for qi in range(2):
                        P.op("pool", lambda e: e.memset(qz_r.t[qi][:], 0.0), writes=[qz_r.b[qi]])

                    def mk_iter(tb, hc, hh, i, pacc, bacc, qz, bqz):
                        t0 = tb * 512
                        h = hc * 2 + hh
                        plo, phi = hh * 64, hh * 64 + 64
                        imax = 4 * tb + 3
                        d = i - 4 * tb
                        first = (i == imax)
                        pz, bpz = pz_.get(); e1, be1 = e1_r.get(); sp, bsp = sp_r.get(); spb, bspb = spb_r.get()
                        pcm, bpcm = pc_.get(); pcl, bpcl = pk_.get(); t1, bt1 = t1_r.get(); t2, bt2 = t2_r.get()
                        w, bw_ = w_r.get()
                        wm, bwm = w, bw_
                        spp = bspp = None
                        if d >= 0:
                            spp, bspp = spp_r.get()
                        Rin = None if first else rstate["R"]
                        Rout = None
                        if i > 0:
                            Rout = R_r.get()
                            rstate["R"] = Rout

                        def stA0():
                            P.op("pe", lambda e: e.matmul(pz[:], lhsT=kT[:, hc, i * 128:(i + 1) * 128], rhs=qz[:, hh, :],
                                                          start=True, stop=True), reads=[bqz], writes=[bpz])

                        def stA():
                            P.op("act", lambda e: e.activation(out=e1[:], in_=pz[:], func=AF.Exp, scale=SCALE), reads=[bpz], writes=[be1])
                            P.op("act", lambda e: e.activation(out=sp[:].bitcast(F32R), in_=e1[:], func=AF.Ln, bias=1.0), reads=[be1], writes=[bsp])
                            if d >= 0:
                                P.op("pool", lambda e: TT(e, spb[:].bitcast(F32R), sp[:], sbm[:, d, :], ALU.mult), reads=[bsp], writes=[bspb])
                                P.op("dve", lambda e: e.scalar_tensor_tensor(out=spp[:], in0=sbm[:, d, :], scalar=-MBIG, in1=sp[:], op0=ALU.mult, op1=ALU.add),
                                     reads=[bsp], writes=[bspp])

                        def stB0():
                            sp1, bsp1 = (spp, bspp) if d >= 0 else (sp, bsp)
                            P.op("dve", lambda e: e.scalar_tensor_tensor(out=t1[:], in0=pz[:], scalar=SCALE, in1=sp1[:], op0=ALU.mult, op1=ALU.subtract),
                                 reads=[bpz, bsp1], writes=[bt1])
                            rsp, brsp = (spb, bspb) if d >= 0 else (sp, bsp)
                            P.op("pe", lambda e: e.matmul(pcm[:], lhsT=ustrf[:].bitcast(F32R), rhs=rsp[:].bitcast(F32R), start=True, stop=True), reads=[brsp], writes=[bpcm])
                            if i > 0:
                                P.op("pe", lambda e: e.matmul(pcl[:], lhsT=onesf[:].bitcast(F32R), rhs=rsp[:].bitcast(F32R), start=True, stop=True), reads=[brsp], writes=[bpcl])

                        def stB():
                            P.op("dve", lambda e: TT(e, t2[:], t1[:], pcm[:], ALU.subtract), reads=[bt1, bpcm], writes=[bt2])
                            if Rout is not None:
                                if Rin is None:
                                    P.op("dve", lambda e: e.tensor_copy(out=Rout[0][:], in_=pcl[:]), reads=[bpcl], writes=[Rout[1]])
                                else:
                                    P.op("dve", lambda e: TT(e, Rout[0][:], Rin[0][:], pcl[:], ALU.add), reads=[bpcl, Rin[1]], writes=[Rout[1]])

                        def stB2():
                            if Rin is not None:
                                P.op("pool", lambda e: TT(e, t2[:], t2[:], Rin[0][:], ALU.subtract), reads=[bt2, Rin[1]], writes=[bt2])

                        def stC0():
                            if d >= 0:
                                P.op("act", lambda e: e.activation(out=w[:], in_=t2[:], func=AF.Exp, bias=mbias[:, 0:1]), reads=[bt2], writes=[bw_])
                            else:
                                P.op("act", lambda e: e.activation(out=w[:], in_=t2[:], func=AF.Exp), reads=[bt2], writes=[bw_])

                        def stC():
                            P.op("pe", lambda e: e.matmul(pacc[:], lhsT=vz[:, i, h, :], rhs=wm[:],
                                                          start=(first and hh == 0), stop=(i == 0 and hh == 1), skip_group_check=True), reads=[bwm], writes=[bacc])
                            if i == 0 and hh == 1:
                                o, bo = o_r.get()
                                P.op("dve", lambda e: TT(e, o[:], pacc[:], szT[:, hc, t0:t0 + 512], ALU.mult), reads=[bacc], writes=[bo])
                                P.dma(oT_d[2, hc * 128:(hc + 1) * 128, t0:t0 + 512], o[:], reads=[bo])
                        return [stA0, stA, stB0, stB, stB2, stC0, stC]

                    def mk_qz(tb, hc, qz, bqz):
                        def s():
                            t0 = tb * 512
                            P.op("pool", lambda e: e.tensor_copy(out=qz[0:64, 0, :], in_=qT[0:64, hc, t0:t0 + 512]), writes=[bqz])
                            P.op("pool", lambda e: e.tensor_copy(out=qz[64:128, 1, :], in_=qT[64:128, hc, t0:t0 + 512]), writes=[bqz])
                        return [s, None, None, None, None, None, None]

                    for tb in range(NB):
                        for hc in range(2):
                            pacc, bacc = pa_.get()
                            qz, bqz = qz_r.get()
                            iters.append(mk_qz(tb, hc, qz, bqz))
                            for hh in range(2):
                                for i in range(4 * tb + 3, -1, -1):
                                    iters.append(mk_iter(tb, hc, hh, i, pacc, bacc, qz, bqz))
                    run_pipeline(iters, 7)
                    P.barrier()

            if "nsa" in phases:
                with ExitStack() as ph:
                    TT = lambda e, o, a, b, op: e.tensor_tensor(out=o, in0=a, in1=b, op=op)
                    NCH = (NCMP + 127) // 128
                    b_k = Buf("nsaconst")
                    kcmpT = sbt(ph, "kcmpT", [128, 2, 256], BF16)
                    vcX = sbt(ph, "vcX", [128, 2, 2, 130], BF16)
                    tmpc = ExitStack()
                    g1 = sbt(tmpc, "g1", [128, 128], F32)
                    P.dma(g1[:], qkg1[l:l + 1, :].partition_broadcast(128), writes=[b_k])
                    wbf = Ring(nc, tmpc, "wbf", [128, NKC, 512], BF16, 1)
                    wst = Ring(nc, tmpc, "wst", [128, NKC, 512], F32, 1)
                    ovs = sbt(tmpc, "ovs", [128, 2, 66], F32)
                    P.dma(ovs[:], c_ovl, writes=[b_k])
                    for kh in range(2):
                        P.op("dve", lambda e: e.tensor_copy(out=vcX[:, :, kh, 64:130], in_=ovs[:]), reads=[b_k], writes=[b_k])
                    pet = sbt(tmpc, "pet", [128, 2, 32], F32)
                    P.dma(pet[:], pe_t[l].rearrange("j p l -> p j l"), writes=[b_k])
                    w1b = sbt(tmpc, "w1b", [128, 2, 32, 128], BF16)
                    for j in range(2):
                        for hf in range(2):
                            cast_dma(w1b[:, j, hf * 16:(hf + 1) * 16, :].rearrange("p a b -> p (a b)"), w1bd[l, j][:, hf * 2048:(hf + 1) * 2048], b_k)
                    w2s = sbt(tmpc, "w2s", [128, 2, 128], F32); w2b = sbt(tmpc, "w2b", [128, 2, 128], BF16)
                    P.dma(w2s[:], w2bd[l].rearrange("j p c -> p j c"), writes=[b_k])
                    P.op("dve", lambda e: e.tensor_copy(out=w2b[:], in_=w2s[:]), reads=[b_k], writes=[b_k])
                    kAB = sbt(tmpc, "kAB", [128, 2, 2, S], BF16)
                    hidT = sbt(tmpc, "hidT", [128, 2, 256], BF16)
                    P.op("pool", lambda e: e.memset(hidT[:], 0.0), writes=[b_k])
                    wt, bw = loadw(wst, wbf, w_in[l], [(C_NKV, 256)])
                    for tb in range(NB):
                        for j in range(2):
                            pt, bpt = ps.get(); mm_fm(pt, bpt, wt, bw, j * 128, 128, tb)
                            for ab in range(2):
                                P.op("dve", lambda e: TT(e, kAB[:, j, ab, tb * 512:(tb + 1) * 512].rearrange("p (a b) -> p a b", b=16),
                                                         pt[:].rearrange("p (a b) -> p a b", b=16),
                                                         pet[:, j, ab * 16:(ab + 1) * 16].unsqueeze(1).to_broadcast([128, 32, 16]), ALU.add),
                                     reads=[bpt, b_k], writes=[b_k])
                    sq_r = Ring(nc, tmpc, "csq", [128, 128], F32, 2); st_r = Ring(nc, tmpc, "cst", [128, 8], F32, 2)
                    xn_r = Ring(nc, tmpc, "cxn", [128, 128], F32, 2); ra_r = Ring(nc, tmpc, "cra", [128, 2, 32], F32, 4)
                    rk_r = Ring(nc, tmpc, "crk", [128, 2, 2, 32], BF16, 2); dup_r = Ring(nc, tmpc, "cdup", [128, 2, 2, 64], BF16, 2)
                    for j in range(2):
                        phd, bphd = ps.get()
                        for ll in range(32):
                            ab = ll // 16
                            rhs = kAB[:, j, ab, :].rearrange("p (a b) -> p a b", b=16)[:, ab:ab + NCMP, ll % 16]
                            P.op("pe", lambda e: e.matmul(phd[:, 0:NCMP], lhsT=w1b[:, j, ll, :], rhs=rhs, start=(ll == 0), stop=(ll == 31)),
                                 reads=[b_k], writes=[bphd])
                        P.op("act", lambda e: e.activation(out=hidT[:, j, 0:NCMP], in_=phd[:, 0:NCMP], func=AF.Silu), reads=[bphd, b_k], writes=[b_k])
                        for nch in range(NCH):
                            po, bpo = ps.get()
                            P.op("pe", lambda e: e.matmul(po[:, 0:128], lhsT=hidT[:, j, nch * 128:(nch + 1) * 128], rhs=w2b[:, j, :], start=True, stop=True),
                                 reads=[b_k], writes=[bpo])
                            if j == 1:
                                P.op("act", lambda e: e.activation(out=vcX[:, nch, :, 0:64], in_=po[:, 0:128].rearrange("p (k d) -> p k d", k=2), func=AF.Copy),
                                     reads=[bpo, b_k], writes=[b_k])
                                continue
                            sq, bsq = sq_r.get(); st, bst = st_r.get()
                            P.op("act", lambda e: e.activation(out=sq[:], in_=po[:, 0:128], func=AF.Square), reads=[bpo], writes=[bsq])
                            P.op("dve", lambda e: e.reduce_sum(out=st[:, 0:2], in_=sq[:].rearrange("p (k d) -> p k d", k=2), axis=AX.X), reads=[bsq], writes=[bst])
                            P.op("act", lambda e: e.activation(out=st[:, 2:4], in_=st[:, 0:2], func=AF.Sqrt, bias=EPS, scale=1.0 / 64), reads=[bst], writes=[bst])
                            P.op("dve", lambda e: e.reciprocal(out=st[:, 4:6], in_=st[:, 2:4]), reads=[bst], writes=[bst])
                            xn, bxn = xn_r.get()
                            P.op("dve", lambda e: TT(e, xn[:].rearrange("p (k d) -> p k d", k=2), po[:, 0:128].rearrange("p (k d) -> p k d", k=2),
                                                     st[:, 4:6].unsqueeze(2).to_broadcast([128, 2, 64]), ALU.mult), reads=[bpo, bst], writes=[bxn])
                            P.op("dve", lambda e: TT(e, xn[:], xn[:], g1[:], ALU.mult), reads=[bxn, b_k], writes=[bxn])
                            x4 = xn[:].rearrange("p (k h d) -> p k h d", k=2, h=2)
                            cb = cosC[:, nch, :].unsqueeze(1).to_broadcast([128, 2, 32]); sb_ = sinC[:, nch, :].unsqueeze(1).to_broadcast([128, 2, 32])
                            rk, brk = rk_r.get()
                            a_, ba = ra_r.get(); b_, bb = ra_r.get()
                            P.op("dve", lambda e: TT(e, a_[:], x4[:, :, 0, :], cb, ALU.mult), reads=[bxn], writes=[ba])
                            P.op("dve", lambda e: TT(e, b_[:], x4[:, :, 1, :], sb_, ALU.mult), reads=[bxn], writes=[bb])
                            P.op("dve", lambda e: TT(e, rk[:, :, 0, :], a_[:], b_[:], ALU.subtract), reads=[ba, bb], writes=[brk])
                            a_, ba = ra_r.get(); b_, bb = ra_r.get()
                            P.op("dve", lambda e: TT(e, a_[:], x4[:, :, 1, :], cb, ALU.mult), reads=[bxn], writes=[ba])
                            P.op("dve", lambda e: TT(e, b_[:], x4[:, :, 0, :], sb_, ALU.mult), reads=[bxn], writes=[bb])
                            P.op("dve", lambda e: TT(e, rk[:, :, 1, :], a_[:], b_[:], ALU.add), reads=[ba, bb], writes=[brk])
                            dup, bdup = dup_r.get()
                            P.op("dve", lambda e: e.tensor_copy(out=dup[:], in_=rk[:].rearrange("p k h d -> p k (h d)").unsqueeze(2).to_broadcast([128, 2, 2, 64])),
                                 reads=[brk], writes=[bdup])
                            ptr, bptr = ps.get()
                            ptb = ptr[:].bitcast(BF16)
                            for kh in range(2):
                                P.op("pe", lambda e: e.transpose(out=ptb[:, kh * 128:(kh + 1) * 128], in_=dup[:, kh].rearrange("p a d -> p (a d)"), identity=identb[:]),
                                     reads=[bdup], writes=[bptr])
                            P.op("act", lambda e: e.activation(out=kcmpT[:, :, nch * 128:(nch + 1) * 128], in_=ptb[:, 0:256].rearrange("p (k n) -> p k n", k=2), func=AF.Copy),
                                 reads=[bptr], writes=[b_k])
                    if NCH == 1:
                        P.op("pool", lambda e: e.memset(kcmpT[:, :, 128:256], 0.0), reads=[b_k], writes=[b_k])
                        P.op("pool", lambda e: e.memset(vcX[:, 1, :, 0:64], 0.0), reads=[b_k], writes=[b_k])
                    P.barrier()
                    tmpc.close()
                    qT = sbt(ph, "nqT", [128, 2, S], BF16); ksT = sbt(ph, "nksT", [128, 2, S], BF16); kwT = sbt(ph, "nkwT", [128, 2, S], BF16)
                    vS = sbt(ph, "nvS", [128, NT, 2, 66], BF16); vW = sbt(ph, "nvW", [128, NT, 2, 66], BF16)
                    gS = sbt(ph, "ngS", [128, NT, 12], F32)
                    wz = sbt(ph, "nwz", [128, NKC, 256], BF16)
                    P.op("pool", lambda e: e.memset(vS[:, :, :, 64:66], 1.0), writes=[b_k])
                    P.op("pool", lambda e: e.memset(vW[:, :, :, 64:66], 1.0), writes=[b_k])
                    tmpc = ExitStack()
                    gq = sbt(tmpc, "gq", [128, 512], F32)
                    P.dma(gq[:], qkg_row[l:l + 1, :].partition_broadcast(128), writes=[b_k])
                    wbf = Ring(nc, tmpc, "wbf", [128, NKC, 512], BF16, 1)
                    wst = Ring(nc, tmpc, "wst", [128, NKC, 512], F32, 1)
                    wbf2 = Ring(nc, tmpc, "wbf2", [128, NKC, 512], BF16, 1)
                    wt, bw = loadw(wst, wbf, w_in[l], [(C_NZ, 256)])
                    P.op("pool", lambda e: e.tensor_copy(out=wz[:], in_=wt[:, :, 0:256]), reads=[bw], writes=[b_k])
                    wtA, bwA = loadw(wst, wbf, w_in[l], [(C_NQ, 256), (C_NKV + 256, 128), (C_NKV + 512, 128)])
                    wtB, bwB = loadw(wst, wbf2, w_in[l], [(C_NKV + 384, 128), (C_NKV + 640, 128), (C_NG, 12)])
                    sq_r = Ring(nc, tmpc, "nsq", [128, 512], F32, 2); st_r = Ring(nc, tmpc, "nst", [128, 24], F32, 2)
                    xn_r = Ring(nc, tmpc, "nxn", [128, 512], F32, 2); ra_r = Ring(nc, tmpc, "nra", [128, 8, 32], F32, 8)
                    rq_r = Ring(nc, tmpc, "nrq", [128, 8, 2, 32], BF16, 4); dup_r = Ring(nc, tmpc, "ndup", [128, 4, 2, 64], BF16, 3)
                    pp1 = PR([0, 1, 2, 3]); pp2 = PR([4, 5]); ppt = PR([6, 7])
                    st_r = Ring(nc, tmpc, "nst3", [128, 24], F32, 3)
                    iters = []

                    def mk_pre(tt):
                        p1, bp1 = pp1.get(); p2, bp2 = pp2.get(); ptr, bptr = ppt.get()
                        sq, bsq = sq_r.get(); st, bst = st_r.get(); xn, bxn = xn_r.get()
                        rq, brq = rq_r.get(); dup, bdup = dup_r.get()
                        tsl = slice(tt * 128, (tt + 1) * 128)

                        def s0():
                            mm_tm(p1, bp1, wtA, bwA, 0, 512, tt)
                            mm_tm(p2, bp2, wtB, bwB, 0, 268, tt)

                        def s1():
                            P.op("act", lambda e: e.activation(out=sq[:], in_=p1[:], func=AF.Square), reads=[bp1], writes=[bsq])
                            P.op("act", lambda e: e.activation(out=vS[:, tt, :, 0:64], in_=p2[:, 0:128].rearrange("p (k d) -> p k d", k=2), func=AF.Copy), reads=[bp2], writes=[b_k])
                            P.op("dve", lambda e: e.tensor_copy(out=vW[:, tt, :, 0:64], in_=p2[:, 128:256].rearrange("p (k d) -> p k d", k=2)), reads=[bp2], writes=[b_k])
                            P.op("act", lambda e: e.activation(out=gS[:, tt, :], in_=p2[:, 256:268], func=AF.Copy), reads=[bp2], writes=[b_k])

                        def s2():
                            P.op("dve", lambda e: e.reduce_sum(out=st[:, 0:8], in_=sq[:].rearrange("p (k d) -> p k d", k=8), axis=AX.X), reads=[bsq], writes=[bst])
                            P.op("act", lambda e: e.activation(out=st[:, 8:16], in_=st[:, 0:8], func=AF.Sqrt, bias=EPS, scale=1.0 / 64), reads=[bst], writes=[bst])
                            P.op("dve", lambda e: e.reciprocal(out=st[:, 16:24], in_=st[:, 8:16]), reads=[bst], writes=[bst])

                        def s3():
                            P.op("dve", lambda e: TT(e, xn[:].rearrange("p (k d) -> p k d", k=8), p1[:].rearrange("p (k d) -> p k d", k=8),
                                                     st[:, 16:24].unsqueeze(2).to_broadcast([128, 8, 64]), ALU.mult), reads=[bp1, bst], writes=[bxn])
                            P.op("pool", lambda e: TT(e, xn[:], xn[:], gq[:], ALU.mult), reads=[bxn, b_k], writes=[bxn])

                        ra4 = [ra_r.get() for _ in range(4)]

                        def s4():
                            x4 = xn[:].rearrange("p (k h d) -> p k h d", k=8, h=2)
                            cb = cosT[:, tt, :].unsqueeze(1).to_broadcast([128, 8, 32]); sb_ = sinT[:, tt, :].unsqueeze(1).to_broadcast([128, 8, 32])
                            P.op("dve", lambda e: TT(e, ra4[0][0][:], x4[:, :, 0, :], cb, ALU.mult), reads=[bxn], writes=[ra4[0][1]])
                            P.op("pool", lambda e: TT(e, ra4[1][0][:], x4[:, :, 1, :], sb_, ALU.mult), reads=[bxn], writes=[ra4[1][1]])
                            P.op("dve", lambda e: TT(e, ra4[2][0][:], x4[:, :, 1, :], cb, ALU.mult), reads=[bxn], writes=[ra4[2][1]])
                            P.op("pool", lambda e: TT(e, ra4[3][0][:], x4[:, :, 0, :], sb_, ALU.mult), reads=[bxn], writes=[ra4[3][1]])

                        def s4b():
                            P.op("dve", lambda e: TT(e, rq[:, :, 0, :], ra4[0][0][:], ra4[1][0][:], ALU.subtract), reads=[ra4[0][1], ra4[1][1]], writes=[brq])
                            P.op("dve", lambda e: TT(e, rq[:, :, 1, :], ra4[2][0][:], ra4[3][0][:], ALU.add), reads=[ra4[2][1], ra4[3][1]], writes=[brq])

                        def s4c():
                            rqf = rq[:].rearrange("p k h d -> p k (h d)")
                            P.op("pool", lambda e: e.tensor_copy(out=dup[:], in_=rqf[:, 4:8, :].unsqueeze(2).to_broadcast([128, 4, 2, 64])), reads=[brq], writes=[bdup])

                        def s5():
                            ptb = ptr[:].bitcast(BF16)
                            rq2 = rq[:].rearrange("p k h d -> p (k h d)")
                            for c in range(2):
                                P.op("pe", lambda e: e.transpose(out=ptb[:, c * 128:(c + 1) * 128], in_=rq2[:, c * 128:(c + 1) * 128], identity=identb[:]),
                                     reads=[brq], writes=[bptr])
                            for c in range(4):
                                P.op("pe", lambda e: e.transpose(out=ptb[:, (2 + c) * 128:(3 + c) * 128], in_=dup[:, c].rearrange("p a d -> p (a d)"), identity=identb[:]),
                                     reads=[bdup], writes=[bptr])

                        def s6():
                            ptb = ptr[:].bitcast(BF16)
                            P.op("act", lambda e: e.activation(out=qT[:, :, tsl], in_=ptb[:, 0:256].rearrange("p (k n) -> p k n", k=2), func=AF.Copy), reads=[bptr], writes=[b_k])
                            P.op("act", lambda e: e.activation(out=ksT[:, :, tsl], in_=ptb[:, 256:512].rearrange("p (k n) -> p k n", k=2), func=AF.Copy), reads=[bptr], writes=[b_k])
                            P.op("dve", lambda e: e.tensor_copy(out=kwT[:, :, tsl], in_=ptb[:, 512:768].rearrange("p (k n) -> p k n", k=2)), reads=[bptr], writes=[b_k])
                        return [s0, s1, s2, s3, s4, s4b, s4c, s5, s6]

                    for tt in range(NT):
                        iters.append(mk_pre(tt))
                    run_pipeline(iters, 9)
                    P.op("act", lambda e: e.activation(out=gS[:], in_=gS[:], func=AF.Sigmoid), reads=[b_k], writes=[b_k])
                    P.barrier()
                    tmpc.close()
                    slm = sbt(ph, "nslm", [128, 4, 512], BF16); band = sbt(ph, "nband", [128, 8, 512], BF16)
                    eexp = sbt(ph, "neexp", [128, S], BF16); sfv = sbt(ph, "nsfv", [128, NT, 64], BF16)
                    for dst, src in ((slm, c_slm), (band, c_band), (eexp, c_eexp), (sfv, c_selfv)):
                        P.dma(dst[:], src, writes=[b_k])
                    P.barrier()
                    pacc = [(pst[k], psb[k]) for k in range(4)]
                    psc = PR([4, 5]); pmx = PR([6, 7])
                    e_r = Ring(nc, ph, "ne", [128, 512], BF16, 3); em_r = Ring(nc, ph, "nem", [128, 512], BF16, 3)
                    mm_r = Ring(nc, ph, "nmm", [128, 512], BF16, 3)
                    oacc_r = Ring(nc, ph, "noacc", [128, 4, 256], F32, 2)
                    cur = {}
                    imp = sbt(ph, "nimp", [128, 4, 2, 64], F32); b_imp = Buf("imp")
                    impf8 = sbt(ph, "nimpf8", [128, 8, 64], F32); impt8 = sbt(ph, "nimpt8", [128, 8, 64], F32)
                    m8a = sbt(ph, "nm8a", [128, 8, 8], F32); m8b = sbt(ph, "nm8b", [128, 8, 8], F32); selb8 = sbt(ph, "nselb8", [128, 8, 64], BF16)
                    selT = sbt(ph, "nselT", [128, 2, 512], BF16); b_selT = Buf("selT")
                    P.op("pool", lambda e: e.memset(selT[:], 0.0), writes=[b_selT])
                    qz_r = Ring(nc, ph, "nqz", [128, 4, 512], BF16, 1)
                    for qi in range(1):
                        P.op("pool", lambda e: e.memset(qz_r.t[qi][:], 0.0), writes=[qz_r.b[qi]])
                    sc_r = Ring(nc, ph, "nsc8", [128, 12], F32, 4)
                    sz_r = Ring(nc, ph, "nsz", [128, 512], F32, 1); ob_r = Ring(nc, ph, "nob", [128, 4, 256], BF16, 1)
                    oTb_r = Ring(nc, ph, "noTb", [128, 2, 512], BF16, 1)

                    ftmp_r = Ring(nc, ph, "nftmp", [128, 4, 64], F32, 1)

                    def finalize(view, bank_b, dencol, ts0, nts, tb, hq, gidx, first, imp_g=None):
                        sc8, bsc = sc_r.get()
                        T0 = 4 * tb + ts0
                        P.op("dve", lambda e: e.tensor_scalar(out=sc8[:, 0:nts].unsqueeze(2), in0=view[:, :, dencol:dencol + 1], scalar1=1e-30, scalar2=None, op0=ALU.max),
                             reads=[bank_b], writes=[bsc])
                        P.op("dve", lambda e: e.reciprocal(out=sc8[:, 4:4 + nts], in_=sc8[:, 0:nts]), reads=[bsc], writes=[bsc])
                        P.op("dve", lambda e: TT(e, sc8[:, 8:8 + nts], sc8[:, 4:4 + nts], gS[:, T0:T0 + nts, gidx], ALU.mult), reads=[bsc], writes=[bsc])
                        oacc, b_oa = cur["oacc"]
                        osl = oacc[:, ts0:ts0 + nts, hq * 64:(hq + 1) * 64]
                        sgb = sc8[:, 8:8 + nts].unsqueeze(2).to_broadcast([128, nts, 64])
                        if first:
                            P.op("dve", lambda e: TT(e, osl, view[:, :, 0:64], sgb, ALU.mult), reads=[bank_b, bsc], writes=[b_oa])
                        else:
                            ft, bft = ftmp_r.get()
                            P.op("dve", lambda e: TT(e, ft[:, 0:nts, :], view[:, :, 0:64], sgb, ALU.mult), reads=[bank_b, bsc], writes=[bft])
                            P.op("pool", lambda e: TT(e, osl, osl, ft[:, 0:nts, :], ALU.add), reads=[bft, b_oa], writes=[b_oa])
                        if imp_g is not None:
                            kh_, g_ = imp_g
                            isl = imp[:, ts0:ts0 + nts, kh_, :]
                            rb = sc8[:, 4:4 + nts].unsqueeze(2).to_broadcast([128, nts, 64])
                            if g_ == 0:
                                P.op("dve", lambda e: TT(e, isl, view[:, :, 64:128], rb, ALU.mult), reads=[bank_b, bsc], writes=[b_imp])
                            else:
                                ft, bft = ftmp_r.get()
                                P.op("dve", lambda e: TT(e, ft[:, 0:nts, :], view[:, :, 64:128], rb, ALU.mult), reads=[bank_b, bsc], writes=[bft])
                                P.op("dve", lambda e: TT(e, isl, isl, ft[:, 0:nts, :], ALU.add), reads=[bft, b_imp], writes=[b_imp])

                    def emit_E(tb, oacc, b_oa):
                        t0 = tb * 512
                        oTb, boT = oTb_r.get()
                        ob, bob = ob_r.get()
                        for hb in range(2):
                            pz, bpz = psc.get()
                            for t2_ in range(2):
                                T = 4 * tb + 2 * hb + t2_
                                for kc in range(NKC):
                                    P.op("pe", lambda e: e.matmul(pz[:, t2_ * 256:(t2_ + 1) * 256], lhsT=hT[:, kc, T * 128:(T + 1) * 128], rhs=wz[:, kc, :],
                                                                  start=(kc == 0 and t2_ == 0), stop=(kc == NKC - 1), skip_group_check=True), writes=[bpz])
                            sz, bsz = sz_r.get()
                            P.op("act", lambda e: e.activation(out=sz[:], in_=pz[:], func=AF.Silu), reads=[bpz], writes=[bsz])
                            P.op("pool" if hb == 0 else "dve", lambda e: TT(e, ob[:, 2 * hb:2 * hb + 2, :], oacc[:, 2 * hb:2 * hb + 2, :], sz[:].rearrange("p (a c) -> p a c", a=2), ALU.mult),
                                 reads=[b_oa, bsz], writes=[bob])
                        ptr, bptr = pmx.get()
                        ptb = ptr[:].bitcast(BF16)
                        for c in range(2):
                            for ts in range(4):
                                P.op("pe", lambda e: e.transpose(out=ptb[:, (c * 4 + ts) * 128:(c * 4 + ts + 1) * 128], in_=ob[:, ts, c * 128:(c + 1) * 128], identity=identb[:]),
                                     reads=[bob], writes=[bptr])
                        P.op("act", lambda e: e.activation(out=oTb[:].rearrange("p c n -> p (c n)"), in_=ptb[:, :], func=AF.Copy), reads=[bptr], writes=[boT])
                        for c in range(2):
                            P.dma(oT_d[0, c * 128:(c + 1) * 128, t0:t0 + 512], oTb[:, c, :], reads=[boT])

                    pending_E = None
                    for tb in range(NB if "nsa_pre" not in phases else 0):
                        t0 = tb * 512
                        cur["oacc"] = oacc_r.get()
                        qz, bqz = qz_r.get()
                        for hq in range(4):
                            kh, g = hq // 2, hq % 2
                            P.op("pool" if hq % 2 else "act",
                                 (lambda e: e.tensor_copy(out=qz[g * 64:(g + 1) * 64, hq, :], in_=qT[g * 64:(g + 1) * 64, kh, t0:t0 + 512])) if hq % 2 else
                                 (lambda e: e.activation(out=qz[g * 64:(g + 1) * 64, hq, :], in_=qT[g * 64:(g + 1) * 64, kh, t0:t0 + 512], func=AF.Copy)),
                                 writes=[bqz])
                        nchs = [n_ for n_ in range(NCH) if 16 * (n_ * 128) + 31 <= t0 + 511]
                        iters = []

                        def mk_cmp(hq, nch):
                            kh, g = hq // 2, hq % 2
                            plo, phi = g * 64, g * 64 + 64
                            pair = (pacc[0], pacc[1]) if hq % 2 == 0 else (pacc[2], pacc[3])
                            sc, bsc_ = psc.get(); e_, be = e_r.get(); em, bem = em_r.get()

                            def s0():
                                P.op("pe", lambda e: e.matmul(sc[:], lhsT=kcmpT[:, kh, nch * 128:(nch + 1) * 128], rhs=qz[:, hq, :],
                                                              start=True, stop=True), reads=[bqz], writes=[bsc_])

                            def s1():
                                P.op("act", lambda e: e.activation(out=e_[:], in_=sc[:], func=AF.Exp, scale=SCALE), reads=[bsc_], writes=[be])

                            def s2():
                                P.op("pool", lambda e: e.affine_select(out=em[:], in_=e_[:], pattern=[[1, 512]], compare_op=ALU.is_ge, fill=0.0,
                                                                       base=t0 - 16 * 128 * nch - 31, channel_multiplier=-16), reads=[be], writes=[bem])

                            def s3():
                                for ts in range(4):
                                    bank, bbank = pair[ts // 2]
                                    region = bank[:, (ts % 2) * 130:(ts % 2) * 130 + 130]
                                    P.op("pe", lambda e: e.matmul(region, lhsT=em[:, ts * 128:(ts + 1) * 128], rhs=vcX[:, nch, kh, :],
                                                                  start=(nch == nchs[0] and ts % 2 == 0), stop=(nch == nchs[-1]), skip_group_check=True),
                                         reads=[bem], writes=[bbank])
                                if nch == nchs[-1]:
                                    for hb in range(2):
                                        bank, bbank = pair[hb]
                                        finalize(bank[:, 0:260].rearrange("p (a w) -> p a w", a=2), bbank, 128, 2 * hb, 2, tb, hq, hq, True, imp_g=(kh, g))
                            return [s0, s1, s2, s3]

                        for hq in range(4):
                            for nch in nchs:
                                iters.append(mk_cmp(hq, nch))
                        run_pipeline(iters, 4)
                        if pending_E is not None:
                            emit_E(*pending_E)
                            pending_E = None
                        bif = Buf("impf")
                        P.op("dve", lambda e: TT(e, impf8[:].rearrange("p (k t) j -> p k t j", k=2), imp[:].rearrange("p t k j -> p k t j"),
                                                 sfv[:, 4 * tb:4 * tb + 4, :].unsqueeze(1).to_broadcast([128, 2, 4, 64]), ALU.add), reads=[b_imp], writes=[bif])
                        gb = [Buf("g%d" % q) for q in range(8)]
                        for q in range(8):
                            P.op("dve", lambda e: e.max(out=m8a[:, q, :], in_=impf8[:, q, :]), reads=[bif], writes=[gb[q]])
                        for q in range(8):
                            P.op("dve", lambda e: e.match_replace(out=impt8[:, q, :], in_to_replace=m8a[:, q, :], in_values=impf8[:, q, :], imm_value=-3.0e4),
                                 reads=[bif, gb[q]], writes=[gb[q]])
                        for q in range(8):
                            P.op("dve", lambda e: e.max(out=m8b[:, q, :], in_=impt8[:, q, :]), reads=[gb[q]], writes=[gb[q]])
                        bsl = Buf("selb")
                        P.op("dve", lambda e: TT(e, selb8[:], impf8[:], m8b[:, :, 7:8].to_broadcast([128, 8, 64]), ALU.is_ge), reads=[bif] + gb, writes=[bsl])
                        ptr, bptr = pmx.get()
                        ptb = ptr[:].bitcast(BF16)
                        for q in range(8):
                            P.op("pe", lambda e: e.transpose(out=ptb[0:64, q * 128:(q + 1) * 128], in_=selb8[:, q, :], identity=identb[:]), reads=[bsl], writes=[bptr])
                        P.op("act", lambda e: e.activation(out=selT[0:64].rearrange("j k t -> j (k t)"), in_=ptb[0:64, :], func=AF.Copy), reads=[bptr], writes=[b_selT])
                        iters = []
                        started = set()
                        state = {"mm": None}

                        def mk_att(kind, i, kh, g, dd):
                            hq = 2 * kh + g
                            plo, phi = g * 64, g * 64 + 64
                            d = i - 4 * tb
                            sc, bsc_ = psc.get(); e_, be = e_r.get(); em, bem = em_r.get()
                            bank, bbank = pacc[hq]
                            if kind == "sel":
                                kmat, vmat, gbase = ksT, vS, 4
                                ts_list = list(range(max(d, 0), 4))
                                if g == 0:
                                    mx, bmx = pmx.get(); mm, bmm = mm_r.get()
                                    state["mm"] = (mm, bmm)
                                else:
                                    mx = bmx = None
                                    mm, bmm = state["mm"]
                                last_i = 4 * tb + 3
                            else:
                                kmat, vmat, gbase = kwT, vW, 8
                                ts_list = list(range(max(dd, 0), min(dd + 4, 3) + 1))
                                mx = bmx = mm = bmm = None
                                last_i = 4 * tb + 3
                            key = (kind, hq)
                            flags = []
                            for ts in ts_list:
                                flags.append(key not in started)
                                started.add(key)

                            def s0():
                                if mx is not None:
                                    P.op("pe", lambda e: e.matmul(mx[:], lhsT=eexp[:, i * 128:(i + 1) * 128], rhs=selT[:, kh, :], start=True, stop=True),
                                         reads=[b_selT], writes=[bmx])
                                P.op("pe", lambda e: e.matmul(sc[:], lhsT=kmat[:, kh, i * 128:(i + 1) * 128], rhs=qz[:, hq, :],
                                                              start=True, stop=True), reads=[bqz], writes=[bsc_])

                            def s1():
                                if mx is not None:
                                    if d >= 0:
                                        P.op("dve", lambda e: TT(e, mm[:], mx[:], slm[:, d, :], ALU.mult), reads=[bmx], writes=[bmm])
                                    else:
                                        P.op("dve", lambda e: e.tensor_copy(out=mm[:], in_=mx[:]), reads=[bmx], writes=[bmm])
                                P.op("act", lambda e: e.activation(out=e_[:], in_=sc[:], func=AF.Exp, scale=SCALE), reads=[bsc_], writes=[be])

                            def s2():
                                if kind == "sel":
                                    P.op("dve" if g == 0 else "pool", lambda e: TT(e, em[:], e_[:], mm[:], ALU.mult), reads=[be, bmm], writes=[bem])
                                else:
                                    P.op("pool" if g == 0 else "dve", lambda e: TT(e, em[:], e_[:], band[:, dd + 4, :], ALU.mult), reads=[be], writes=[bem])

                            def s3():
                                for ts, fl in zip(ts_list, flags):
                                    P.op("pe", lambda e: e.matmul(bank[:, ts * 66:(ts + 1) * 66], lhsT=em[:, ts * 128:(ts + 1) * 128], rhs=vmat[:, i, kh, :],
                                                                  start=fl, stop=(i == 4 * tb + ts), skip_group_check=True),
                                         reads=[bem], writes=[bbank])
                                if i == last_i:
                                    finalize(bank[:, 0:264].rearrange("p (a w) -> p a w", a=4), bbank, 64, 0, 4, tb, hq, gbase + hq, False)
                            return [s0, s1, s2, s3]

                        for i in range(0, 4 * tb + 4):
                            for kh in range(2):
                                for g in range(2):
                                    iters.append(mk_att("sel", i, kh, g, None))
                        for dd in range(-4, 4):
                            i = 4 * tb + dd
                            if i < 0:
                                continue
                            for kh in range(2):
                                for g in range(2):
                                    iters.append(mk_att("win", i, kh, g, dd))
                        run_pipeline(iters, 4)
                        pending_E = (tb, cur["oacc"][0], cur["oacc"][1])
                    emit_E(*pending_E)
                    P.barrier()


            if "final" in phases:
                with ExitStack() as ph:
                    TT = lambda e, o, a, b, op: e.tensor_tensor(out=o, in0=a, in1=b, op=op)
                    SBW = min(S, 1024)
                    NSB = S // SBW
                    NBB = SBW // 512
                    b_k = Buf("finconst")
                    wo = sbt(ph, "fwo", [128, NKC, D], BF16)
                    wbr = sbt(ph, "fwbr", [128, 4, 2, D], BF16)
                    b_wbr = Buf("wbr"); b_wo = Buf("wo")
                    for m in range(4):
                        for hf in range(2):
                            cast_dma(wbr[:, m, :, hf * 512:(hf + 1) * 512], w_br[l, m, :, hf * 512:(hf + 1) * 512].rearrange("(k p) c -> p k c", p=128), b_wbr)
                    for hf in range(2):
                        for k2 in range(2):
                            cast_dma(wo[:, k2 * 4:(k2 + 1) * 4, hf * 512:(hf + 1) * 512],
                                     w_o[l, k2 * 512:(k2 + 1) * 512, hf * 512:(hf + 1) * 512].rearrange("(k p) c -> p k c", p=128), b_wo)
                    oTs_r = Ring(nc, ph, "foTs", [128, 4, 2, SBW], BF16, 2)
                    mixT = sbt(ph, "fmixT", [128, NKC, SBW], BF16); b_mix = Buf("mixT")
                    wbf = Ring(nc, ph, "wbf", [128, NKC, 512], BF16, 2)
                    sg_r = Ring(nc, ph, "fsg", [128, 512], F32, 3); tmp_r = Ring(nc, ph, "ftmp", [128, 512], F32, 2)
                    acc_r = Ring(nc, ph, "facc", [128, 512], F32, 2)
                    xr = Ring(nc, ph, "fx", [128, D], F32, 2); yr = Ring(nc, ph, "fy", [128, D], F32, 2)
                    pg_ = PR([0, 1, 2]); pb_ = PR([3, 4, 5]); po_ = PR([6, 7])
                    units = [(sbk, cch) for sbk in range(NSB) for cch in range(NKC)]
                    wtiles = {}
                    otiles = {}

                    def load_unit(idx):
                        sbk, cch = units[idx]
                        wtiles[idx] = loadw_cast(wbf, w_in[l], [(C_MG + m * D + cch * 128, 128) for m in range(4)])

                    def load_oT(sbk):
                        oTs, b_oT = oTs_r.get()
                        s0 = sbk * SBW
                        for m in range(4):
                            for kk in range(2):
                                P.dma(oTs[:, m, kk, :], oT_d[m, kk * 128:(kk + 1) * 128, s0:s0 + SBW], writes=[b_oT])
                        otiles[sbk] = (oTs, b_oT)

                    load_oT(0)
                    load_unit(0)
                    for idx, (sbk, cch) in enumerate(units):
                        s0 = sbk * SBW
                        if idx + 1 < len(units):
                            load_unit(idx + 1)
                        if cch == 0 and sbk + 1 < NSB:
                            load_oT(sbk + 1)
                        wt, bw = wtiles.pop(idx)
                        oTs, b_oT = otiles[sbk]
                        for tbb in range(NBB):
                            tb = sbk * NBB + tbb
                            acc, bacc = acc_r.get()
                            for m in range(4):
                                pg, bpg = pg_.get(); mm_fm(pg, bpg, wt, bw, m * 128, 128, tb)
                                sg, bsg = sg_r.get()
                                P.op("act", lambda e: e.activation(out=sg[:], in_=pg[:], func=AF.Sigmoid), reads=[bpg], writes=[bsg])
                                pb, bpb = pb_.get()
                                for kk in range(2):
                                    P.op("pe", lambda e: e.matmul(pb[:], lhsT=wbr[:, m, kk, cch * 128:(cch + 1) * 128], rhs=oTs[:, m, kk, tbb * 512:(tbb + 1) * 512],
                                                                  start=(kk == 0), stop=(kk == 1)), reads=[b_oT, b_wbr], writes=[bpb])
                                if m == 0:
                                    P.op("dve", lambda e: TT(e, acc[:], pb[:], sg[:], ALU.mult), reads=[bpb, bsg], writes=[bacc])
                                else:
                                    tmp, btmp = tmp_r.get()
                                    P.op("dve", lambda e: TT(e, tmp[:], pb[:], sg[:], ALU.mult), reads=[bpb, bsg], writes=[btmp])
                                    if m < 3:
                                        P.op("pool", lambda e: TT(e, acc[:], acc[:], tmp[:], ALU.add), reads=[bacc, btmp], writes=[bacc])
                                    else:
                                        P.op("pool", lambda e: TT(e, mixT[:, cch, tbb * 512:(tbb + 1) * 512], acc[:], tmp[:], ALU.add),
                                             reads=[bacc, btmp], writes=[b_mix])
                        if cch != NKC - 1:
                            continue
                        for tt_ in range(SBW // 128):
                            tt = s0 // 128 + tt_
                            xt, bx = xr.get()
                            P.dma(xt[:], xsrc[tt * 128:(tt + 1) * 128, :], writes=[bx])
                            yt, by = yr.get()
                            for hf in range(2):
                                po, bpo = po_.get()
                                for c2 in range(NKC):
                                    P.op("pe", lambda e: e.matmul(po[:], lhsT=mixT[:, c2, tt_ * 128:(tt_ + 1) * 128], rhs=wo[:, c2, hf * 512:(hf + 1) * 512],
                                                                  start=(c2 == 0), stop=(c2 == NKC - 1)), reads=[b_mix, b_wo], writes=[bpo])
                                P.op("dve", lambda e: TT(e, yt[:, hf * 512:(hf + 1) * 512], po[:], xt[:, hf * 512:(hf + 1) * 512], ALU.add),
                                     reads=[bpo, bx], writes=[by])
                            P.dma(xdst[tt * 128:(tt + 1) * 128, :], yt[:], reads=[by])
                    P.barrier()

            PHASES_PLACEHOLDER = None

            if debug == "oT":
                P.barrier()
                P.dma(dbg, oT_d)
                P.barrier()
                break

        P.barrier()
        print("instructions", P.nins, "waits", P.nwaits)
    return nc


def prep_inputs(inputs, S, n_layers):
    f = lambda a: np.ascontiguousarray(np.asarray(a), dtype=np.float32)
    L = n_layers
    com = {}
    com["norm_g"] = f(inputs["norm_g"])[:L]
    com["w_in"] = f(inputs["w_in"])[:L]
    com["convw"] = f(np.transpose(f(inputs["sc_conv_w"])[:L], (0, 2, 1)))
    sm = lambda a: f(np.transpose(f(a)[:L].reshape(L, 8, 128), (0, 2, 1)))
    com["s5_lr"] = sm(inputs["s5_a_re"]); com["s5_li"] = sm(inputs["s5_a_im"])
    com["s5_ldt"] = sm(np.repeat(f(inputs["s5_log_dt"])[:L, :, None], 64, axis=2))
    def expand_b(b):
        out = np.zeros((L, 128, 8, 128), np.float32)
        b = f(b)[:L]
        for j in range(8):
            for hq in range(2):
                g = 2 * j + hq
                col = 32 * (j % 4) + 16 * hq
                out[:, hq * 64:(hq + 1) * 64, j, col:col + 16] = b[:, g]
        return out
    def expand_c(c_):
        out = np.zeros((L, 128, 8, 128), np.float32)
        c_ = f(c_)[:L]
        for j in range(8):
            for hq in range(2):
                g = 2 * j + hq
                col = 32 * (j % 4) + 16 * hq
                out[:, hq * 64:(hq + 1) * 64, j, col:col + 16] = np.transpose(c_[:, g], (0, 2, 1))
        return out
    com["s5_bre"] = expand_b(inputs["s5_b_re"]); com["s5_bim"] = expand_b(inputs["s5_b_im"])
    com["s5_cre"] = expand_c(inputs["s5_c_re"]); com["s5_cim"] = expand_c(inputs["s5_c_im"])
    com["s5_d"] = f(np.transpose(f(inputs["s5_d"])[:L].reshape(L, 2, 128), (0, 2, 1)))
    com["glu_w"] = f(inputs["s5_glu_w"])[:L]
    com["glu_b"] = f(np.transpose(f(inputs["s5_glu_b"])[:L].reshape(L, 4, 128), (0, 2, 1)))
    com["w_br"] = f(inputs["w_branch"])[:L]; com["w_o"] = f(inputs["w_out"])[:L]
    g = f(inputs["nsa_qk_g"])[:L]
    com["qkg_row"] = f(np.concatenate([np.tile(g[:, 0], (1, 4)), np.tile(g[:, 2], (1, 2)), np.tile(g[:, 3], (1, 2))], axis=1))
    com["qkg1"] = f(np.tile(g[:, 1], (1, 2)))
    pe = f(inputs["nsa_cmp_pe"])[:L]
    com["pe_t"] = f(np.tile(np.transpose(pe, (0, 1, 3, 2)), (1, 1, 2, 1)))
    w1 = f(inputs["nsa_cmp_w1"])[:L].reshape(L, 2, 32, 64, 64)
    w1bd = np.zeros((L, 2, 128, 32, 128), np.float32)
    for kh in range(2):
        w1bd[:, :, kh * 64:(kh + 1) * 64, :, kh * 64:(kh + 1) * 64] = np.transpose(w1, (0, 1, 3, 2, 4))
    com["w1bd"] = w1bd.reshape(L, 2, 128, 32 * 128)
    w2 = f(inputs["nsa_cmp_w2"])[:L]
    w2bd = np.zeros((L, 2, 128, 128), np.float32)
    for kh in range(2):
        w2bd[:, :, kh * 64:(kh + 1) * 64, kh * 64:(kh + 1) * 64] = w2
    com["w2bd"] = w2bd
    com.update(host_consts(S))
    x = np.asarray(inputs["x"]); pos = np.asarray(inputs["positions"])
    maps = []
    for b in range(x.shape[0]):
        m = dict(com)
        m["x"] = np.ascontiguousarray(x[b, :S], dtype=np.float32)
        pb = np.asarray(pos[b, :S], dtype=np.int32)
        pl = np.zeros((128, S // 128 + 2), np.int32)
        pl[:, :S // 128] = pb.reshape(S // 128, 128).T
        ce = pb[31::16]
        n1 = min(128, len(ce))
        pl[:n1, S // 128] = ce[:n1]
        if len(ce) > 128:
            pl[:len(ce) - 128, S // 128 + 1] = ce[128:]
        m["pos"] = pl
        maps.append(m)
    return maps


_NC_CACHE = {}


def kernel(**inputs):
    S = 4096
    L = 2
    maps = prep_inputs(inputs, S, L)
    if "nc" not in _NC_CACHE:
        _NC_CACHE["nc"] = build(S, L)
    res = run_bass_kernel_spmd(_NC_CACHE["nc"], maps, core_ids=list(range(8)))
    return np.stack([np.asarray(r["y"], dtype=np.float32) for r in res.results], axis=0)
```
